# Optimizing a Trainium2 kernel written in Bass

```python
import jax, jax.numpy as jnp
from jax import lax
import numpy as np

D_MODEL = 1024
BATCH = 32
SEQ = 256
DEPTH = 2
DEC_BATCH = 8
DEC_SEQ = 4096
PAST_LEN = 256

GRID_W = 64
N_MIXERS = 2
N_MLA_LAYERS = (DEPTH + 1) // 2
N_NA_LAYERS = DEPTH // 2
MLA_HEADS = 16
Q_LORA_RANK = 256
KV_LORA_RANK = 128
QK_NOPE_DIM = 128
QK_ROPE_DIM = 64
V_HEAD_DIM = 128
MLA_WIDTH = MLA_HEADS * V_HEAD_DIM
MLA_SCALE = (QK_NOPE_DIM + QK_ROPE_DIM) ** -0.5
ROPE_AXIS_FREQS = QK_ROPE_DIM // 4
ROPE_THETA = 10000.0
Q_BLOCK = 128
NA_HEADS = 16
NA_HEAD_DIM = 64
NA_WIDTH = NA_HEADS * NA_HEAD_DIM
NA_MAX_ROWS = 8
NA_COLS = 16
NA_SCALE = NA_HEAD_DIM ** -0.5
EPS = 1e-6

kernel_name = "hybrid_mla_natten_dit_step"


def rms_norm(x, g):
    xf = x.astype(jnp.float32)
    y = xf * lax.rsqrt(jnp.mean(xf * xf, axis=-1, keepdims=True) + EPS)
    return (y * g.astype(jnp.float32)).astype(x.dtype)


def modulation(cond, w, b):
    return jnp.split(jax.nn.silu(cond) @ w + b, 3, axis=-1)


def softmax_f32(s):
    return jax.nn.softmax(s.astype(jnp.float32), axis=-1)


def grid_rope_tables(n):
    t = jnp.arange(n)
    pos = jnp.stack([t // GRID_W, t % GRID_W], axis=-1).astype(jnp.float32)
    inv = ROPE_THETA ** (-jnp.arange(ROPE_AXIS_FREQS, dtype=jnp.float32) / ROPE_AXIS_FREQS)
    ang = pos[:, :, None] * inv
    return jnp.cos(ang), jnp.sin(ang)


def axial_rope(x, cos, sin):
    xs = x.reshape(x.shape[:-1] + (2, 2, ROPE_AXIS_FREQS))
    x1, x2 = xs[..., 0, :], xs[..., 1, :]
    out = jnp.stack([x1 * cos - x2 * sin, x2 * cos + x1 * sin], axis=-2)
    return out.reshape(x.shape).astype(x.dtype)


def mla_project(h, w_in, q_norm_g, w_qb, kv_norm_g):
    splits = [Q_LORA_RANK, Q_LORA_RANK + KV_LORA_RANK, Q_LORA_RANK + KV_LORA_RANK + QK_ROPE_DIM]
    q_a, kv_a, k_rope, gate = jnp.split(h @ w_in, splits, axis=-1)
    q = (rms_norm(q_a, q_norm_g) @ w_qb).reshape(h.shape[:2] + (MLA_HEADS, QK_NOPE_DIM + QK_ROPE_DIM))
    c_kv = rms_norm(kv_a, kv_norm_g)
    return q[..., :QK_NOPE_DIM], q[..., QK_NOPE_DIM:], c_kv, k_rope, gate


def mla_expand(c_kv, w_kvb):
    kv = (c_kv @ w_kvb).reshape(c_kv.shape[:2] + (MLA_HEADS, QK_NOPE_DIM + V_HEAD_DIM))
    return kv[..., :QK_NOPE_DIM], kv[..., QK_NOPE_DIM:]


def mla_attend(q_nope, q_rope, k_nope, k_rope, v):
    s = (jnp.einsum('bqhd,bkhd->bhqk', q_nope, k_nope)
         + jnp.einsum('bqhr,bkr->bhqk', q_rope, k_rope)) * MLA_SCALE
    p = softmax_f32(s).astype(v.dtype)
    return jnp.einsum('bhqk,bkhd->bqhd', p, v)


def mla_output(o, gate, w_out):
    b, n = o.shape[:2]
    return (o.reshape(b, n, MLA_WIDTH) * jax.nn.silu(gate)) @ w_out


def mla_context(h, w_in, q_norm_g, w_qb, kv_norm_g, w_kvb, w_out):
    q_nope, q_rope, c_kv, k_rope, gate = mla_project(h, w_in, q_norm_g, w_qb, kv_norm_g)
    k_nope, v = mla_expand(c_kv, w_kvb)
    o = mla_attend(q_nope, q_rope, k_nope, k_rope, v)
    return mla_output(o, gate, w_out), c_kv, k_rope


def mla_latent(h, ckv_ctx, krope_ctx, w_in, q_norm_g, w_qb, kv_norm_g, w_kvb, w_out):
    b, n, _ = h.shape
    q_nope, q_rope, c_kv, k_rope, gate = mla_project(h, w_in, q_norm_g, w_qb, kv_norm_g)
    cos, sin = grid_rope_tables(n)
    q_rope = axial_rope(q_rope, cos[:, None], sin[:, None])
    k_rope = axial_rope(k_rope, cos, sin)
    k_nope, v = mla_expand(jnp.concatenate([ckv_ctx, c_kv], axis=1), w_kvb)
    k_rope_all = jnp.concatenate([krope_ctx, k_rope], axis=1)
    nb = n // Q_BLOCK
    qn_b = q_nope.reshape(b, nb, Q_BLOCK, MLA_HEADS, QK_NOPE_DIM).transpose(1, 0, 2, 3, 4)
    qr_b = q_rope.reshape(b, nb, Q_BLOCK, MLA_HEADS, QK_ROPE_DIM).transpose(1, 0, 2, 3, 4)
    o = lax.map(lambda qs: mla_attend(qs[0], qs[1], k_nope, k_rope_all, v), (qn_b, qr_b))
    o = o.transpose(1, 0, 2, 3, 4).reshape(b, n, MLA_HEADS, V_HEAD_DIM)
    return mla_output(o, gate, w_out)


def na_project(h, w_in):
    q, k, v, gate = jnp.split(h @ w_in, 4, axis=-1)
    shp = h.shape[:2] + (NA_HEADS, NA_HEAD_DIM)
    return q.reshape(shp), k.reshape(shp), v.reshape(shp), gate


def na_context(h, w_in, w_out):
    b, n, _ = h.shape
    q, k, v, gate = na_project(h, w_in)
    p = softmax_f32(jnp.einsum('bqhd,bkhd->bhqk', q, k) * NA_SCALE).astype(v.dtype)
    o = jnp.einsum('bhqk,bkhd->bqhd', p, v).reshape(b, n, NA_WIDTH)
    return (o * jax.nn.silu(gate)) @ w_out, k, v


def na_latent(h, k_ctx, v_ctx, w_in, rel_bias, w_out):
    b, n, _ = h.shape
    rows = n // GRID_W
    kr = min(NA_MAX_ROWS, rows)
    n_loc = kr * GRID_W
    q, k, v, gate = na_project(h, w_in)
    grid_shape = (b, rows, GRID_W, NA_HEADS, NA_HEAD_DIM)
    qg, kg, vg = q.reshape(grid_shape), k.reshape(grid_shape), v.reshape(grid_shape)
    cols = jnp.arange(GRID_W)
    col_start = jnp.clip(cols - NA_COLS // 2, 0, GRID_W - NA_COLS)
    col_ok = (cols[None, :] >= col_start[:, None]) & (cols[None, :] < col_start[:, None] + NA_COLS)
    dc_idx = jnp.clip(cols[None, :] - cols[:, None] + NA_COLS - 1, 0, 2 * NA_COLS - 2)
    mask = jnp.broadcast_to(col_ok[:, None, :], (GRID_W, kr, GRID_W)).reshape(GRID_W, n_loc)

    def row_block(r):
        rs = jnp.clip(r - kr // 2, 0, rows - kr)
        q_r = lax.dynamic_index_in_dim(qg, r, axis=1, keepdims=False)
        k_blk = lax.dynamic_slice_in_dim(kg, rs, kr, axis=1).reshape(b, n_loc, NA_HEADS, NA_HEAD_DIM)
        v_blk = lax.dynamic_slice_in_dim(vg, rs, kr, axis=1).reshape(b, n_loc, NA_HEADS, NA_HEAD_DIM)
        dr_idx = rs + jnp.arange(kr) - r + NA_MAX_ROWS - 1
        bias = rel_bias[:, dr_idx[:, None, None], dc_idx[None, :, :]]
        bias = bias.transpose(0, 2, 1, 3).reshape(NA_HEADS, GRID_W, n_loc).astype(jnp.float32)
        s_loc = jnp.einsum('bqhd,bkhd->bhqk', q_r, k_blk).astype(jnp.float32) * NA_SCALE + bias
        s_loc = jnp.where(mask, s_loc, -jnp.inf)
        s_ctx = jnp.einsum('bqhd,bkhd->bhqk', q_r, k_ctx).astype(jnp.float32) * NA_SCALE
        p = softmax_f32(jnp.concatenate([s_loc, s_ctx], axis=-1)).astype(v.dtype)
        return (jnp.einsum('bhqk,bkhd->bqhd', p[..., :n_loc], v_blk)
                + jnp.einsum('bhqk,bkhd->bqhd', p[..., n_loc:], v_ctx))

    o = lax.map(row_block, jnp.arange(rows))
    o = o.transpose(1, 0, 2, 3, 4).reshape(b, n, NA_WIDTH)
    return (o * jax.nn.silu(gate)) @ w_out


def setup_inputs(seed: int = 0) -> dict:
    key = jax.random.key(seed)
    ks = jax.random.split(key, 24)
    nrm = jax.random.normal
    mla_in = Q_LORA_RANK + KV_LORA_RANK + QK_ROPE_DIM + MLA_WIDTH
    return {
        "x_prompt": nrm(ks[0], (BATCH, SEQ, D_MODEL), jnp.float32),
        "x_sample": nrm(ks[1], (DEC_BATCH, DEC_SEQ, D_MODEL), jnp.float32),
        "cache_mla_ckv": nrm(ks[2], (DEC_BATCH, N_MLA_LAYERS, PAST_LEN, KV_LORA_RANK), jnp.float32),
        "cache_mla_krope": nrm(ks[3], (DEC_BATCH, N_MLA_LAYERS, PAST_LEN, QK_ROPE_DIM), jnp.float32),
        "cache_na_k": nrm(ks[4], (DEC_BATCH, N_NA_LAYERS, PAST_LEN, NA_HEADS, NA_HEAD_DIM), jnp.float32),
        "cache_na_v": nrm(ks[5], (DEC_BATCH, N_NA_LAYERS, PAST_LEN, NA_HEADS, NA_HEAD_DIM), jnp.float32),
        "c": nrm(ks[6], (DEC_BATCH, D_MODEL), jnp.float32),
        "c_ctx": nrm(ks[7], (D_MODEL,), jnp.float32),
        "w_ada": nrm(ks[8], (DEPTH, D_MODEL, 3 * D_MODEL), jnp.float32) * D_MODEL ** -0.5,
        "b_ada": nrm(ks[9], (DEPTH, 3 * D_MODEL), jnp.float32) * 0.02,
        "pre_norm_g": 1.0 + 0.05 * nrm(ks[10], (DEPTH, D_MODEL), jnp.float32),
        "post_norm_g": 1.0 + 0.05 * nrm(ks[11], (DEPTH, D_MODEL), jnp.float32),
        "mla_w_in": nrm(ks[12], (N_MLA_LAYERS, D_MODEL, mla_in), jnp.float32) * D_MODEL ** -0.5,
        "mla_q_norm_g": 1.0 + 0.05 * nrm(ks[13], (N_MLA_LAYERS, Q_LORA_RANK), jnp.float32),
        "mla_w_qb": nrm(ks[14], (N_MLA_LAYERS, Q_LORA_RANK, MLA_HEADS * (QK_NOPE_DIM + QK_ROPE_DIM)), jnp.float32) * Q_LORA_RANK ** -0.5,
        "mla_kv_norm_g": 1.0 + 0.05 * nrm(ks[15], (N_MLA_LAYERS, KV_LORA_RANK), jnp.float32),
        "mla_w_kvb": nrm(ks[16], (N_MLA_LAYERS, KV_LORA_RANK, MLA_HEADS * (QK_NOPE_DIM + V_HEAD_DIM)), jnp.float32) * KV_LORA_RANK ** -0.5,
        "mla_w_out": nrm(ks[17], (N_MLA_LAYERS, MLA_WIDTH, D_MODEL), jnp.float32) * MLA_WIDTH ** -0.5,
        "na_w_in": nrm(ks[18], (N_NA_LAYERS, D_MODEL, 4 * NA_WIDTH), jnp.float32) * D_MODEL ** -0.5,
        "na_rel_bias": nrm(ks[19], (N_NA_LAYERS, NA_HEADS, 2 * NA_MAX_ROWS - 1, 2 * NA_COLS - 1), jnp.float32) * 0.5,
        "na_w_out": nrm(ks[20], (N_NA_LAYERS, NA_WIDTH, D_MODEL), jnp.float32) * NA_WIDTH ** -0.5,
    }


def reference(x_prompt, x_sample, cache_mla_ckv, cache_mla_krope, cache_na_k, cache_na_v, c, c_ctx,
              w_ada, b_ada, pre_norm_g, post_norm_g, mla_w_in, mla_q_norm_g, mla_w_qb, mla_kv_norm_g,
              mla_w_kvb, mla_w_out, na_w_in, na_rel_bias, na_w_out):
    xp, xs = x_prompt, x_sample
    new_ckv, new_krope, new_k, new_v = [], [], [], []
    for i in range(DEPTH):
        j = i // N_MIXERS
        sh_p, sc_p, g_p = modulation(c_ctx, w_ada[i], b_ada[i])
        sh_s, sc_s, g_s = modulation(c, w_ada[i], b_ada[i])
        hp = rms_norm(xp, pre_norm_g[i]) * (1.0 + sc_p) + sh_p
        hs = rms_norm(xs, pre_norm_g[i]) * (1.0 + sc_s[:, None]) + sh_s[:, None]
        if i % N_MIXERS == 0:
            op, ckv, krope = mla_context(hp, mla_w_in[j], mla_q_norm_g[j], mla_w_qb[j], mla_kv_norm_g[j],
                                         mla_w_kvb[j], mla_w_out[j])
            osm = mla_latent(hs, cache_mla_ckv[:, j], cache_mla_krope[:, j], mla_w_in[j], mla_q_norm_g[j],
                             mla_w_qb[j], mla_kv_norm_g[j], mla_w_kvb[j], mla_w_out[j])
            new_ckv.append(ckv)
            new_krope.append(krope)
        else:
            op, kc, vc = na_context(hp, na_w_in[j], na_w_out[j])
            osm = na_latent(hs, cache_na_k[:, j], cache_na_v[:, j], na_w_in[j], na_rel_bias[j], na_w_out[j])
            new_k.append(kc)
            new_v.append(vc)
        xp = xp + g_p * rms_norm(op, post_norm_g[i])
        xs = xs + g_s[:, None] * rms_norm(osm, post_norm_g[i])
    state_mla_ckv = jnp.stack(new_ckv, axis=1)
    state_mla_krope = jnp.stack(new_krope, axis=1)
    state_na_k = jnp.stack(new_k, axis=1)
    state_na_v = jnp.stack(new_v, axis=1)
    return (xp, xs, state_mla_ckv, state_mla_krope, state_na_k, state_na_v)
```

```python
import numpy as np
from contextlib import ExitStack
import concourse.bass as bass
import concourse.mybir as mybir
from concourse.bass_utils import run_bass_kernel_spmd

F32 = mybir.dt.float32
BF16 = mybir.dt.bfloat16
AF = mybir.ActivationFunctionType
ALU = mybir.AluOpType

PE, ACT, DVE, POOL, SP = "pe", "act", "dve", "pool", "sp"
ENGS = (PE, ACT, DVE, POOL, SP)
NDS = 24

D = 1024
SEQ_S = 4096
SEQ_P = 256
NPB = 4
EPS = 1e-6
MLA_SCALE = 192.0 ** -0.5
NA_SCALE = 0.125
GRID_W = 64


class _Stop(Exception):
    pass


class Dummy:
    def __getitem__(self, k):
        return self

    def __getattr__(self, k):
        return lambda *a, **kw: self


class Buf:
    __slots__ = ("name", "w", "r")

    def __init__(self, name):
        self.name = name
        self.w = None
        self.r = {}


class EngProxy:
    def __init__(self, gen, name, obj):
        self._g = gen
        self._n = name
        self._o = obj

    def __getattr__(self, op):
        g, n, o = self._g, self._n, self._o

        def f(*args, R=(), W=(), **kw):
            return g.emit(n, (lambda: getattr(o, op)(*args, **kw)), R, W)
        return f


class Gen:
    def __init__(self, nc, marked, es):
        self.nc = nc
        self.dry = nc is None
        self.marked = marked if marked is not None else set()
        self.need = set()
        self.idx = {e: 0 for e in ENGS}
        self.sigcount = {e: 0 for e in ENGS}
        self.sigval = {}
        self.waited = {e: {} for e in ENGS}
        self.dma_val = {}
        self.dma_rr = {SP: 0, POOL: 0, ACT: 0}
        self.bufs = {}
        self.ninst = 0
        if self.dry:
            self.engobj = {e: Dummy() for e in ENGS}
            self.sem = {}
            self.dsem = {}
        else:
            self.engobj = {PE: nc.tensor, ACT: nc.scalar, DVE: nc.vector, POOL: nc.gpsimd, SP: nc.sync}
            self.sem = {e: es.enter_context(nc.semaphore("s_" + e)) for e in (PE, ACT, DVE, POOL)}
            self.dsem = {}
            for q in (SP, POOL):
                for k in range(NDS):
                    self.dsem[(q, k)] = es.enter_context(nc.semaphore("d_%s_%d" % (q, k)))
        self.pe = EngProxy(self, PE, self.engobj[PE])
        self.act = EngProxy(self, ACT, self.engobj[ACT])
        self.dve = EngProxy(self, DVE, self.engobj[DVE])
        self.pool = EngProxy(self, POOL, self.engobj[POOL])

    def B(self, *key):
        b = self.bufs.get(key)
        if b is None:
            b = Buf(key)
            self.bufs[key] = b
        return b

    def wait(self, eng, ev):
        if ev[0] == "c":
            _, pe_, pi = ev
            if self.dry:
                self.need.add((pe_, pi))
                return
            val = self.sigval[(pe_, pi)]
            key = ("c", pe_)
            if self.waited[eng].get(key, 0) >= val:
                return
            self.engobj[eng].wait_ge(self.sem[pe_], val)
            self.waited[eng][key] = val
        else:
            _, qk, val = ev
            key = ("d", qk)
            if self.waited[eng].get(key, 0) >= val:
                return
            if not self.dry:
                self.engobj[eng].wait_ge(self.dsem[qk], val)
            self.waited[eng][key] = val

    def _deps(self, eng, R, W):
        for b in R:
            if b.w is not None:
                self._dep(eng, b.w, True)
            if b.name[0] in ("ps", "ps_out"):
                for k, ev in list(b.r.items()):
                    if k != eng:
                        self._dep(eng, ev, False)
        for b in W:
            if b.w is not None:
                self._dep(eng, b.w, False)
            for ev in list(b.r.values()):
                self._dep(eng, ev, False)

    def _dep(self, eng, ev, raw):
        if ev[0] == "c" and ev[1] == eng:
            if eng == PE or not raw:
                return
        self.wait(eng, ev)

    def emit(self, eng, fn, R=(), W=()):
        self._deps(eng, R, W)
        i = self.idx[eng]
        self.idx[eng] += 1
        self.ninst += 1
        me = ("c", eng, i)
        if not self.dry:
            ins = fn()
            if (eng, i) in self.marked:
                ins.then_inc(self.sem[eng], 1)
                self.sigcount[eng] += 1
                self.sigval[(eng, i)] = self.sigcount[eng]
        for b in R:
            b.r[eng] = me
        for b in W:
            b.w = me
            b.r = {}
        return me

    def dma(self, q, out, in_, R=(), W=(), **kw):
        self._deps(q, R, W)
        k = self.dma_rr[q]
        self.dma_rr[q] = (k + 1) % (NDS if q == SP else 8)
        qk = (q, k)
        prev = self.dma_val.get(qk, 0)
        if prev > 0:
            self.wait(q, ("d", qk, prev))
        val = prev + 16
        self.dma_val[qk] = val
        self.ninst += 1
        if not self.dry:
            self.engobj[q].dma_start(out=out, in_=in_, **kw).then_inc(self.dsem[qk], 16)
        me = ("d", qk, val)
        for b in R:
            b.r[("d", qk)] = me
        for b in W:
            b.w = me
            b.r = {}
        return me

    def barrier(self):
        for e in ENGS:
            for o in (PE, ACT, DVE, POOL):
                if o != e and self.idx[o] > 0:
                    self.wait(e, ("c", o, self.idx[o] - 1))
            for qk, v in self.dma_val.items():
                self.wait(e, ("d", qk, v))

    def finish(self):
        for qk, v in self.dma_val.items():
            self.wait(SP, ("d", qk, v))


def build_program(nc, marked, debug=False, stage=99):
    es = ExitStack()
    G = Gen(nc, marked, es)
    try:
        _build_body(nc, G, es, debug, stage)
    except _Stop:
        G.finish()
        return G
    G.finish()
    es.close()
    return G


def _build_body(nc, G, es, debug, stage):
    dry = nc is None
    B = G.B
    pe, act, dve = G.pe, G.act, G.dve

    def dram(name, shape, dt=F32, kind="Internal"):
        if dry:
            return Dummy()
        if kind == "Internal":
            return nc.dram_tensor(name, list(shape), dt).ap()
        return nc.dram_tensor(name, list(shape), dt, kind=kind).ap()

    def sb(stack, name, shape, dt):
        if dry:
            return Dummy()
        return stack.enter_context(nc.sbuf_tensor(name, list(shape), dt))

    def psum(stack, name, shape, dt=F32):
        if dry:
            return Dummy()
        return stack.enter_context(nc.psum_tensor(name, list(shape), dt))

    IN = lambda n, s: dram(n, s, F32, "ExternalInput")
    OUT = lambda n, s: dram(n, s, F32, "ExternalOutput")

    xs = IN("xs", [SEQ_S, D])
    xp = IN("xp", [NPB * SEQ_P, D])
    c_ckv = IN("c_ckv", [256, 128])
    c_kr = IN("c_kr", [256, 64])
    c_nak = IN("c_nak", [256, 1024])
    c_nav = IN("c_nav", [256, 1024])
    ccols = IN("ccols", [128, 16])
    w_ada = IN("w_ada", [2, 1024, 3072])
    b_ada = IN("b_ada", [2, 3072])
    pre_g = IN("pre_g", [2, 1024])
    post_g = IN("post_g", [2, 1024])
    mla_w_in = IN("mla_w_in", [1024, 2496])
    gq_cols = IN("gq_cols", [128, 2])
    mla_w_qb = IN("mla_w_qb", [256, 3072])
    gkv = IN("gkv", [128])
    mla_w_kvb = IN("mla_w_kvb", [128, 4096])
    mla_w_out = IN("mla_w_out", [2048, 1024])
    na_w_in = IN("na_w_in", [1024, 4096])
    rbp = IN("rbp", [16, 23, 127])
    na_w_out = IN("na_w_out", [1024, 1024])
    ident_in = IN("ident_in", [128, 128])
    j2_in = IN("j2_in", [128, 128])
    selm_in = IN("selm_in", [2, 256])
    i2_in = IN("i2_in", [2, 2])
    cs_tab = IN("cs_tab", [128, SEQ_S])
    kt_tab = IN("kt_tab", [SEQ_S, 128])
    mask_in = IN("mask_in", [128, 2, 22 * 64])
    sw_in = IN("sw_in", [128, 128])

    y_s = OUT("y_s", [SEQ_S, D])
    y_p = OUT("y_p", [NPB * SEQ_P, D])
    st_ckv = OUT("st_ckv", [NPB * SEQ_P, 128])
    st_kr = OUT("st_kr", [NPB * SEQ_P, 64])
    st_k = OUT("st_k", [NPB * SEQ_P, 1024])
    st_v = OUT("st_v", [NPB * SEQ_P, 1024])

    x1s = dram("x1s", [SEQ_S, D]) if not debug else OUT("x1s", [SEQ_S, D])
    x1p = dram("x1p", [NPB * SEQ_P, D]) if not debug else OUT("x1p", [NPB * SEQ_P, D])
    wg_scr = dram("wg_scr", [16, 128, 8 * 128], BF16)
    wo_scr = dram("wo_scr", [16, 128, 1024], BF16)
    nwi_scr = dram("nwi_scr", [32, 128, 8 * 128], BF16)
    nwo_scr = dram("nwo_scr", [8, 128, 1024], BF16)
    tab_scr = dram("tab_scr", [16, 128, 2 * 22 * 64], BF16)

    P0 = ExitStack()
    es.enter_context(P0)
    ident_bf = sb(P0, "ident_bf", [128, 128], BF16)
    ones_bf = sb(P0, "ones_bf", [128, 128], BF16)
    j2_bf = sb(P0, "j2_bf", [128, 128], BF16)
    selm = sb(P0, "selm", [2, 256], F32)
    i2 = sb(P0, "i2", [2, 2], F32)
    scs = sb(P0, "scs", [128, 16], F32)
    modcol = sb(P0, "modcol", [128, 2, 32], F32)
    Gbc = sb(P0, "Gbc", [128, 2, 2, 1024], F32)
    epsc = sb(P0, "epsc", [128, 1], F32)
    xt = [sb(P0, "xt%d" % i, [128, 1024], F32) for i in range(3)]
    xn = [sb(P0, "xn%d" % i, [128, 1024], BF16) for i in range(2)]
    junk = sb(P0, "junk", [128, 1024], BF16)
    ytmp = sb(P0, "ytmp", [128, 1024], F32)
    stats = [sb(P0, "stat%d" % i, [128, 4], F32) for i in range(8)]
    hT = sb(P0, "hT", [128, 8, 512], BF16)
    wpool = [sb(P0, "wp%d" % i, [128, 1024], BF16) for i in range(6)]
    PT = [sb(P0, "PT%d" % i, [128, 512], BF16) for i in range(6)]
    ones_f32 = sb(P0, "ones_f32", [128, 128], F32)
    recip = [sb(P0, "recip%d" % i, [128, 512], F32) for i in range(2)]
    sgT = sb(P0, "sgT", [128, 16, 512], BF16)

    ps = [psum(P0, "ps%d" % i, [128, 512]) for i in range(6)]
    ps_out = psum(P0, "ps_out", [128, 1024])
    PS = [B("ps", i) for i in range(6)]
    PSO = B("ps_out")
    PSO0 = B("ps_out", 0)
    PSO1 = B("ps_out", 1)

    cnt = {"xt": 0, "xn": 0, "st": 0, "wp": 0, "pj": 0, "PT": 0}

    def rot(name, n):
        v = cnt[name]
        cnt[name] = (v + 1) % n
        return v

    pjbanks = [[4, 5]]
    pjc = [0]

    def next_pj():
        v = pjbanks[0][pjc[0] % len(pjbanks[0])]
        pjc[0] += 1
        return v

    def set_pj(lst):
        pjbanks[0] = lst

    G.dma(POOL, ident_bf[:], ident_in[:], W=[B("ident")])
    G.dma(POOL, j2_bf[:], j2_in[:], W=[B("j2")])
    G.dma(SP, selm[:], selm_in[:], W=[B("selm")])
    G.dma(SP, i2[:], i2_in[:], W=[B("i2")])
    dve.memset(ones_bf[:], 1.0, W=[B("ones")])
    dve.memset(ones_f32[:], 1.0, W=[B("ones_f32")])
    dve.memset(epsc[:], EPS, W=[B("epsc")])

    W0 = ExitStack()
    es.enter_context(W0)
    w_in_qk = sb(W0, "w_in_qk", [128, 8, 448], BF16)
    wqb_bf = sb(W0, "wqb_bf", [128, 2, 3072], BF16)
    wkvb_bf = sb(W0, "wkvb_bf", [128, 4096], BF16)
    set_pj([0, 1, 2, 3, 4, 5])
    TP = ExitStack()
    if True:
        mask_bf = sb(TP, "mask_bf", [128, 2, 1408], BF16)
        Hraw = [sb(TP, "Hraw%d" % i, [128, 1408], F32) for i in range(1)]
        Tp = [sb(TP, "Tp%d" % i, [128, 2816], BF16) for i in range(1)]
        Tfin = [sb(TP, "Tfin%d" % i, [128, 2816], BF16) for i in range(1)]
        G.dma(POOL, mask_bf[:], mask_in[:], W=[B("mask_bf")])
        for c in range(8):
            G.dma(POOL, w_in_qk[:, c, :], mla_w_in[c * 128:(c + 1) * 128, 0:448], W=[B("w_in_qk")])
        for c in range(2):
            for hh in range(2):
                G.dma(POOL, wqb_bf[:, c, hh * 1536:(hh + 1) * 1536],
                      mla_w_qb[c * 128:(c + 1) * 128, hh * 1536:(hh + 1) * 1536], W=[B("wqb_bf")])
        for hh in range(2):
            G.dma(POOL, wkvb_bf[:, hh * 2048:(hh + 1) * 2048], mla_w_kvb[:, hh * 2048:(hh + 1) * 2048],
                  W=[B("wkvb_bf")])
        for h in range(16):
            G.dma(POOL, wg_scr[h].rearrange("p (c j) -> p c j", c=8),
                  mla_w_in[:, 448 + h * 128: 448 + (h + 1) * 128].rearrange("(c p) j -> p c j", p=128),
                  W=[B("wg_scr", h)])
        for h in range(16):
            G.dma(POOL, wo_scr[h], mla_w_out[h * 128:(h + 1) * 128, :], W=[B("wo_scr", h)])
        na_casts = []
        for m in range(32):
            na_casts.append(lambda m=m: G.dma(
                POOL, nwi_scr[m].rearrange("p (c j) -> p c j", c=8),
                na_w_in[:, m * 128:(m + 1) * 128].rearrange("(c p) j -> p c j", p=128), W=[B("nwi_scr", m)]))
        for c in range(8):
            na_casts.append(lambda c=c: G.dma(POOL, nwo_scr[c], na_w_out[c * 128:(c + 1) * 128, :],
                                              W=[B("nwo_scr", c)]))


        def table_head(h):
            hb = 0
            tbanks = [(ps_out[:, 0:512], PSO0), (ps_out[:, 512:1024], PSO1)]
            for half in range(2):
                if dry:
                    src = Dummy()
                else:
                    src = bass.AP(rbp.tensor, h * 23 * 127 + (1 - half) * 127, [[1, 64], [127, 22], [1, 64]])
                G.dma(SP, Hraw[hb][half * 64:(half + 1) * 64, :].rearrange("p (u q) -> p u q", u=22), src,
                      W=[B("Hraw", hb)])
            act.activation(Hraw[hb][:], Hraw[hb][:], AF.Exp, R=[B("Hraw", hb)], W=[B("Hraw", hb)])
            for k in range(2):
                dve.tensor_tensor(Tp[hb][:, k * 1408:(k + 1) * 1408], Hraw[hb][:], mask_bf[:, k, :], ALU.mult,
                                  R=[B("Hraw", hb), B("mask_bf")], W=[B("Tp", hb)])
            for n in range(6):
                w = min(512, 2816 - n * 512)
                tb_t, tb_b = tbanks[n % 2]
                pe.matmul(tb_t[:, 0:w], j2_bf[:], Tp[hb][:, n * 512:n * 512 + w], start=True, stop=True,
                          R=[B("j2"), B("Tp", hb)], W=[tb_b])
                if n % 2 == 0:
                    dve.tensor_copy(Tfin[hb][:, n * 512:n * 512 + w], tb_t[:, 0:w], R=[tb_b], W=[B("Tfin", hb)])
                else:
                    act.copy(Tfin[hb][:, n * 512:n * 512 + w], tb_t[:, 0:w], R=[tb_b], W=[B("Tfin", hb)])
            G.dma(SP, tab_scr[h], Tfin[hb][:], R=[B("Tfin", hb)], W=[B("tab_scr", h)])

    set_pj([4, 5])
    if stage == 1:
        raise _Stop()
    with ExitStack() as T0:
        wa = [sb(T0, "wa%d" % i, [128, 3072], F32) for i in range(2)]
        mod_sb = sb(T0, "mod_sb", [2, 3072], F32)
        bada = sb(T0, "bada", [2, 3072], F32)
        gpre2 = sb(T0, "gpre2", [2, 1024], F32)
        gpost2 = sb(T0, "gpost2", [2, 1024], F32)
        Arow = sb(T0, "Arow", [2, 1024], F32)
        Grow = sb(T0, "Grow", [2, 1024], F32)
        scin = sb(T0, "scin", [128, 16], F32)
        G.dma(SP, scin[:], ccols[:], W=[B("scin")])
        act.activation(scs[:], scin[:], AF.Silu, R=[B("scin")], W=[B("scs")])
        for l in range(2):
            G.dma(SP, bada[:], b_ada[l].partition_broadcast(2), W=[B("bada")])
            G.dma(SP, gpre2[:], pre_g[l].partition_broadcast(2), W=[B("gpre2")])
            G.dma(SP, gpost2[:], post_g[l].partition_broadcast(2), W=[B("gpost2")])
            G.dma(SP, wa[0][:], w_ada[l, 0:128, :], W=[B("wa", 0)])
            for k in range(8):
                wb = k % 2
                if k + 1 < 8:
                    G.dma(SP, wa[1 - wb][:], w_ada[l, (k + 1) * 128:(k + 2) * 128, :], W=[B("wa", 1 - wb)])
                for n in range(6):
                    pe.matmul(ps[n][0:2, :], scs[:, 2 * k:2 * k + 2], wa[wb][:, n * 512:(n + 1) * 512],
                              start=(k == 0), stop=(k == 7), R=[B("scs"), B("wa", wb)], W=[PS[n]])
                table_head(l * 8 + k)
            for n in range(6):
                dve.tensor_tensor(mod_sb[:, n * 512:(n + 1) * 512], ps[n][0:2, :], bada[:, n * 512:(n + 1) * 512],
                                  ALU.add, R=[PS[n], B("bada")], W=[B("mod_sb")])
            dve.scalar_tensor_tensor(Arow[:], mod_sb[:, 1024:2048], 1.0, gpre2[:], ALU.add, ALU.mult,
                                     R=[B("mod_sb"), B("gpre2")], W=[B("Arow")])
            dve.tensor_tensor(Grow[:], mod_sb[:, 2048:3072], gpost2[:], ALU.mult,
                              R=[B("mod_sb"), B("gpost2")], W=[B("Grow")])
            for c in range(8):
                pe.matmul(ps[0][:, 2 * c:2 * c + 2], Arow[:, c * 128:(c + 1) * 128], i2[:], start=True, stop=True,
                          R=[B("Arow"), B("i2")], W=[PS[0]])
                pe.matmul(ps[0][:, 16 + 2 * c:16 + 2 * c + 2], mod_sb[:, c * 128:(c + 1) * 128], i2[:],
                          start=True, stop=True, R=[B("mod_sb"), B("i2")], W=[PS[0]])
            dve.tensor_copy(modcol[:, l, :], ps[0][:, 0:32], R=[PS[0]], W=[B("modcol", l)])
            for g in range(2):
                for n in range(2):
                    pe.matmul(ps[1 + n][:, :], selm[:, g * 128:(g + 1) * 128], Grow[:, n * 512:(n + 1) * 512],
                              start=True, stop=True, R=[B("selm"), B("Grow")], W=[PS[1 + n]])
                    dve.tensor_copy(Gbc[:, l, g, n * 512:(n + 1) * 512], ps[1 + n][:, :], R=[PS[1 + n]],
                                    W=[B("Gbc", l, g)])
        G.barrier()
    TP.close()

    if stage == 2:
        raise _Stop()
    def load_x(src_ap, srcbuf):
        i = rot("xt", 3)
        G.dma(SP, xt[i][:], src_ap, R=[srcbuf] if srcbuf is not None else [], W=[B("xt", i)])
        return xt[i], B("xt", i)

    def rstd_of(src_ap, srcbuf, n, width_scale):
        i = rot("st", 8)
        st = stats[i]
        sbuf = B("stat", i)
        act.activation(junk[:, 0:n], src_ap, AF.Square, accum_out=st[:, 0:1], R=[srcbuf], W=[sbuf, B("junk")])
        act.activation(st[:, 1:2], st[:, 0:1], AF.Ln, bias=epsc[:], scale=width_scale, R=[sbuf, B("epsc")], W=[sbuf])
        act.activation(st[:, 2:3], st[:, 1:2], AF.Exp, scale=-0.5, R=[sbuf], W=[sbuf])
        return st[:, 2:3], sbuf

    def make_hT(xtile, xbuf, l, g, col0):
        r_ap, rbuf = rstd_of(xtile[:], xbuf, 1024, 1.0 / D)
        if stage == 3.01:
            raise _Stop()
        i = rot("xn", 2)
        act.activation(xn[i][:], xtile[:], AF.Identity, scale=r_ap, R=[xbuf, rbuf], W=[B("xn", i)])
        if stage == 3.02:
            raise _Stop()
        pj = next_pj()
        psb = ps[pj][:].bitcast(BF16)
        for c in range(8):
            pe.transpose(psb[:, c * 128:(c + 1) * 128], xn[i][:, c * 128:(c + 1) * 128], ident_bf[:],
                         R=[B("xn", i), B("ident")], W=[PS[pj]])
        if stage == 3.03:
            raise _Stop()
        for c in range(8):
            eng = dve if c % 2 == 0 else dve
            eng.tensor_scalar(hT[:, c, col0:col0 + 128], psb[:, c * 128:(c + 1) * 128],
                              modcol[:, l, 2 * c + g:2 * c + g + 1], modcol[:, l, 16 + 2 * c + g:16 + 2 * c + g + 1],
                              ALU.mult, ALU.add, R=[PS[pj], B("modcol", l)], W=[B("hT", col0 // 128)])

    def wload(src_ap, srcbuf):
        i = rot("wp", 6)
        G.dma(SP, wpool[i][:], src_ap, R=[srcbuf], W=[B("wp", i)])
        return wpool[i], B("wp", i)

    def epilogue(bk, l, g, xsrc_ap, xsrcbuf, dst_ap, dstbuf):
        a0, a1, b0, b1 = bk
        i = rot("st", 8)
        st = stats[i]
        sbuf = B("stat", i)
        act.activation(junk[:, 0:512], a0, AF.Square, accum_out=st[:, 0:1], R=[b0], W=[sbuf, B("junk")])
        act.activation(junk[:, 512:1024], a1, AF.Square, accum_out=st[:, 3:4], R=[b1], W=[sbuf, B("junk")])
        dve.tensor_tensor(st[:, 0:1], st[:, 0:1], st[:, 3:4], ALU.add, R=[sbuf], W=[sbuf])
        act.activation(st[:, 1:2], st[:, 0:1], AF.Ln, bias=epsc[:], scale=1.0 / D, R=[sbuf, B("epsc")], W=[sbuf])
        act.activation(st[:, 2:3], st[:, 1:2], AF.Exp, scale=-0.5, R=[sbuf], W=[sbuf])
        r_ap = st[:, 2:3]
        xr, xrb = load_x(xsrc_ap, xsrcbuf)
        dve.scalar_tensor_tensor(ytmp[:, 0:512], a0, r_ap, Gbc[:, l, g, 0:512], ALU.mult, ALU.mult,
                                 R=[b0, sbuf, B("Gbc", l, g)], W=[B("ytmp")])
        dve.scalar_tensor_tensor(ytmp[:, 512:1024], a1, r_ap, Gbc[:, l, g, 512:1024], ALU.mult, ALU.mult,
                                 R=[b1, sbuf, B("Gbc", l, g)], W=[B("ytmp")])
        dve.tensor_tensor(xr[:], ytmp[:], xr[:], ALU.add, R=[B("ytmp"), xrb], W=[xrb])
        G.dma(POOL, dst_ap, xr[:], R=[xrb], W=[dstbuf])

    out_banks = [(ps_out[:, 0:512], ps_out[:, 512:1024], PSO0, PSO1),
                 (ps[0][:, :], ps[1][:, :], PS[0], PS[1]),
                 (ps[2][:, :], ps[3][:, :], PS[2], PS[3]),
                 (ps[4][:, :], ps[5][:, :], PS[4], PS[5])]

    with ExitStack() as L0:
        Wqlat = sb(L0, "Wqlat", [128, 2, 16, 128], BF16)
        WqrAB = sb(L0, "WqrAB", [128, 2, 16, 128], BF16)
        Wv = sb(L0, "Wv", [128, 16, 128], BF16)
        gqc = sb(L0, "gqc", [128, 2], F32)
        gkv_bc = sb(L0, "gkv_bc", [128, 128], F32)
        cs_prompt = sb(L0, "cs_prompt", [128, 256], F32)
        ckvT = sb(L0, "ckvT", [128, 34 * 128], BF16)
        krT2 = sb(L0, "krT2", [128, 34 * 128], BF16)
        ckv_tok = sb(L0, "ckv_tok", [128, 34, 128], BF16)
        qaT = sb(L0, "qaT", [128, 2, 512], BF16)
        sq = sb(L0, "sq", [128, 2, 512], BF16)
        rstd_bc = sb(L0, "rstd_bc", [128, 512], F32)
        csr = sb(L0, "csr", [128, 512], F32)
        cstab = [sb(L0, "cstab%d" % i, [128, 512], F32) for i in range(2)]
        QlatT = [sb(L0, "QlatT%d" % i, [128, 512], BF16) for i in range(4)]
        QrT = [sb(L0, "QrT%d" % i, [128, 512], BF16) for i in range(4)]
        OlatT = [sb(L0, "OlatT%d" % i, [128, 512], BF16) for i in range(2)]
        denacc = [sb(L0, "denacc%d" % i, [128, 512], F32) for i in range(4)]
        kv32 = [sb(L0, "kv32_%d" % i, [128, 128], F32) for i in range(2)]
        kr32 = [sb(L0, "kr32_%d" % i, [128, 64], F32) for i in range(2)]
        krf = sb(L0, "krf", [128, 64], F32)
        krtmp = sb(L0, "krtmp", [128, 64], F32)
        krf2 = [sb(L0, "krf2_%d" % i, [128, 128], BF16) for i in range(2)]
        ctab = [sb(L0, "ctab%d" % i, [128, 128], F32) for i in range(2)]
        cache32 = sb(L0, "cache32", [128, 192], F32)

        with ExitStack() as T1:
            WqnT = sb(T1, "WqnT", [128, 16, 256], BF16)
            WkT = sb(T1, "WkT", [128, 16, 128], BF16)
            G.dma(SP, gqc[:], gq_cols[:], W=[B("gqc")])
            G.dma(SP, gkv_bc[:], gkv.partition_broadcast(128), W=[B("gkv_bc")])
            dve.memset(cs_prompt[0:64, :], 1.0, W=[B("cs_prompt")])
            dve.memset(cs_prompt[64:128, :], 0.0, W=[B("cs_prompt")])
            dve.tensor_copy(Wv[:], wkvb_bf[:].rearrange("p (h t) -> p h t", h=16)[:, :, 128:256],
                            R=[B("wkvb_bf")], W=[B("Wv")])
            for j in range(4):
                pj = next_pj()
                psb = ps[pj][:].bitcast(BF16)
                for hh in range(4):
                    h = 4 * j + hh
                    for c in range(2):
                        pe.transpose(psb[:, (hh * 2 + c) * 128:(hh * 2 + c + 1) * 128],
                                     wqb_bf[:, c, h * 192:h * 192 + 128], ident_bf[:],
                                     R=[B("wqb_bf"), B("ident")], W=[PS[pj]])
                dve.tensor_copy(WqnT[:, 4 * j:4 * j + 4, :], psb.rearrange("p (h t) -> p h t", h=4),
                                R=[PS[pj]], W=[B("WqnT")])
            for j in range(2):
                pj = next_pj()
                psb = ps[pj][:].bitcast(BF16)
                for hh in range(8):
                    h = 8 * j + hh
                    pe.transpose(psb[:, hh * 128:(hh + 1) * 128], wkvb_bf[:, h * 256:h * 256 + 128], ident_bf[:],
                                 R=[B("wkvb_bf"), B("ident")], W=[PS[pj]])
                dve.tensor_copy(WkT[:, 8 * j:8 * j + 8, :], psb.rearrange("p (h t) -> p h t", h=8),
                                R=[PS[pj]], W=[B("WkT")])
            for c in range(2):
                for j in range(4):
                    pj = next_pj()
                    for hh in range(4):
                        h = 4 * j + hh
                        pe.matmul(ps[pj][:, hh * 128:(hh + 1) * 128], WqnT[:, h, c * 128:(c + 1) * 128], WkT[:, h, :],
                                  start=True, stop=True, R=[B("WqnT"), B("WkT")], W=[PS[pj]])
                    dve.tensor_scalar(Wqlat[:, c, 4 * j:4 * j + 4, :], ps[pj][:].rearrange("p (h t) -> p h t", h=4),
                                      gqc[:, c:c + 1], None, ALU.mult, R=[PS[pj], B("gqc")], W=[B("Wqlat")])
            for c in range(2):
                w4 = wqb_bf[:, c, :].rearrange("p (h t) -> p h t", h=16)
                dve.tensor_scalar(WqrAB[:, c, :, 0:64], w4[:, :, 128:192], gqc[:, c:c + 1], None, ALU.mult,
                                  R=[B("wqb_bf"), B("gqc")], W=[B("WqrAB")])
                for a in range(2):
                    o = 32 * a
                    dve.tensor_scalar(WqrAB[:, c, :, 64 + o:64 + o + 16], w4[:, :, 128 + o + 16:128 + o + 32],
                                      gqc[:, c:c + 1], -1.0, ALU.mult, ALU.mult,
                                      R=[B("wqb_bf"), B("gqc")], W=[B("WqrAB")])
                    dve.tensor_scalar(WqrAB[:, c, :, 64 + o + 16:64 + o + 32], w4[:, :, 128 + o:128 + o + 16],
                                      gqc[:, c:c + 1], None, ALU.mult,
                                      R=[B("wqb_bf"), B("gqc")], W=[B("WqrAB")])
            G.barrier()

        if stage == 3:
            raise _Stop()

        def kpass_front(xsrc_ap, g, slot):
            xtile, xbuf = load_x(xsrc_ap, None)
            make_hT(xtile, xbuf, 0, g, slot * 128)

        def kpass_back(g, slot, chunk, rope_pos, st_row):
            pj = next_pj()
            for c in range(8):
                pe.matmul(ps[pj][:, 0:192], hT[:, c, slot * 128:(slot + 1) * 128], w_in_qk[:, c, 256:448],
                          start=(c == 0), stop=(c == 7), R=[B("hT", slot), B("w_in_qk")], W=[PS[pj]])
            r_ap, rbuf = rstd_of(ps[pj][:, 0:128], PS[pj], 128, 1.0 / 128)
            i = chunk % 2
            dve.scalar_tensor_tensor(kv32[i][:], ps[pj][:, 0:128], r_ap, gkv_bc[:], ALU.mult, ALU.mult,
                                     R=[PS[pj], rbuf, B("gkv_bc")], W=[B("kv32", i)])
            act.copy(ckv_tok[:, chunk, :], kv32[i][:], R=[B("kv32", i)], W=[B("ckv_tok", chunk)])
            dve.tensor_copy(kr32[i][:], ps[pj][:, 128:192], R=[PS[pj]], W=[B("kr32", i)])
            if st_row is not None:
                G.dma(POOL, st_ckv[st_row:st_row + 128, :], kv32[i][:], R=[B("kv32", i)], W=[B("st_ckv", st_row)])
                G.dma(POOL, st_kr[st_row:st_row + 128, :], kr32[i][:], R=[B("kr32", i)], W=[B("st_kr", st_row)])
            if rope_pos is not None:
                G.dma(SP, ctab[i][:], kt_tab[rope_pos:rope_pos + 128, :], W=[B("ctab", i)])
                dve.tensor_tensor(krf[:], kr32[i][:], ctab[i][:, 0:64], ALU.mult,
                                  R=[B("kr32", i), B("ctab", i)], W=[B("krf")])
                k4 = kr32[i][:].rearrange("p (a s f) -> p a s f", a=2, s=2)
                t4 = krtmp[:].rearrange("p (a s f) -> p a s f", a=2, s=2)
                s4 = ctab[i][:, 64:128].rearrange("p (a s f) -> p a s f", a=2, s=2)
                dve.tensor_tensor(t4[:, :, 0, :], k4[:, :, 1, :], s4[:, :, 0, :], ALU.mult,
                                  R=[B("kr32", i), B("ctab", i)], W=[B("krtmp")])
                dve.tensor_tensor(t4[:, :, 1, :], k4[:, :, 0, :], s4[:, :, 1, :], ALU.mult,
                                  R=[B("kr32", i), B("ctab", i)], W=[B("krtmp")])
                dve.tensor_tensor(krf[:], krf[:], krtmp[:], ALU.add, R=[B("krf"), B("krtmp")], W=[B("krf")])
                src, sbuf_ = krf, B("krf")
            else:
                src, sbuf_ = kr32[i], B("kr32", i)
            dve.tensor_copy(krf2[i][:, 0:64], src[:], R=[sbuf_], W=[B("krf2", i)])
            dve.tensor_copy(krf2[i][:, 64:128], src[:], R=[sbuf_], W=[B("krf2", i)])
            k_transposes(chunk, krf2[i], B("krf2", i))

        def k_transposes(chunk, krsrc, krbuf):
            pj = next_pj()
            psb = ps[pj][:].bitcast(BF16)
            pe.transpose(psb[:, 0:128], ckv_tok[:, chunk, :], ident_bf[:], R=[B("ckv_tok", chunk), B("ident")],
                         W=[PS[pj]])
            pe.transpose(psb[:, 128:256], krsrc[:], ident_bf[:], R=[krbuf, B("ident")], W=[PS[pj]])
            dve.tensor_copy(ckvT[:, chunk * 128:(chunk + 1) * 128], psb[:, 0:128], R=[PS[pj]], W=[B("ckvT", chunk)])
            dve.tensor_copy(krT2[:, chunk * 128:(chunk + 1) * 128], psb[:, 128:256], R=[PS[pj]],
                            W=[B("krT2", chunk)])

        def cache_chunk(chunk):
            i = chunk % 2
            G.dma(SP, cache32[:, 0:128], c_ckv[chunk * 128:(chunk + 1) * 128, :], W=[B("cache32")])
            G.dma(SP, cache32[:, 128:192], c_kr[chunk * 128:(chunk + 1) * 128, :], W=[B("cache32")])
            act.copy(ckv_tok[:, chunk, :], cache32[:, 0:128], R=[B("cache32")], W=[B("ckv_tok", chunk)])
            dve.tensor_copy(krf2[i][:, 0:64], cache32[:, 128:192], R=[B("cache32")], W=[B("krf2", i)])
            dve.tensor_copy(krf2[i][:, 64:128], cache32[:, 128:192], R=[B("cache32")], W=[B("krf2", i)])
            k_transposes(chunk, krf2[i], B("krf2", i))

        def qblock(xsrc, g, row0, QB, nchunks, cs_ap, csbuf, dst, dstname):
            NT = QB // 128
            hbufs = [B("hT", j_) for j_ in range(NT)]
            set_pj([0, 1, 2, 3, 4, 5])
            kch = [B("ckvT", k) for k in range(nchunks)]
            for j in range(NT):
                xtile, xbuf = load_x(xsrc[row0 + j * 128:row0 + (j + 1) * 128, :], None)
                make_hT(xtile, xbuf, 0, g, j * 128)
            if stage == 3.3:
                raise _Stop()
            for c2 in range(2):
                pj = next_pj()
                for c in range(8):
                    pe.matmul(ps[pj][:, 0:QB], w_in_qk[:, c, c2 * 128:(c2 + 1) * 128], hT[:, c, 0:QB],
                              start=(c == 0), stop=(c == 7), R=[B("w_in_qk")] + hbufs, W=[PS[pj]])
                act.copy(qaT[:, c2, 0:QB], ps[pj][:, 0:QB], R=[PS[pj]], W=[B("qaT")])
                act.activation(sq[:, c2, 0:QB], ps[pj][:, 0:QB], AF.Square, R=[PS[pj]], W=[B("sq")])
            pj = next_pj()
            for c2 in range(2):
                pe.matmul(ps[pj][:, 0:QB], ones_bf[:], sq[:, c2, 0:QB], start=(c2 == 0), stop=(c2 == 1),
                          R=[B("ones"), B("sq")], W=[PS[pj]])
            act.activation(rstd_bc[:, 0:QB], ps[pj][:, 0:QB], AF.Ln, bias=epsc[:], scale=1.0 / 256,
                           R=[PS[pj], B("epsc")], W=[B("rstd_bc")])
            act.activation(rstd_bc[:, 0:QB], rstd_bc[:, 0:QB], AF.Exp, scale=-0.5, R=[B("rstd_bc")], W=[B("rstd_bc")])
            dve.tensor_tensor(csr[:, 0:QB], cs_ap, rstd_bc[:, 0:QB], ALU.mult, R=[csbuf, B("rstd_bc")], W=[B("csr")])
            if stage == 3.4:
                raise _Stop()
            for h in range(16):
                wt, wb = wload(wg_scr[h], B("wg_scr", h))
                pj = next_pj()
                for c in range(8):
                    pe.matmul(ps[pj][:, 0:QB], wt[:, c * 128:(c + 1) * 128], hT[:, c, 0:QB],
                              start=(c == 0), stop=(c == 7), R=[wb] + hbufs, W=[PS[pj]])
                act.activation(sgT[:, h, 0:QB], ps[pj][:, 0:QB], AF.Silu, R=[PS[pj]], W=[B("sgT", h)])

            if stage == 3.5:
                raise _Stop()

            set_pj([4, 5])

            def head_q(h):
                i = h % 4
                pj = next_pj()
                for c2 in range(2):
                    pe.matmul(ps[pj][:, 0:QB], Wqlat[:, c2, h, :], qaT[:, c2, 0:QB], start=(c2 == 0), stop=(c2 == 1),
                              R=[B("Wqlat"), B("qaT")], W=[PS[pj]])
                dve.tensor_tensor(QlatT[i][:, 0:QB], ps[pj][:, 0:QB], rstd_bc[:, 0:QB], ALU.mult,
                                  R=[PS[pj], B("rstd_bc")], W=[B("QlatT", i)])
                pj = next_pj()
                for c2 in range(2):
                    pe.matmul(ps[pj][:, 0:QB], WqrAB[:, c2, h, :], qaT[:, c2, 0:QB], start=(c2 == 0), stop=(c2 == 1),
                              R=[B("WqrAB"), B("qaT")], W=[PS[pj]])
                dve.tensor_tensor(QrT[i][:, 0:QB], ps[pj][:, 0:QB], csr[:, 0:QB], ALU.mult,
                                  R=[PS[pj], B("csr")], W=[B("QrT", i)])

            SBK = [0, 1, 3]
            LA = 3
            sslot = {}
            scnt = [0]

            def o_bank(h):
                if h % 2 == 0:
                    return ps[2], PS[2]
                return ps_out[:, 0:512], PSO0

            def qk(st):
                h, kc = st
                i = h % 4
                s = SBK[scnt[0] % 3]
                scnt[0] += 1
                sslot[st] = s
                pe.matmul(ps[s][:, 0:QB], ckvT[:, kc * 128:(kc + 1) * 128], QlatT[i][:, 0:QB], start=True, stop=False,
                          R=[kch[kc], B("QlatT", i)], W=[PS[s]])
                pe.matmul(ps[s][:, 0:QB], krT2[:, kc * 128:(kc + 1) * 128], QrT[i][:, 0:QB], start=False, stop=True,
                          R=[B("krT2", kc), B("QrT", i)], W=[PS[s]])

            def pv(st):
                h, kc = st
                s = sslot[st]
                o_t, o_b = o_bank(h)
                p = rot("PT", 6)
                act.activation(PT[p][:, 0:QB], ps[s][:, 0:QB], AF.Exp, scale=MLA_SCALE, R=[PS[s]], W=[B("PT", p)])
                a = 2 * (h % 2) + (kc % 2)
                if kc < 2:
                    dve.tensor_copy(denacc[a][:, 0:QB], PT[p][:, 0:QB], R=[B("PT", p)], W=[B("denacc", a)])
                else:
                    dve.tensor_tensor(denacc[a][:, 0:QB], denacc[a][:, 0:QB], PT[p][:, 0:QB], ALU.add,
                                      R=[B("PT", p), B("denacc", a)], W=[B("denacc", a)])
                pe.matmul(o_t[:, 0:QB], ckv_tok[:, kc, :], PT[p][:, 0:QB], start=(kc == 0), stop=(kc == nchunks - 1),
                          R=[B("ckv_tok", kc), B("PT", p)], W=[o_b])

            def head_tail_a(h):
                i = h % 2
                o_t, o_b = o_bank(h)
                act.copy(OlatT[i][:, 0:QB], o_t[:, 0:QB], R=[o_b], W=[B("OlatT", i)])

            def head_tail_b(h):
                i = h % 2
                pj = next_pj()
                na = min(2, nchunks)
                for a_ in range(na):
                    a = 2 * i + a_
                    pe.matmul(ps[pj][:, 0:QB], ones_f32[:], denacc[a][:, 0:QB], start=(a_ == 0), stop=(a_ == na - 1),
                              R=[B("ones_f32"), B("denacc", a)], W=[PS[pj]])
                act.activation(recip[i][:, 0:QB], ps[pj][:, 0:QB], AF.Ln, R=[PS[pj]], W=[B("recip", i)])
                act.activation(recip[i][:, 0:QB], recip[i][:, 0:QB], AF.Exp, scale=-1.0, R=[B("recip", i)],
                               W=[B("recip", i)])
                dve.tensor_tensor(recip[i][:, 0:QB], recip[i][:, 0:QB], sgT[:, h, 0:QB], ALU.mult,
                                  R=[B("recip", i), B("sgT", h)], W=[B("recip", i)])
                pj = next_pj()
                pe.matmul(ps[pj][:, 0:QB], Wv[:, h, :], OlatT[i][:, 0:QB], start=True, stop=True,
                          R=[B("Wv"), B("OlatT", i)], W=[PS[pj]])
                dve.tensor_tensor(sgT[:, h, 0:QB], ps[pj][:, 0:QB], recip[i][:, 0:QB], ALU.mult,
                                  R=[PS[pj], B("recip", i)], W=[B("sgT", h)])

            steps = [(h, kc) for h in range(16) for kc in range(nchunks)]
            ns = len(steps)
            TD = min(3, nchunks - 1)
            QD = 4 if nchunks > 4 else 2
            hq_done = [0]

            def ensure_hq(upto_step):
                while hq_done[0] < 16 and hq_done[0] * nchunks <= upto_step:
                    head_q(hq_done[0])
                    hq_done[0] += 1

            ensure_hq(LA + QD)
            for k in range(min(LA, ns)):
                qk(steps[k])
            for k in range(ns):
                h, kc = steps[k]
                pv(steps[k])
                ensure_hq(k + 1 + LA + QD)
                if k + LA < ns:
                    qk(steps[k + LA])
                if kc == nchunks - 1:
                    head_tail_a(h)
                if kc == TD and h >= 1:
                    head_tail_b(h - 1)
            head_tail_b(15)
            if stage == 3.6:
                raise _Stop()
            for h in range(16):
                wt, wb = wload(wo_scr[h], B("wo_scr", h))
                for j in range(NT):
                    for n in range(2):
                        pe.matmul(out_banks[j][n], sgT[:, h, j * 128:(j + 1) * 128],
                                  wt[:, n * 512:(n + 1) * 512], start=(h == 0), stop=(h == 15),
                                  R=[B("sgT", h), wb], W=[out_banks[j][2 + n]])
            for j in range(NT):
                r = row0 + j * 128
                epilogue(out_banks[j], 0, g, xsrc[r:r + 128, :], None, dst[r:r + 128, :], B(dstname, r))


        for b in range(NPB):
            set_pj([0, 1, 2, 3, 4, 5])
            for t in range(2):
                kpass_front(xp[b * 256 + t * 128: b * 256 + (t + 1) * 128, :], 1, t)
            for t in range(2):
                kpass_back(1, t, t, None, b * 256 + t * 128)
            qblock(xp, 1, b * 256, 256, 2, cs_prompt[:, 0:256], B("cs_prompt"), x1p, "x1p")
        if stage == 4:
            raise _Stop()
        cache_chunk(0)
        cache_chunk(1)
        set_pj([0, 1, 2, 3, 4, 5])
        for t in range(2):
            kpass_front(xs[t * 128:(t + 1) * 128, :], 0, t % 4)
        for t in range(32):
            kpass_back(0, t % 4, 2 + t, t * 128, None)
            if t + 2 < 32:
                kpass_front(xs[(t + 2) * 128:(t + 3) * 128, :], 0, (t + 2) % 4)
        for qb in range(8):
            i = qb % 2
            G.dma(SP, cstab[i][:], cs_tab[:, qb * 512:(qb + 1) * 512], W=[B("cstab", i)])
            qblock(xs, 0, qb * 512, 512, 34, cstab[i][:, 0:512], B("cstab", i), x1s, "x1s")
            for f_ in na_casts[5 * qb:5 * qb + 5]:
                f_()
        G.barrier()


    W0.close()
    if debug == 1:
        return
    if stage == 5:
        raise _Stop()
    with ExitStack() as L1:
        Kpad = sb(L1, "Kpad", [128, 16, 1024], BF16)
        Vaug = sb(L1, "Vaug", [128, 8, 8, 192], BF16)
        Kc_pad = sb(L1, "Kc_pad", [128, 16, 256], BF16)
        Vc_aug = sb(L1, "Vc_aug", [128, 2, 8, 192], BF16)
        qT = sb(L1, "qT", [128, 8, 1024], BF16)
        tabs = [sb(L1, "tabs%d" % i, [128, 2816], BF16) for i in range(2)]
        sw = sb(L1, "sw", [128, 128], F32)
        Dst = sb(L1, "Dst", [128, 512], F32)
        vst = [sb(L1, "vst%d" % i, [128, 512], F32) for i in range(2)]
        sg1 = sgT[:].rearrange("p h q -> p (h q)").rearrange("p (m t) -> p m t", m=8)

        G.dma(SP, sw[:], sw_in[:], W=[B("sw")])
        dve.memset(Kpad[:], 0.0, W=[B("Kpad", s_) for s_ in range(8)])
        dve.memset(Kc_pad[:], 0.0, W=[B("Kc_pad")])
        dve.memset(Vaug[:, :, :, 64:128], 1.0, W=[B("Vaug", s_) for s_ in range(8)])
        dve.memset(Vc_aug[:, :, :, 64:128], 1.0, W=[B("Vc_aug")])

        def runs(tiles):
            out = []
            for jj, t in enumerate(tiles):
                sl = t % 8
                if out and out[-1][1] + out[-1][2] == sl:
                    out[-1][2] += 1
                else:
                    out.append([jj, sl, 1])
            return out

        vcnt = [0]

        def proj_group(xsrc, xname, g, tiles, st_row0):
            n = len(tiles)
            N = n * 128
            rr = runs(tiles)
            hbufs1 = [B("hT", j_) for j_ in range(n)]
            set_pj([0, 1, 2, 3, 4, 5])
            for jj, t in enumerate(tiles):
                r = t * 128
                xtile, xbuf = load_x(xsrc[r:r + 128, :], B(xname, r))
                make_hT(xtile, xbuf, 1, g, jj * 128)

            def fm(widx):
                wt, wb = wload(nwi_scr[widx], B("nwi_scr", widx))
                pj = next_pj()
                for c in range(8):
                    pe.matmul(ps[pj][:, 0:N], wt[:, c * 128:(c + 1) * 128], hT[:, c, 0:N], start=(c == 0),
                              stop=(c == 7), R=[wb] + hbufs1, W=[PS[pj]])
                return wt, wb, pj

            def tm(wt, wb):
                pj = next_pj()
                for jj in range(n):
                    for c in range(8):
                        pe.matmul(ps[pj][:, jj * 128:(jj + 1) * 128], hT[:, c, jj * 128:(jj + 1) * 128],
                                  wt[:, c * 128:(c + 1) * 128], start=(c == 0), stop=(c == 7),
                                  R=[wb, B("hT", jj)], W=[PS[pj]])
                return pj

            def state_store(pj, dst, m):
                i = vcnt[0] % 2
                vcnt[0] += 1
                act.copy(vst[i][:, 0:N], ps[pj][:, 0:N], R=[PS[pj]], W=[B("vst", i)])
                for jj in range(n):
                    r = st_row0 + jj * 128
                    G.dma(POOL, dst[r:r + 128, m * 128:(m + 1) * 128], vst[i][:, jj * 128:(jj + 1) * 128],
                          R=[B("vst", i)], W=[B("stkv", vcnt[0], jj)])

            for m in range(8):
                wt, wb, pj = fm(m)
                for jj0, sl0, cn in rr:
                    act.copy(qT[:, m, sl0 * 128:(sl0 + cn) * 128], ps[pj][:, jj0 * 128:(jj0 + cn) * 128],
                             R=[PS[pj]], W=[B("qT", sl0 + x_) for x_ in range(cn)])
            for m in range(8):
                wt, wb, pj = fm(8 + m)
                for jj0, sl0, cn in rr:
                    wl = [B("Kpad", sl0 + x_) for x_ in range(cn)]
                    dve.tensor_copy(Kpad[0:64, 2 * m, sl0 * 128:(sl0 + cn) * 128],
                                    ps[pj][0:64, jj0 * 128:(jj0 + cn) * 128], R=[PS[pj]], W=wl)
                    dve.tensor_copy(Kpad[64:128, 2 * m + 1, sl0 * 128:(sl0 + cn) * 128],
                                    ps[pj][64:128, jj0 * 128:(jj0 + cn) * 128], R=[PS[pj]], W=wl)
                if st_row0 is not None:
                    pj2 = tm(wt, wb)
                    state_store(pj2, st_k, m)
            for m in range(8):
                wt, wb = wload(nwi_scr[16 + m], B("nwi_scr", 16 + m))
                pj = tm(wt, wb)
                p3 = ps[pj][:, 0:N].rearrange("p (t d) -> p t d", d=128)
                for jj0, sl0, cn in rr:
                    wl = [B("Vaug", sl0 + x_) for x_ in range(cn)]
                    dve.tensor_copy(Vaug[:, sl0:sl0 + cn, m, 0:64], p3[:, jj0:jj0 + cn, 0:64], R=[PS[pj]], W=wl)
                    dve.tensor_copy(Vaug[:, sl0:sl0 + cn, m, 128:192], p3[:, jj0:jj0 + cn, 64:128], R=[PS[pj]], W=wl)
                if st_row0 is not None:
                    state_store(pj, st_v, m)
            for m in range(8):
                wt, wb, pj = fm(24 + m)
                for jj0, sl0, cn in rr:
                    act.activation(sg1[:, m, sl0 * 128:(sl0 + cn) * 128], ps[pj][:, jj0 * 128:(jj0 + cn) * 128],
                                   AF.Silu, R=[PS[pj]], W=[B("sg1", sl0 + x_) for x_ in range(cn)])

        def na_attn(blk, QB, qtiles, chunks, g, xsrc, xname, dst, dstname):
            NT = QB // 128
            qs0 = qtiles[0] % 8
            qc0 = qs0 * 128
            qbufs = [B("qT", qs0 + x_) for x_ in range(NT)]
            sgbufs = [B("sg1", qs0 + x_) for x_ in range(NT)]
            nch = len(chunks)
            SBK = [0, 1, 4]
            LA = 3
            scnt = [0]
            sslot = {}

            def qrange(i):
                kind, idx = chunks[i]
                if kind != "r" or blk is None:
                    return 0, QB
                rows = []
                for ii in range(8):
                    r = 8 * blk + ii
                    rs = min(max(r - 4, 0), 56)
                    if any(rs <= kr <= rs + 7 for kr in (2 * idx, 2 * idx + 1)):
                        rows.append(ii)
                return 64 * rows[0], 64 * (rows[-1] + 1)

            def accs_of(m):
                if m % 2 == 0:
                    return [(ps[2], PS[2]), (ps[3], PS[3])]
                return [(ps_out[:, 0:512], PSO0), (ps_out[:, 512:1024], PSO1)]

            def load_tab(h):
                if blk is not None and h < 16:
                    G.dma(SP, tabs[h % 2][:], tab_scr[h], R=[B("tab_scr", h)], W=[B("tabs", h % 2)])

            def qk(st):
                h, i = st
                m = h // 2
                kind, idx = chunks[i]
                if kind == "r":
                    sl = idx % 8
                    lhs, lb = Kpad[:, h, sl * 128:(sl + 1) * 128], B("Kpad", sl)
                else:
                    lhs, lb = Kc_pad[:, h, idx * 128:(idx + 1) * 128], B("Kc_pad")
                sbk = SBK[scnt[0] % 3]
                scnt[0] += 1
                sslot[st] = sbk
                q0, q1 = qrange(i)
                pe.matmul(ps[sbk][:, q0:q1], lhs, qT[:, m, qc0 + q0:qc0 + q1], start=True, stop=True,
                          R=[lb] + qbufs, W=[PS[sbk]])

            def pv(st):
                h, i = st
                m, e = h // 2, h % 2
                acc_t, acc_b = accs_of(m)[e]
                kind, idx = chunks[i]
                sbk = sslot[st]
                q0, q1 = qrange(i)
                p = rot("PT", 6)
                act.activation(PT[p][:, q0:q1], ps[sbk][:, q0:q1], AF.Exp, scale=NA_SCALE, R=[PS[sbk]],
                               W=[B("PT", p)])
                if kind == "r" and blk is not None:
                    tb = tabs[h % 2]
                    u0 = 10 - 2 * (idx - 4 * blk)
                    ti_ = tb[:, u0 * 64:u0 * 64 + 512]
                    tf_ = tb[:, 1408 + u0 * 64:1408 + u0 * 64 + 512]
                    if blk == 0 and idx <= 3:
                        segs = [(0, 256, tf_), (256, 512, ti_)]
                    elif blk == 7 and idx >= 28:
                        segs = [(0, 320, ti_), (320, 512, tf_)]
                    else:
                        segs = [(0, 512, ti_)]
                    for a0, a1, tt in segs:
                        a0, a1 = max(a0, q0), min(a1, q1)
                        if a1 <= a0:
                            continue
                        dve.tensor_tensor(PT[p][:, a0:a1], PT[p][:, a0:a1], tt[:, a0:a1], ALU.mult,
                                          R=[B("PT", p), B("tabs", h % 2)], W=[B("PT", p)])
                if kind == "r":
                    sl = idx % 8
                    lhs, lb = Vaug[:, sl, m, 64 * e:64 * e + 128], B("Vaug", sl)
                else:
                    lhs, lb = Vc_aug[:, idx, m, 64 * e:64 * e + 128], B("Vc_aug")
                pe.matmul(acc_t[:, q0:q1], lhs, PT[p][:, q0:q1], start=(i == 0), stop=(i == nch - 1),
                          R=[lb, B("PT", p)], W=[acc_b])

            def tail(m):
                (A_t, A_b), (B_t, B_b) = accs_of(m)
                dve.tensor_copy(Dst[0:64, 0:QB], B_t[0:64, 0:QB], R=[B_b], W=[B("Dst")])
                dve.tensor_copy(Dst[64:128, 0:QB], A_t[64:128, 0:QB], R=[A_b], W=[B("Dst")])
                pj = 5
                pe.matmul(ps[pj][:, 0:QB], sw[:], Dst[:, 0:QB], start=True, stop=True, R=[B("sw"), B("Dst")],
                          W=[PS[pj]])
                ri = m % 2
                act.activation(recip[ri][:, 0:QB], ps[pj][:, 0:QB], AF.Ln, R=[PS[pj]], W=[B("recip", ri)])
                act.activation(recip[ri][:, 0:QB], recip[ri][:, 0:QB], AF.Exp, scale=-1.0, R=[B("recip", ri)],
                               W=[B("recip", ri)])
                dve.tensor_tensor(recip[ri][:, 0:QB], recip[ri][:, 0:QB], sg1[:, m, qc0:qc0 + QB], ALU.mult,
                                  R=[B("recip", ri)] + sgbufs, W=[B("recip", ri)])
                dve.tensor_tensor(sg1[0:64, m, qc0:qc0 + QB], A_t[0:64, 0:QB], recip[ri][0:64, 0:QB], ALU.mult,
                                  R=[A_b, B("recip", ri)], W=sgbufs)
                dve.tensor_tensor(sg1[64:128, m, qc0:qc0 + QB], B_t[64:128, 0:QB], recip[ri][64:128, 0:QB],
                                  ALU.mult, R=[B_b, B("recip", ri)], W=sgbufs)

            steps = [(h, i) for h in range(16) for i in range(nch)]
            ns = len(steps)
            TD = min(3, nch - 1)
            load_tab(0)
            load_tab(1)
            for k in range(min(LA, ns)):
                qk(steps[k])
            for k in range(ns):
                h, i = steps[k]
                pv(steps[k])
                if k + LA < ns:
                    qk(steps[k + LA])
                if i == nch - 1 and h + 2 < 16:
                    load_tab(h + 2)
                if h % 2 == 0 and h >= 2 and i == TD:
                    tail(h // 2 - 1)
            tail(7)
            for c in range(8):
                wt, wb = wload(nwo_scr[c], B("nwo_scr", c))
                for j in range(NT):
                    for n in range(2):
                        pe.matmul(out_banks[j][n], sg1[:, c, qc0 + j * 128:qc0 + (j + 1) * 128],
                                  wt[:, n * 512:(n + 1) * 512], start=(c == 0), stop=(c == 7),
                                  R=sgbufs + [wb], W=[out_banks[j][2 + n]])
            for j in range(NT):
                r = qtiles[j] * 128
                epilogue(out_banks[j], 1, g, xsrc[r:r + 128, :], B(xname, r), dst[r:r + 128, :], B(dstname, r))

        for pb in range(NPB):
            tiles = [2 * pb, 2 * pb + 1]
            proj_group(x1p, "x1p", 1, tiles, pb * 256)
            if stage == 6:
                raise _Stop()
            na_attn(None, 256, tiles, [("r", tiles[0]), ("r", tiles[1])], 1, x1p, "x1p", y_p, "y_p")
            if stage == 7:
                raise _Stop()
        if stage == 8:
            raise _Stop()
        for k in range(2):
            xtile, xbuf = load_x(c_nak[k * 128:(k + 1) * 128, :], None)
            i = rot("xn", 2)
            act.copy(xn[i][:], xtile[:], R=[xbuf], W=[B("xn", i)])
            pj = next_pj()
            psb = ps[pj][:].bitcast(BF16)
            for m in range(8):
                pe.transpose(psb[:, m * 128:(m + 1) * 128], xn[i][:, m * 128:(m + 1) * 128], ident_bf[:],
                             R=[B("xn", i), B("ident")], W=[PS[pj]])
            kc4 = Kc_pad[:].rearrange("p (m e) t -> p m e t", e=2)
            p3 = psb.rearrange("p (m t) -> p m t", m=8)
            dve.tensor_copy(kc4[0:64, :, 0, k * 128:(k + 1) * 128], p3[0:64, :, :], R=[PS[pj]], W=[B("Kc_pad")])
            dve.tensor_copy(kc4[64:128, :, 1, k * 128:(k + 1) * 128], p3[64:128, :, :], R=[PS[pj]], W=[B("Kc_pad")])
            xtile, xbuf = load_x(c_nav[k * 128:(k + 1) * 128, :], None)
            x3 = xtile[:].rearrange("p (m d) -> p m d", m=8)
            dve.tensor_copy(Vc_aug[:, k, :, 0:64], x3[:, :, 0:64], R=[xbuf], W=[B("Vc_aug")])
            dve.tensor_copy(Vc_aug[:, k, :, 128:192], x3[:, :, 64:128], R=[xbuf], W=[B("Vc_aug")])
        if stage == 9:
            raise _Stop()
        groups = [[0, 1]] + [[4 * c + 2, 4 * c + 3, 4 * c + 4, 4 * c + 5] for c in range(7)] + [[30, 31]]
        proj_group(x1s, "x1s", 0, groups[0], None)
        for b in range(8):
            proj_group(x1s, "x1s", 0, groups[b + 1], None)
            chunks = [("c", 0), ("c", 1)] + [("r", t) for t in range(max(0, 4 * b - 2), min(31, 4 * b + 5) + 1)]
            na_attn(b, 512, [4 * b, 4 * b + 1, 4 * b + 2, 4 * b + 3], chunks, 0, x1s, "x1s", y_s, "y_s")


def build_nc(debug=False, stage=99):
    gd = build_program(None, None, debug, stage)
    nc = bass.Bass("TRN2", target_bir_lowering=False)
    g = build_program(nc, gd.need, debug, stage)
    return nc, g


def rope_tables():
    t = np.arange(SEQ_S)
    pos = np.stack([t // GRID_W, t % GRID_W], axis=-1).astype(np.float32)
    inv = (10000.0 ** (-np.arange(16, dtype=np.float32) / 16)).astype(np.float32)
    ang = pos[:, :, None] * inv
    cos, sin = np.cos(ang).astype(np.float32), np.sin(ang).astype(np.float32)
    cos_full = np.concatenate([cos[:, 0], cos[:, 0], cos[:, 1], cos[:, 1]], axis=1)
    sin_full = np.concatenate([sin[:, 0], sin[:, 0], sin[:, 1], sin[:, 1]], axis=1)
    sin_signed = np.concatenate([-sin[:, 0], sin[:, 0], -sin[:, 1], sin[:, 1]], axis=1)
    cs_tab = np.ascontiguousarray(np.concatenate([cos_full, sin_full], axis=1).T)
    kt_tab = np.ascontiguousarray(np.concatenate([cos_full, sin_signed], axis=1))
    return cs_tab, kt_tab


def make_in_maps(inp):
    cs_tab, kt_tab = rope_tables()
    ident = np.eye(128, dtype=np.float32)
    j2 = np.zeros((128, 128), np.float32)
    for m in range(128):
        j2[(m // 64) * 64 + 63 - (m % 64), m] = 1.0
    selm = np.zeros((2, 256), np.float32)
    selm[0, 0:128] = 1.0
    selm[1, 128:256] = 1.0
    i2 = np.eye(2, dtype=np.float32)
    rb = inp["na_rel_bias"][0]
    rbp = np.zeros((16, 23, 127), np.float32)
    rbp[:, 4:19, 48:79] = rb[:, ::-1, ::-1]
    mask = np.zeros((128, 2, 22, 64), np.float32)
    qc = np.arange(64)
    cstart = np.clip(qc - 8, 0, 48)
    for half in range(2):
        for kcp in range(64):
            kc = 63 - kcp
            colok = (kc >= cstart) & (kc < cstart + 16)
            for ui in range(22):
                dr = half - (ui - 10)
                if -4 <= dr <= 3:
                    mask[half * 64 + kcp, 0, ui, :] = colok
                if -7 <= dr <= 7:
                    mask[half * 64 + kcp, 1, ui, :] = colok
    mask = mask.reshape(128, 2, 22 * 64)
    swm = np.zeros((128, 128), np.float32)
    for m in range(128):
        swm[(m + 64) % 128, m] = 1.0
    shared = {
        "w_ada": inp["w_ada"], "b_ada": inp["b_ada"], "pre_g": inp["pre_norm_g"], "post_g": inp["post_norm_g"],
        "mla_w_in": inp["mla_w_in"][0],
        "gq_cols": np.ascontiguousarray(inp["mla_q_norm_g"][0].reshape(2, 128).T),
        "mla_w_qb": inp["mla_w_qb"][0], "gkv": inp["mla_kv_norm_g"][0], "mla_w_kvb": inp["mla_w_kvb"][0],
        "mla_w_out": inp["mla_w_out"][0], "na_w_in": inp["na_w_in"][0], "rbp": rbp, "na_w_out": inp["na_w_out"][0],
        "ident_in": ident, "j2_in": j2, "selm_in": selm, "i2_in": i2, "cs_tab": cs_tab, "kt_tab": kt_tab,
        "mask_in": mask, "sw_in": swm,
    }
    shared = {k: np.ascontiguousarray(v, dtype=np.float32) for k, v in shared.items()}
    maps = []
    for i in range(8):
        cc = np.stack([inp["c"][i].reshape(8, 128).T, inp["c_ctx"].reshape(8, 128).T], axis=-1)
        m = dict(shared)
        m["xs"] = np.ascontiguousarray(inp["x_sample"][i])
        m["xp"] = np.ascontiguousarray(inp["x_prompt"][4 * i:4 * i + 4].reshape(1024, 1024))
        m["c_ckv"] = np.ascontiguousarray(inp["cache_mla_ckv"][i, 0])
        m["c_kr"] = np.ascontiguousarray(inp["cache_mla_krope"][i, 0])
        m["c_nak"] = np.ascontiguousarray(inp["cache_na_k"][i, 0].reshape(256, 1024))
        m["c_nav"] = np.ascontiguousarray(inp["cache_na_v"][i, 0].reshape(256, 1024))
        m["ccols"] = np.ascontiguousarray(cc.reshape(128, 16), dtype=np.float32)
        maps.append(m)
    return maps


_NC_CACHE = {}


def kernel(**inputs):
    inp = {k: np.asarray(v) for k, v in inputs.items()}
    if "nc" not in _NC_CACHE:
        _NC_CACHE["nc"] = build_nc()[0]
    nc = _NC_CACHE["nc"]
    maps = make_in_maps(inp)
    res = run_bass_kernel_spmd(nc, maps, core_ids=list(range(8)))
    r = res.results
    y_p = np.concatenate([r[i]["y_p"].reshape(4, 256, 1024) for i in range(8)], axis=0)
    y_s = np.stack([r[i]["y_s"] for i in range(8)], axis=0)
    s_ckv = np.concatenate([r[i]["st_ckv"].reshape(4, 1, 256, 128) for i in range(8)], axis=0)
    s_kr = np.concatenate([r[i]["st_kr"].reshape(4, 1, 256, 64) for i in range(8)], axis=0)
    s_k = np.concatenate([r[i]["st_k"].reshape(4, 1, 256, 16, 64) for i in range(8)], axis=0)
    s_v = np.concatenate([r[i]["st_v"].reshape(4, 1, 256, 16, 64) for i in range(8)], axis=0)
    return (y_p.astype(np.float32), y_s.astype(np.float32), s_ckv.astype(np.float32), s_kr.astype(np.float32),
            s_k.astype(np.float32), s_v.astype(np.float32))
```

```python
import numpy as np
from contextlib import ExitStack
import concourse.bass as bass
import concourse.mybir as mybir
from concourse.bass_utils import run_bass_kernel_spmd

F32 = mybir.dt.float32
BF16 = mybir.dt.bfloat16
AF = mybir.ActivationFunctionType
ALU = mybir.AluOpType

PE, ACT, DVE, POOL, SP = "pe", "act", "dve", "pool", "sp"
ENGS = (PE, ACT, DVE, POOL, SP)
NDS = 24

D = 1024
SEQ_S = 4096
SEQ_P = 256
NPB = 4
EPS = 1e-6
MLA_SCALE = 192.0 ** -0.5
NA_SCALE = 0.125
GRID_W = 64


class _Stop(Exception):
    pass


class Dummy:
    def __getitem__(self, k):
        return self

    def __getattr__(self, k):
        return lambda *a, **kw: self


class Buf:
    __slots__ = ("name", "w", "r")

    def __init__(self, name):
        self.name = name
        self.w = None
        self.r = {}


class EngProxy:
    def __init__(self, gen, name, obj):
        self._g = gen
        self._n = name
        self._o = obj

    def __getattr__(self, op):
        g, n, o = self._g, self._n, self._o

        def f(*args, R=(), W=(), **kw):
            return g.emit(n, (lambda: getattr(o, op)(*args, **kw)), R, W)
        return f


class Gen:
    def __init__(self, nc, marked, es):
        self.nc = nc
        self.dry = nc is None
        self.marked = marked if marked is not None else set()
        self.need = set()
        self.idx = {e: 0 for e in ENGS}
        self.sigcount = {e: 0 for e in ENGS}
        self.sigval = {}
        self.waited = {e: {} for e in ENGS}
        self.dma_val = {}
        self.dma_rr = {SP: 0, POOL: 0, ACT: 0}
        self.bufs = {}
        self.ninst = 0
        if self.dry:
            self.engobj = {e: Dummy() for e in ENGS}
            self.sem = {}
            self.dsem = {}
        else:
            self.engobj = {PE: nc.tensor, ACT: nc.scalar, DVE: nc.vector, POOL: nc.gpsimd, SP: nc.sync}
            self.sem = {e: es.enter_context(nc.semaphore("s_" + e)) for e in (PE, ACT, DVE, POOL)}
            self.dsem = {}
            for q in (SP, POOL):
                for k in range(NDS):
                    self.dsem[(q, k)] = es.enter_context(nc.semaphore("d_%s_%d" % (q, k)))
        self.pe = EngProxy(self, PE, self.engobj[PE])
        self.act = EngProxy(self, ACT, self.engobj[ACT])
        self.dve = EngProxy(self, DVE, self.engobj[DVE])
        self.pool = EngProxy(self, POOL, self.engobj[POOL])

    def B(self, *key):
        b = self.bufs.get(key)
        if b is None:
            b = Buf(key)
            self.bufs[key] = b
        return b

    def wait(self, eng, ev):
        if ev[0] == "c":
            _, pe_, pi = ev
            if self.dry:
                self.need.add((pe_, pi))
                return
            val = self.sigval[(pe_, pi)]
            key = ("c", pe_)
            if self.waited[eng].get(key, 0) >= val:
                return
            self.engobj[eng].wait_ge(self.sem[pe_], val)
            self.waited[eng][key] = val
        else:
            _, qk, val = ev
            key = ("d", qk)
            if self.waited[eng].get(key, 0) >= val:
                return
            if not self.dry:
                self.engobj[eng].wait_ge(self.dsem[qk], val)
            self.waited[eng][key] = val

    def _deps(self, eng, R, W):
        for b in R:
            if b.w is not None:
                self._dep(eng, b.w, True)
            if b.name[0] in ("ps", "ps_out"):
                for k, ev in list(b.r.items()):
                    if k != eng:
                        self._dep(eng, ev, False)
        for b in W:
            if b.w is not None:
                self._dep(eng, b.w, False)
            for ev in list(b.r.values()):
                self._dep(eng, ev, False)

    def _dep(self, eng, ev, raw):
        if ev[0] == "c" and ev[1] == eng:
            if eng == PE or not raw:
                return
        self.wait(eng, ev)

    def emit(self, eng, fn, R=(), W=()):
        self._deps(eng, R, W)
        i = self.idx[eng]
        self.idx[eng] += 1
        self.ninst += 1
        me = ("c", eng, i)
        if not self.dry:
            ins = fn()
            if (eng, i) in self.marked:
                ins.then_inc(self.sem[eng], 1)
                self.sigcount[eng] += 1
                self.sigval[(eng, i)] = self.sigcount[eng]
        for b in R:
            b.r[eng] = me
        for b in W:
            b.w = me
            b.r = {}
        return me

    def dma(self, q, out, in_, R=(), W=(), **kw):
        self._deps(q, R, W)
        k = self.dma_rr[q]
        self.dma_rr[q] = (k + 1) % (NDS if q == SP else 4)
        qk = (q, k)
        prev = self.dma_val.get(qk, 0)
        if prev > 0:
            self.wait(q, ("d", qk, prev))
        val = prev + 16
        self.dma_val[qk] = val
        self.ninst += 1
        if not self.dry:
            self.engobj[q].dma_start(out=out, in_=in_, **kw).then_inc(self.dsem[qk], 16)
        me = ("d", qk, val)
        for b in R:
            b.r[("d", qk)] = me
        for b in W:
            b.w = me
            b.r = {}
        return me

    def barrier(self):
        for e in ENGS:
            for o in (PE, ACT, DVE, POOL):
                if o != e and self.idx[o] > 0:
                    self.wait(e, ("c", o, self.idx[o] - 1))
            for qk, v in self.dma_val.items():
                self.wait(e, ("d", qk, v))

    def finish(self):
        for qk, v in self.dma_val.items():
            self.wait(SP, ("d", qk, v))


def build_program(nc, marked, debug=False, stage=99):
    es = ExitStack()
    G = Gen(nc, marked, es)
    try:
        _build_body(nc, G, es, debug, stage)
    except _Stop:
        G.finish()
        return G
    G.finish()
    es.close()
    return G


def _build_body(nc, G, es, debug, stage):
    dry = nc is None
    B = G.B
    pe, act, dve = G.pe, G.act, G.dve

    def dram(name, shape, dt=F32, kind="Internal"):
        if dry:
            return Dummy()
        if kind == "Internal":
            return nc.dram_tensor(name, list(shape), dt).ap()
        return nc.dram_tensor(name, list(shape), dt, kind=kind).ap()

    def sb(stack, name, shape, dt):
        if dry:
            return Dummy()
        return stack.enter_context(nc.sbuf_tensor(name, list(shape), dt))

    def psum(stack, name, shape, dt=F32):
        if dry:
            return Dummy()
        return stack.enter_context(nc.psum_tensor(name, list(shape), dt))

    IN = lambda n, s: dram(n, s, F32, "ExternalInput")
    OUT = lambda n, s: dram(n, s, F32, "ExternalOutput")

    xs = IN("xs", [SEQ_S, D])
    xp = IN("xp", [NPB * SEQ_P, D])
    c_ckv = IN("c_ckv", [256, 128])
    c_kr = IN("c_kr", [256, 64])
    c_nak = IN("c_nak", [256, 1024])
    c_nav = IN("c_nav", [256, 1024])
    ccols = IN("ccols", [128, 16])
    w_ada = IN("w_ada", [2, 1024, 3072])
    b_ada = IN("b_ada", [2, 3072])
    pre_g = IN("pre_g", [2, 1024])
    post_g = IN("post_g", [2, 1024])
    mla_w_in = IN("mla_w_in", [1024, 2496])
    gq_cols = IN("gq_cols", [128, 2])
    mla_w_qb = IN("mla_w_qb", [256, 3072])
    gkv = IN("gkv", [128])
    mla_w_kvb = IN("mla_w_kvb", [128, 4096])
    mla_w_out = IN("mla_w_out", [2048, 1024])
    na_w_in = IN("na_w_in", [1024, 4096])
    rbp = IN("rbp", [16, 23, 127])
    na_w_out = IN("na_w_out", [1024, 1024])
    ident_in = IN("ident_in", [128, 128])
    j2_in = IN("j2_in", [128, 128])
    selm_in = IN("selm_in", [2, 256])
    i2_in = IN("i2_in", [2, 2])
    cs_tab = IN("cs_tab", [128, SEQ_S])
    kt_tab = IN("kt_tab", [SEQ_S, 128])
    mask_in = IN("mask_in", [128, 2, 22 * 64])
    sw_in = IN("sw_in", [128, 128])

    y_s = OUT("y_s", [SEQ_S, D])
    y_p = OUT("y_p", [NPB * SEQ_P, D])
    st_ckv = OUT("st_ckv", [NPB * SEQ_P, 128])
    st_kr = OUT("st_kr", [NPB * SEQ_P, 64])
    st_k = OUT("st_k", [NPB * SEQ_P, 1024])
    st_v = OUT("st_v", [NPB * SEQ_P, 1024])

    x1s = dram("x1s", [SEQ_S, D]) if not debug else OUT("x1s", [SEQ_S, D])
    x1p = dram("x1p", [NPB * SEQ_P, D]) if not debug else OUT("x1p", [NPB * SEQ_P, D])
    wg_scr = dram("wg_scr", [16, 128, 8 * 128], BF16)
    wo_scr = dram("wo_scr", [16, 128, 1024], BF16)
    nwi_scr = dram("nwi_scr", [32, 128, 8 * 128], BF16)
    nwo_scr = dram("nwo_scr", [8, 128, 1024], BF16)
    tab_scr = dram("tab_scr", [16, 128, 2 * 22 * 64], BF16)

    P0 = ExitStack()
    es.enter_context(P0)
    ident_bf = sb(P0, "ident_bf", [128, 128], BF16)
    ones_bf = sb(P0, "ones_bf", [128, 128], BF16)
    j2_bf = sb(P0, "j2_bf", [128, 128], BF16)
    selm = sb(P0, "selm", [2, 256], F32)
    i2 = sb(P0, "i2", [2, 2], F32)
    scs = sb(P0, "scs", [128, 16], F32)
    modcol = sb(P0, "modcol", [128, 2, 32], F32)
    Gbc = sb(P0, "Gbc", [128, 2, 2, 1024], F32)
    epsc = sb(P0, "epsc", [128, 1], F32)
    xt = [sb(P0, "xt%d" % i, [128, 1024], F32) for i in range(3)]
    xn = [sb(P0, "xn%d" % i, [128, 1024], BF16) for i in range(2)]
    junk = sb(P0, "junk", [128, 1024], BF16)
    ytmp = sb(P0, "ytmp", [128, 1024], F32)
    stats = [sb(P0, "stat%d" % i, [128, 4], F32) for i in range(8)]
    hT = sb(P0, "hT", [128, 8, 512], BF16)
    wpool = [sb(P0, "wp%d" % i, [128, 1024], BF16) for i in range(6)]
    PT = [sb(P0, "PT%d" % i, [128, 512], BF16) for i in range(6)]
    ones_f32 = sb(P0, "ones_f32", [128, 128], F32)
    recip = [sb(P0, "recip%d" % i, [128, 512], F32) for i in range(2)]
    sgT = sb(P0, "sgT", [128, 16, 512], BF16)

    ps = [psum(P0, "ps%d" % i, [128, 512]) for i in range(6)]
    ps_out = psum(P0, "ps_out", [128, 1024])
    PS = [B("ps", i) for i in range(6)]
    PSO = B("ps_out")
    PSO0 = B("ps_out", 0)
    PSO1 = B("ps_out", 1)

    cnt = {"xt": 0, "xn": 0, "st": 0, "wp": 0, "pj": 0, "PT": 0}

    def rot(name, n):
        v = cnt[name]
        cnt[name] = (v + 1) % n
        return v

    pjbanks = [[4, 5]]
    pjc = [0]

    def next_pj():
        v = pjbanks[0][pjc[0] % len(pjbanks[0])]
        pjc[0] += 1
        return v

    def set_pj(lst):
        pjbanks[0] = lst

    G.dma(POOL, ident_bf[:], ident_in[:], W=[B("ident")])
    G.dma(POOL, j2_bf[:], j2_in[:], W=[B("j2")])
    G.dma(SP, selm[:], selm_in[:], W=[B("selm")])
    G.dma(SP, i2[:], i2_in[:], W=[B("i2")])
    dve.memset(ones_bf[:], 1.0, W=[B("ones")])
    dve.memset(ones_f32[:], 1.0, W=[B("ones_f32")])
    dve.memset(epsc[:], EPS, W=[B("epsc")])

    W0 = ExitStack()
    es.enter_context(W0)
    w_in_qk = sb(W0, "w_in_qk", [128, 8, 448], BF16)
    wqb_bf = sb(W0, "wqb_bf", [128, 2, 3072], BF16)
    wkvb_bf = sb(W0, "wkvb_bf", [128, 4096], BF16)
    set_pj([0, 1, 2, 3, 4, 5])
    TP = ExitStack()
    if True:
        mask_bf = sb(TP, "mask_bf", [128, 2, 1408], BF16)
        Hraw = [sb(TP, "Hraw%d" % i, [128, 1408], F32) for i in range(1)]
        Tp = [sb(TP, "Tp%d" % i, [128, 2816], BF16) for i in range(1)]
        Tfin = [sb(TP, "Tfin%d" % i, [128, 2816], BF16) for i in range(1)]
        G.dma(POOL, mask_bf[:], mask_in[:], W=[B("mask_bf")])
        for c in range(8):
            G.dma(POOL, w_in_qk[:, c, :], mla_w_in[c * 128:(c + 1) * 128, 0:448], W=[B("w_in_qk")])
        for c in range(2):
            for hh in range(2):
                G.dma(POOL, wqb_bf[:, c, hh * 1536:(hh + 1) * 1536],
                      mla_w_qb[c * 128:(c + 1) * 128, hh * 1536:(hh + 1) * 1536], W=[B("wqb_bf")])
        for hh in range(2):
            G.dma(POOL, wkvb_bf[:, hh * 2048:(hh + 1) * 2048], mla_w_kvb[:, hh * 2048:(hh + 1) * 2048],
                  W=[B("wkvb_bf")])
        for h in range(16):
            G.dma(POOL, wg_scr[h].rearrange("p (c j) -> p c j", c=8),
                  mla_w_in[:, 448 + h * 128: 448 + (h + 1) * 128].rearrange("(c p) j -> p c j", p=128),
                  W=[B("wg_scr", h)])
        for h in range(16):
            G.dma(POOL, wo_scr[h], mla_w_out[h * 128:(h + 1) * 128, :], W=[B("wo_scr", h)])
        na_casts = []
        for m in range(32):
            na_casts.append(lambda m=m: G.dma(
                POOL, nwi_scr[m].rearrange("p (c j) -> p c j", c=8),
                na_w_in[:, m * 128:(m + 1) * 128].rearrange("(c p) j -> p c j", p=128), W=[B("nwi_scr", m)]))
        for c in range(8):
            na_casts.append(lambda c=c: G.dma(POOL, nwo_scr[c], na_w_out[c * 128:(c + 1) * 128, :],
                                              W=[B("nwo_scr", c)]))


        def table_head(h):
            hb = 0
            tbanks = [(ps_out[:, 0:512], PSO0), (ps_out[:, 512:1024], PSO1)]
            for half in range(2):
                if dry:
                    src = Dummy()
                else:
                    src = bass.AP(rbp.tensor, h * 23 * 127 + (1 - half) * 127, [[1, 64], [127, 22], [1, 64]])
                G.dma(SP, Hraw[hb][half * 64:(half + 1) * 64, :].rearrange("p (u q) -> p u q", u=22), src,
                      W=[B("Hraw", hb)])
            act.activation(Hraw[hb][:], Hraw[hb][:], AF.Exp, R=[B("Hraw", hb)], W=[B("Hraw", hb)])
            for k in range(2):
                dve.tensor_tensor(Tp[hb][:, k * 1408:(k + 1) * 1408], Hraw[hb][:], mask_bf[:, k, :], ALU.mult,
                                  R=[B("Hraw", hb), B("mask_bf")], W=[B("Tp", hb)])
            for n in range(6):
                w = min(512, 2816 - n * 512)
                tb_t, tb_b = tbanks[n % 2]
                pe.matmul(tb_t[:, 0:w], j2_bf[:], Tp[hb][:, n * 512:n * 512 + w], start=True, stop=True,
                          R=[B("j2"), B("Tp", hb)], W=[tb_b])
                if n % 2 == 0:
                    dve.tensor_copy(Tfin[hb][:, n * 512:n * 512 + w], tb_t[:, 0:w], R=[tb_b], W=[B("Tfin", hb)])
                else:
                    act.copy(Tfin[hb][:, n * 512:n * 512 + w], tb_t[:, 0:w], R=[tb_b], W=[B("Tfin", hb)])
            G.dma(SP, tab_scr[h], Tfin[hb][:], R=[B("Tfin", hb)], W=[B("tab_scr", h)])

    set_pj([4, 5])
    if stage == 1:
        raise _Stop()
    with ExitStack() as T0:
        wa = [sb(T0, "wa%d" % i, [128, 3072], F32) for i in range(2)]
        mod_sb = sb(T0, "mod_sb", [2, 3072], F32)
        bada = sb(T0, "bada", [2, 3072], F32)
        gpre2 = sb(T0, "gpre2", [2, 1024], F32)
        gpost2 = sb(T0, "gpost2", [2, 1024], F32)
        Arow = sb(T0, "Arow", [2, 1024], F32)
        Grow = sb(T0, "Grow", [2, 1024], F32)
        scin = sb(T0, "scin", [128, 16], F32)
        G.dma(SP, scin[:], ccols[:], W=[B("scin")])
        act.activation(scs[:], scin[:], AF.Silu, R=[B("scin")], W=[B("scs")])
        for l in range(2):
            G.dma(SP, bada[:], b_ada[l].partition_broadcast(2), W=[B("bada")])
            G.dma(SP, gpre2[:], pre_g[l].partition_broadcast(2), W=[B("gpre2")])
            G.dma(SP, gpost2[:], post_g[l].partition_broadcast(2), W=[B("gpost2")])
            G.dma(SP, wa[0][:], w_ada[l, 0:128, :], W=[B("wa", 0)])
            for k in range(8):
                wb = k % 2
                if k + 1 < 8:
                    G.dma(SP, wa[1 - wb][:], w_ada[l, (k + 1) * 128:(k + 2) * 128, :], W=[B("wa", 1 - wb)])
                for n in range(6):
                    pe.matmul(ps[n][0:2, :], scs[:, 2 * k:2 * k + 2], wa[wb][:, n * 512:(n + 1) * 512],
                              start=(k == 0), stop=(k == 7), R=[B("scs"), B("wa", wb)], W=[PS[n]])
                table_head(l * 8 + k)
            for n in range(6):
                dve.tensor_tensor(mod_sb[:, n * 512:(n + 1) * 512], ps[n][0:2, :], bada[:, n * 512:(n + 1) * 512],
                                  ALU.add, R=[PS[n], B("bada")], W=[B("mod_sb")])
            dve.scalar_tensor_tensor(Arow[:], mod_sb[:, 1024:2048], 1.0, gpre2[:], ALU.add, ALU.mult,
                                     R=[B("mod_sb"), B("gpre2")], W=[B("Arow")])
            dve.tensor_tensor(Grow[:], mod_sb[:, 2048:3072], gpost2[:], ALU.mult,
                              R=[B("mod_sb"), B("gpost2")], W=[B("Grow")])
            for c in range(8):
                pe.matmul(ps[0][:, 2 * c:2 * c + 2], Arow[:, c * 128:(c + 1) * 128], i2[:], start=True, stop=True,
                          R=[B("Arow"), B("i2")], W=[PS[0]])
                pe.matmul(ps[0][:, 16 + 2 * c:16 + 2 * c + 2], mod_sb[:, c * 128:(c + 1) * 128], i2[:],
                          start=True, stop=True, R=[B("mod_sb"), B("i2")], W=[PS[0]])
            dve.tensor_copy(modcol[:, l, :], ps[0][:, 0:32], R=[PS[0]], W=[B("modcol", l)])
            for g in range(2):
                for n in range(2):
                    pe.matmul(ps[1 + n][:, :], selm[:, g * 128:(g + 1) * 128], Grow[:, n * 512:(n + 1) * 512],
                              start=True, stop=True, R=[B("selm"), B("Grow")], W=[PS[1 + n]])
                    dve.tensor_copy(Gbc[:, l, g, n * 512:(n + 1) * 512], ps[1 + n][:, :], R=[PS[1 + n]],
                                    W=[B("Gbc", l, g)])
        G.barrier()
    TP.close()

    if stage == 2:
        raise _Stop()
    def load_x(src_ap, srcbuf):
        i = rot("xt", 3)
        G.dma(SP, xt[i][:], src_ap, R=[srcbuf] if srcbuf is not None else [], W=[B("xt", i)])
        return xt[i], B("xt", i)

    def rstd_of(src_ap, srcbuf, n, width_scale):
        i = rot("st", 8)
        st = stats[i]
        sbuf = B("stat", i)
        act.activation(junk[:, 0:n], src_ap, AF.Square, accum_out=st[:, 0:1], R=[srcbuf], W=[sbuf, B("junk")])
        act.activation(st[:, 1:2], st[:, 0:1], AF.Ln, bias=epsc[:], scale=width_scale, R=[sbuf, B("epsc")], W=[sbuf])
        act.activation(st[:, 2:3], st[:, 1:2], AF.Exp, scale=-0.5, R=[sbuf], W=[sbuf])
        return st[:, 2:3], sbuf

    def make_hT(xtile, xbuf, l, g, col0):
        r_ap, rbuf = rstd_of(xtile[:], xbuf, 1024, 1.0 / D)
        if stage == 3.01:
            raise _Stop()
        i = rot("xn", 2)
        act.activation(xn[i][:], xtile[:], AF.Identity, scale=r_ap, R=[xbuf, rbuf], W=[B("xn", i)])
        if stage == 3.02:
            raise _Stop()
        pj = next_pj()
        psb = ps[pj][:].bitcast(BF16)
        for c in range(8):
            pe.transpose(psb[:, c * 128:(c + 1) * 128], xn[i][:, c * 128:(c + 1) * 128], ident_bf[:],
                         R=[B("xn", i), B("ident")], W=[PS[pj]])
        if stage == 3.03:
            raise _Stop()
        for c in range(8):
            eng = dve if c % 2 == 0 else dve
            eng.tensor_scalar(hT[:, c, col0:col0 + 128], psb[:, c * 128:(c + 1) * 128],
                              modcol[:, l, 2 * c + g:2 * c + g + 1], modcol[:, l, 16 + 2 * c + g:16 + 2 * c + g + 1],
                              ALU.mult, ALU.add, R=[PS[pj], B("modcol", l)], W=[B("hT", col0 // 128)])

    def wload(src_ap, srcbuf):
        i = rot("wp", 6)
        G.dma(SP, wpool[i][:], src_ap, R=[srcbuf], W=[B("wp", i)])
        return wpool[i], B("wp", i)

    def epilogue(bk, l, g, xsrc_ap, xsrcbuf, dst_ap, dstbuf):
        a0, a1, b0, b1 = bk
        i = rot("st", 8)
        st = stats[i]
        sbuf = B("stat", i)
        act.activation(junk[:, 0:512], a0, AF.Square, accum_out=st[:, 0:1], R=[b0], W=[sbuf, B("junk")])
        act.activation(junk[:, 512:1024], a1, AF.Square, accum_out=st[:, 3:4], R=[b1], W=[sbuf, B("junk")])
        dve.tensor_tensor(st[:, 0:1], st[:, 0:1], st[:, 3:4], ALU.add, R=[sbuf], W=[sbuf])
        act.activation(st[:, 1:2], st[:, 0:1], AF.Ln, bias=epsc[:], scale=1.0 / D, R=[sbuf, B("epsc")], W=[sbuf])
        act.activation(st[:, 2:3], st[:, 1:2], AF.Exp, scale=-0.5, R=[sbuf], W=[sbuf])
        r_ap = st[:, 2:3]
        xr, xrb = load_x(xsrc_ap, xsrcbuf)
        dve.scalar_tensor_tensor(ytmp[:, 0:512], a0, r_ap, Gbc[:, l, g, 0:512], ALU.mult, ALU.mult,
                                 R=[b0, sbuf, B("Gbc", l, g)], W=[B("ytmp")])
        dve.scalar_tensor_tensor(ytmp[:, 512:1024], a1, r_ap, Gbc[:, l, g, 512:1024], ALU.mult, ALU.mult,
                                 R=[b1, sbuf, B("Gbc", l, g)], W=[B("ytmp")])
        dve.tensor_tensor(xr[:], ytmp[:], xr[:], ALU.add, R=[B("ytmp"), xrb], W=[xrb])
        G.dma(POOL, dst_ap, xr[:], R=[xrb], W=[dstbuf])

    out_banks = [(ps_out[:, 0:512], ps_out[:, 512:1024], PSO0, PSO1),
                 (ps[0][:, :], ps[1][:, :], PS[0], PS[1]),
                 (ps[2][:, :], ps[3][:, :], PS[2], PS[3]),
                 (ps[4][:, :], ps[5][:, :], PS[4], PS[5])]

    with ExitStack() as L0:
        Wqlat = sb(L0, "Wqlat", [128, 2, 16, 128], BF16)
        WqrAB = sb(L0, "WqrAB", [128, 2, 16, 128], BF16)
        Wv = sb(L0, "Wv", [128, 16, 128], BF16)
        gqc = sb(L0, "gqc", [128, 2], F32)
        gkv_bc = sb(L0, "gkv_bc", [128, 128], F32)
        cs_prompt = sb(L0, "cs_prompt", [128, 256], F32)
        ckvT = sb(L0, "ckvT", [128, 34 * 128], BF16)
        krT2 = sb(L0, "krT2", [128, 34 * 128], BF16)
        ckv_tok = sb(L0, "ckv_tok", [128, 34, 128], BF16)
        qaT = sb(L0, "qaT", [128, 2, 512], BF16)
        sq = sb(L0, "sq", [128, 2, 512], BF16)
        rstd_bc = sb(L0, "rstd_bc", [128, 512], F32)
        csr = sb(L0, "csr", [128, 512], F32)
        cstab = [sb(L0, "cstab%d" % i, [128, 512], F32) for i in range(2)]
        QlatT = [sb(L0, "QlatT%d" % i, [128, 512], BF16) for i in range(4)]
        QrT = [sb(L0, "QrT%d" % i, [128, 512], BF16) for i in range(4)]
        OlatT = [sb(L0, "OlatT%d" % i, [128, 512], BF16) for i in range(2)]
        denacc = [sb(L0, "denacc%d" % i, [128, 512], F32) for i in range(4)]
        kv32 = [sb(L0, "kv32_%d" % i, [128, 128], F32) for i in range(2)]
        kr32 = [sb(L0, "kr32_%d" % i, [128, 64], F32) for i in range(2)]
        krf = sb(L0, "krf", [128, 64], F32)
        krtmp = sb(L0, "krtmp", [128, 64], F32)
        krf2 = [sb(L0, "krf2_%d" % i, [128, 128], BF16) for i in range(2)]
        ctab = [sb(L0, "ctab%d" % i, [128, 128], F32) for i in range(2)]
        cache32 = sb(L0, "cache32", [128, 192], F32)

        with ExitStack() as T1:
            WqnT = sb(T1, "WqnT", [128, 16, 256], BF16)
            WkT = sb(T1, "WkT", [128, 16, 128], BF16)
            G.dma(SP, gqc[:], gq_cols[:], W=[B("gqc")])
            G.dma(SP, gkv_bc[:], gkv.partition_broadcast(128), W=[B("gkv_bc")])
            dve.memset(cs_prompt[0:64, :], 1.0, W=[B("cs_prompt")])
            dve.memset(cs_prompt[64:128, :], 0.0, W=[B("cs_prompt")])
            dve.tensor_copy(Wv[:], wkvb_bf[:].rearrange("p (h t) -> p h t", h=16)[:, :, 128:256],
                            R=[B("wkvb_bf")], W=[B("Wv")])
            for j in range(4):
                pj = next_pj()
                psb = ps[pj][:].bitcast(BF16)
                for hh in range(4):
                    h = 4 * j + hh
                    for c in range(2):
                        pe.transpose(psb[:, (hh * 2 + c) * 128:(hh * 2 + c + 1) * 128],
                                     wqb_bf[:, c, h * 192:h * 192 + 128], ident_bf[:],
                                     R=[B("wqb_bf"), B("ident")], W=[PS[pj]])
                dve.tensor_copy(WqnT[:, 4 * j:4 * j + 4, :], psb.rearrange("p (h t) -> p h t", h=4),
                                R=[PS[pj]], W=[B("WqnT")])
            for j in range(2):
                pj = next_pj()
                psb = ps[pj][:].bitcast(BF16)
                for hh in range(8):
                    h = 8 * j + hh
                    pe.transpose(psb[:, hh * 128:(hh + 1) * 128], wkvb_bf[:, h * 256:h * 256 + 128], ident_bf[:],
                                 R=[B("wkvb_bf"), B("ident")], W=[PS[pj]])
                dve.tensor_copy(WkT[:, 8 * j:8 * j + 8, :], psb.rearrange("p (h t) -> p h t", h=8),
                                R=[PS[pj]], W=[B("WkT")])
            for c in range(2):
                for j in range(4):
                    pj = next_pj()
                    for hh in range(4):
                        h = 4 * j + hh
                        pe.matmul(ps[pj][:, hh * 128:(hh + 1) * 128], WqnT[:, h, c * 128:(c + 1) * 128], WkT[:, h, :],
                                  start=True, stop=True, R=[B("WqnT"), B("WkT")], W=[PS[pj]])
                    dve.tensor_scalar(Wqlat[:, c, 4 * j:4 * j + 4, :], ps[pj][:].rearrange("p (h t) -> p h t", h=4),
                                      gqc[:, c:c + 1], None, ALU.mult, R=[PS[pj], B("gqc")], W=[B("Wqlat")])
            for c in range(2):
                w4 = wqb_bf[:, c, :].rearrange("p (h t) -> p h t", h=16)
                dve.tensor_scalar(WqrAB[:, c, :, 0:64], w4[:, :, 128:192], gqc[:, c:c + 1], None, ALU.mult,
                                  R=[B("wqb_bf"), B("gqc")], W=[B("WqrAB")])
                for a in range(2):
                    o = 32 * a
                    dve.tensor_scalar(WqrAB[:, c, :, 64 + o:64 + o + 16], w4[:, :, 128 + o + 16:128 + o + 32],
                                      gqc[:, c:c + 1], -1.0, ALU.mult, ALU.mult,
                                      R=[B("wqb_bf"), B("gqc")], W=[B("WqrAB")])
                    dve.tensor_scalar(WqrAB[:, c, :, 64 + o + 16:64 + o + 32], w4[:, :, 128 + o:128 + o + 16],
                                      gqc[:, c:c + 1], None, ALU.mult,
                                      R=[B("wqb_bf"), B("gqc")], W=[B("WqrAB")])
            G.barrier()

        if stage == 3:
            raise _Stop()

        def kpass_front(xsrc_ap, g, slot):
            xtile, xbuf = load_x(xsrc_ap, None)
            make_hT(xtile, xbuf, 0, g, slot * 128)

        def kpass_back(g, slot, chunk, rope_pos, st_row):
            pj = next_pj()
            for c in range(8):
                pe.matmul(ps[pj][:, 0:192], hT[:, c, slot * 128:(slot + 1) * 128], w_in_qk[:, c, 256:448],
                          start=(c == 0), stop=(c == 7), R=[B("hT", slot), B("w_in_qk")], W=[PS[pj]])
            r_ap, rbuf = rstd_of(ps[pj][:, 0:128], PS[pj], 128, 1.0 / 128)
            i = chunk % 2
            dve.scalar_tensor_tensor(kv32[i][:], ps[pj][:, 0:128], r_ap, gkv_bc[:], ALU.mult, ALU.mult,
                                     R=[PS[pj], rbuf, B("gkv_bc")], W=[B("kv32", i)])
            act.copy(ckv_tok[:, chunk, :], kv32[i][:], R=[B("kv32", i)], W=[B("ckv_tok", chunk)])
            dve.tensor_copy(kr32[i][:], ps[pj][:, 128:192], R=[PS[pj]], W=[B("kr32", i)])
            if st_row is not None:
                G.dma(POOL, st_ckv[st_row:st_row + 128, :], kv32[i][:], R=[B("kv32", i)], W=[B("st_ckv", st_row)])
                G.dma(POOL, st_kr[st_row:st_row + 128, :], kr32[i][:], R=[B("kr32", i)], W=[B("st_kr", st_row)])
            if rope_pos is not None:
                G.dma(SP, ctab[i][:], kt_tab[rope_pos:rope_pos + 128, :], W=[B("ctab", i)])
                dve.tensor_tensor(krf[:], kr32[i][:], ctab[i][:, 0:64], ALU.mult,
                                  R=[B("kr32", i), B("ctab", i)], W=[B("krf")])
                k4 = kr32[i][:].rearrange("p (a s f) -> p a s f", a=2, s=2)
                t4 = krtmp[:].rearrange("p (a s f) -> p a s f", a=2, s=2)
                s4 = ctab[i][:, 64:128].rearrange("p (a s f) -> p a s f", a=2, s=2)
                dve.tensor_tensor(t4[:, :, 0, :], k4[:, :, 1, :], s4[:, :, 0, :], ALU.mult,
                                  R=[B("kr32", i), B("ctab", i)], W=[B("krtmp")])
                dve.tensor_tensor(t4[:, :, 1, :], k4[:, :, 0, :], s4[:, :, 1, :], ALU.mult,
                                  R=[B("kr32", i), B("ctab", i)], W=[B("krtmp")])
                dve.tensor_tensor(krf[:], krf[:], krtmp[:], ALU.add, R=[B("krf"), B("krtmp")], W=[B("krf")])
                src, sbuf_ = krf, B("krf")
            else:
                src, sbuf_ = kr32[i], B("kr32", i)
            dve.tensor_copy(krf2[i][:, 0:64], src[:], R=[sbuf_], W=[B("krf2", i)])
            dve.tensor_copy(krf2[i][:, 64:128], src[:], R=[sbuf_], W=[B("krf2", i)])
            k_transposes(chunk, krf2[i], B("krf2", i))

        def k_transposes(chunk, krsrc, krbuf):
            pj = next_pj()
            psb = ps[pj][:].bitcast(BF16)
            pe.transpose(psb[:, 0:128], ckv_tok[:, chunk, :], ident_bf[:], R=[B("ckv_tok", chunk), B("ident")],
                         W=[PS[pj]])
            pe.transpose(psb[:, 128:256], krsrc[:], ident_bf[:], R=[krbuf, B("ident")], W=[PS[pj]])
            dve.tensor_copy(ckvT[:, chunk * 128:(chunk + 1) * 128], psb[:, 0:128], R=[PS[pj]], W=[B("ckvT", chunk)])
            dve.tensor_copy(krT2[:, chunk * 128:(chunk + 1) * 128], psb[:, 128:256], R=[PS[pj]],
                            W=[B("krT2", chunk)])

        def cache_chunk(chunk):
            i = chunk % 2
            G.dma(SP, cache32[:, 0:128], c_ckv[chunk * 128:(chunk + 1) * 128, :], W=[B("cache32")])
            G.dma(SP, cache32[:, 128:192], c_kr[chunk * 128:(chunk + 1) * 128, :], W=[B("cache32")])
            act.copy(ckv_tok[:, chunk, :], cache32[:, 0:128], R=[B("cache32")], W=[B("ckv_tok", chunk)])
            dve.tensor_copy(krf2[i][:, 0:64], cache32[:, 128:192], R=[B("cache32")], W=[B("krf2", i)])
            dve.tensor_copy(krf2[i][:, 64:128], cache32[:, 128:192], R=[B("cache32")], W=[B("krf2", i)])
            k_transposes(chunk, krf2[i], B("krf2", i))

        def qblock(xsrc, g, row0, QB, nchunks, cs_ap, csbuf, dst, dstname):
            NT = QB // 128
            hbufs = [B("hT", j_) for j_ in range(NT)]
            set_pj([0, 1, 2, 3, 4, 5])
            kch = [B("ckvT", k) for k in range(nchunks)]
            for j in range(NT):
                xtile, xbuf = load_x(xsrc[row0 + j * 128:row0 + (j + 1) * 128, :], None)
                make_hT(xtile, xbuf, 0, g, j * 128)
            if stage == 3.3:
                raise _Stop()
            for c2 in range(2):
                pj = next_pj()
                for c in range(8):
                    pe.matmul(ps[pj][:, 0:QB], w_in_qk[:, c, c2 * 128:(c2 + 1) * 128], hT[:, c, 0:QB],
                              start=(c == 0), stop=(c == 7), R=[B("w_in_qk")] + hbufs, W=[PS[pj]])
                act.copy(qaT[:, c2, 0:QB], ps[pj][:, 0:QB], R=[PS[pj]], W=[B("qaT")])
                act.activation(sq[:, c2, 0:QB], ps[pj][:, 0:QB], AF.Square, R=[PS[pj]], W=[B("sq")])
            pj = next_pj()
            for c2 in range(2):
                pe.matmul(ps[pj][:, 0:QB], ones_bf[:], sq[:, c2, 0:QB], start=(c2 == 0), stop=(c2 == 1),
                          R=[B("ones"), B("sq")], W=[PS[pj]])
            act.activation(rstd_bc[:, 0:QB], ps[pj][:, 0:QB], AF.Ln, bias=epsc[:], scale=1.0 / 256,
                           R=[PS[pj], B("epsc")], W=[B("rstd_bc")])
            act.activation(rstd_bc[:, 0:QB], rstd_bc[:, 0:QB], AF.Exp, scale=-0.5, R=[B("rstd_bc")], W=[B("rstd_bc")])
            dve.tensor_tensor(csr[:, 0:QB], cs_ap, rstd_bc[:, 0:QB], ALU.mult, R=[csbuf, B("rstd_bc")], W=[B("csr")])
            if stage == 3.4:
                raise _Stop()
            for h in range(16):
                wt, wb = wload(wg_scr[h], B("wg_scr", h))
                pj = next_pj()
                for c in range(8):
                    pe.matmul(ps[pj][:, 0:QB], wt[:, c * 128:(c + 1) * 128], hT[:, c, 0:QB],
                              start=(c == 0), stop=(c == 7), R=[wb] + hbufs, W=[PS[pj]])
                act.activation(sgT[:, h, 0:QB], ps[pj][:, 0:QB], AF.Silu, R=[PS[pj]], W=[B("sgT", h)])

            if stage == 3.5:
                raise _Stop()

            set_pj([4, 5])

            def head_q(h):
                i = h % 4
                pj = next_pj()
                for c2 in range(2):
                    pe.matmul(ps[pj][:, 0:QB], Wqlat[:, c2, h, :], qaT[:, c2, 0:QB], start=(c2 == 0), stop=(c2 == 1),
                              R=[B("Wqlat"), B("qaT")], W=[PS[pj]])
                dve.tensor_tensor(QlatT[i][:, 0:QB], ps[pj][:, 0:QB], rstd_bc[:, 0:QB], ALU.mult,
                                  R=[PS[pj], B("rstd_bc")], W=[B("QlatT", i)])
                pj = next_pj()
                for c2 in range(2):
                    pe.matmul(ps[pj][:, 0:QB], WqrAB[:, c2, h, :], qaT[:, c2, 0:QB], start=(c2 == 0), stop=(c2 == 1),
                              R=[B("WqrAB"), B("qaT")], W=[PS[pj]])
                dve.tensor_tensor(QrT[i][:, 0:QB], ps[pj][:, 0:QB], csr[:, 0:QB], ALU.mult,
                                  R=[PS[pj], B("csr")], W=[B("QrT", i)])

            SBK = [0, 1, 3]
            LA = 3
            sslot = {}
            scnt = [0]

            def o_bank(h):
                if h % 2 == 0:
                    return ps[2], PS[2]
                return ps_out[:, 0:512], PSO0

            def qk(st):
                h, kc = st
                i = h % 4
                s = SBK[scnt[0] % 3]
                scnt[0] += 1
                sslot[st] = s
                pe.matmul(ps[s][:, 0:QB], ckvT[:, kc * 128:(kc + 1) * 128], QlatT[i][:, 0:QB], start=True, stop=False,
                          R=[kch[kc], B("QlatT", i)], W=[PS[s]])
                pe.matmul(ps[s][:, 0:QB], krT2[:, kc * 128:(kc + 1) * 128], QrT[i][:, 0:QB], start=False, stop=True,
                          R=[B("krT2", kc), B("QrT", i)], W=[PS[s]])

            def pv(st):
                h, kc = st
                s = sslot[st]
                o_t, o_b = o_bank(h)
                p = rot("PT", 6)
                act.activation(PT[p][:, 0:QB], ps[s][:, 0:QB], AF.Exp, scale=MLA_SCALE, R=[PS[s]], W=[B("PT", p)])
                a = 2 * (h % 2) + (kc % 2)
                if kc < 2:
                    dve.tensor_copy(denacc[a][:, 0:QB], PT[p][:, 0:QB], R=[B("PT", p)], W=[B("denacc", a)])
                else:
                    dve.tensor_tensor(denacc[a][:, 0:QB], denacc[a][:, 0:QB], PT[p][:, 0:QB], ALU.add,
                                      R=[B("PT", p), B("denacc", a)], W=[B("denacc", a)])
                pe.matmul(o_t[:, 0:QB], ckv_tok[:, kc, :], PT[p][:, 0:QB], start=(kc == 0), stop=(kc == nchunks - 1),
                          R=[B("ckv_tok", kc), B("PT", p)], W=[o_b])

            def head_tail_a(h):
                i = h % 2
                o_t, o_b = o_bank(h)
                act.copy(OlatT[i][:, 0:QB], o_t[:, 0:QB], R=[o_b], W=[B("OlatT", i)])

            def head_tail_b(h):
                i = h % 2
                pj = next_pj()
                na = min(2, nchunks)
                for a_ in range(na):
                    a = 2 * i + a_
                    pe.matmul(ps[pj][:, 0:QB], ones_f32[:], denacc[a][:, 0:QB], start=(a_ == 0), stop=(a_ == na - 1),
                              R=[B("ones_f32"), B("denacc", a)], W=[PS[pj]])
                act.activation(recip[i][:, 0:QB], ps[pj][:, 0:QB], AF.Ln, R=[PS[pj]], W=[B("recip", i)])
                act.activation(recip[i][:, 0:QB], recip[i][:, 0:QB], AF.Exp, scale=-1.0, R=[B("recip", i)],
                               W=[B("recip", i)])
                dve.tensor_tensor(recip[i][:, 0:QB], recip[i][:, 0:QB], sgT[:, h, 0:QB], ALU.mult,
                                  R=[B("recip", i), B("sgT", h)], W=[B("recip", i)])
                pj = next_pj()
                pe.matmul(ps[pj][:, 0:QB], Wv[:, h, :], OlatT[i][:, 0:QB], start=True, stop=True,
                          R=[B("Wv"), B("OlatT", i)], W=[PS[pj]])
                dve.tensor_tensor(sgT[:, h, 0:QB], ps[pj][:, 0:QB], recip[i][:, 0:QB], ALU.mult,
                                  R=[PS[pj], B("recip", i)], W=[B("sgT", h)])

            steps = [(h, kc) for h in range(16) for kc in range(nchunks)]
            ns = len(steps)
            TD = min(3, nchunks - 1)
            QD = 4 if nchunks > 4 else 2
            hq_done = [0]

            def ensure_hq(upto_step):
                while hq_done[0] < 16 and hq_done[0] * nchunks <= upto_step:
                    head_q(hq_done[0])
                    hq_done[0] += 1

            ensure_hq(LA + QD)
            for k in range(min(LA, ns)):
                qk(steps[k])
            for k in range(ns):
                h, kc = steps[k]
                pv(steps[k])
                ensure_hq(k + 1 + LA + QD)
                if k + LA < ns:
                    qk(steps[k + LA])
                if kc == nchunks - 1:
                    head_tail_a(h)
                if kc == TD and h >= 1:
                    head_tail_b(h - 1)
            head_tail_b(15)
            if stage == 3.6:
                raise _Stop()
            for h in range(16):
                wt, wb = wload(wo_scr[h], B("wo_scr", h))
                for j in range(NT):
                    for n in range(2):
                        pe.matmul(out_banks[j][n], sgT[:, h, j * 128:(j + 1) * 128],
                                  wt[:, n * 512:(n + 1) * 512], start=(h == 0), stop=(h == 15),
                                  R=[B("sgT", h), wb], W=[out_banks[j][2 + n]])
            for j in range(NT):
                r = row0 + j * 128
                epilogue(out_banks[j], 0, g, xsrc[r:r + 128, :], None, dst[r:r + 128, :], B(dstname, r))


        for b in range(NPB):
            set_pj([0, 1, 2, 3, 4, 5])
            for t in range(2):
                kpass_front(xp[b * 256 + t * 128: b * 256 + (t + 1) * 128, :], 1, t)
            for t in range(2):
                kpass_back(1, t, t, None, b * 256 + t * 128)
            qblock(xp, 1, b * 256, 256, 2, cs_prompt[:, 0:256], B("cs_prompt"), x1p, "x1p")
        if stage == 4:
            raise _Stop()
        cache_chunk(0)
        cache_chunk(1)
        set_pj([0, 1, 2, 3, 4, 5])
        for t in range(2):
            kpass_front(xs[t * 128:(t + 1) * 128, :], 0, t % 4)
        for t in range(32):
            kpass_back(0, t % 4, 2 + t, t * 128, None)
            if t + 2 < 32:
                kpass_front(xs[(t + 2) * 128:(t + 3) * 128, :], 0, (t + 2) % 4)
        for qb in range(8):
            i = qb % 2
            G.dma(SP, cstab[i][:], cs_tab[:, qb * 512:(qb + 1) * 512], W=[B("cstab", i)])
            qblock(xs, 0, qb * 512, 512, 34, cstab[i][:, 0:512], B("cstab", i), x1s, "x1s")
            for f_ in na_casts[5 * qb:5 * qb + 5]:
                f_()
        G.barrier()


    W0.close()
    if debug == 1:
        return
    if stage == 5:
        raise _Stop()
    with ExitStack() as L1:
        Kpad = sb(L1, "Kpad", [128, 16, 1024], BF16)
        Vaug = sb(L1, "Vaug", [128, 8, 8, 192], BF16)
        Kc_pad = sb(L1, "Kc_pad", [128, 16, 256], BF16)
        Vc_aug = sb(L1, "Vc_aug", [128, 2, 8, 192], BF16)
        qT = sb(L1, "qT", [128, 8, 1024], BF16)
        tabs = [sb(L1, "tabs%d" % i, [128, 2816], BF16) for i in range(2)]
        sw = sb(L1, "sw", [128, 128], F32)
        Dst = sb(L1, "Dst", [128, 512], F32)
        Osb = sb(L1, "Osb", [128, 512], F32)
        vst = [sb(L1, "vst%d" % i, [128, 512], F32) for i in range(2)]
        sg1 = sgT[:].rearrange("p h q -> p (h q)").rearrange("p (m t) -> p m t", m=8)

        G.dma(SP, sw[:], sw_in[:], W=[B("sw")])
        dve.memset(Kpad[:], 0.0, W=[B("Kpad", s_) for s_ in range(8)])
        dve.memset(Kc_pad[:], 0.0, W=[B("Kc_pad")])
        dve.memset(Vaug[:, :, :, 64:128], 1.0, W=[B("Vaug", s_) for s_ in range(8)])
        dve.memset(Vc_aug[:, :, :, 64:128], 1.0, W=[B("Vc_aug")])

        def runs(tiles):
            out = []
            for jj, t in enumerate(tiles):
                sl = t % 8
                if out and out[-1][1] + out[-1][2] == sl:
                    out[-1][2] += 1
                else:
                    out.append([jj, sl, 1])
            return out

        vcnt = [0]

        def proj_group(xsrc, xname, g, tiles, st_row0):
            n = len(tiles)
            N = n * 128
            rr = runs(tiles)
            hbufs1 = [B("hT", j_) for j_ in range(n)]
            set_pj([0, 1, 2, 3, 4, 5])
            for jj, t in enumerate(tiles):
                r = t * 128
                xtile, xbuf = load_x(xsrc[r:r + 128, :], B(xname, r))
                make_hT(xtile, xbuf, 1, g, jj * 128)

            def fm(widx):
                wt, wb = wload(nwi_scr[widx], B("nwi_scr", widx))
                pj = next_pj()
                for c in range(8):
                    pe.matmul(ps[pj][:, 0:N], wt[:, c * 128:(c + 1) * 128], hT[:, c, 0:N], start=(c == 0),
                              stop=(c == 7), R=[wb] + hbufs1, W=[PS[pj]])
                return wt, wb, pj

            def tm(wt, wb):
                pj = next_pj()
                for jj in range(n):
                    for c in range(8):
                        pe.matmul(ps[pj][:, jj * 128:(jj + 1) * 128], hT[:, c, jj * 128:(jj + 1) * 128],
                                  wt[:, c * 128:(c + 1) * 128], start=(c == 0), stop=(c == 7),
                                  R=[wb, B("hT", jj)], W=[PS[pj]])
                return pj

            def state_store(pj, dst, m):
                i = vcnt[0] % 2
                vcnt[0] += 1
                act.copy(vst[i][:, 0:N], ps[pj][:, 0:N], R=[PS[pj]], W=[B("vst", i)])
                for jj in range(n):
                    r = st_row0 + jj * 128
                    G.dma(POOL, dst[r:r + 128, m * 128:(m + 1) * 128], vst[i][:, jj * 128:(jj + 1) * 128],
                          R=[B("vst", i)], W=[B("stkv", vcnt[0], jj)])

            for m in range(8):
                wt, wb, pj = fm(m)
                for jj0, sl0, cn in rr:
                    act.copy(qT[:, m, sl0 * 128:(sl0 + cn) * 128], ps[pj][:, jj0 * 128:(jj0 + cn) * 128],
                             R=[PS[pj]], W=[B("qT", sl0 + x_) for x_ in range(cn)])
            for m in range(8):
                wt, wb, pj = fm(8 + m)
                for jj0, sl0, cn in rr:
                    wl = [B("Kpad", sl0 + x_) for x_ in range(cn)]
                    dve.tensor_copy(Kpad[0:64, 2 * m, sl0 * 128:(sl0 + cn) * 128],
                                    ps[pj][0:64, jj0 * 128:(jj0 + cn) * 128], R=[PS[pj]], W=wl)
                    dve.tensor_copy(Kpad[64:128, 2 * m + 1, sl0 * 128:(sl0 + cn) * 128],
                                    ps[pj][64:128, jj0 * 128:(jj0 + cn) * 128], R=[PS[pj]], W=wl)
                if st_row0 is not None:
                    pj2 = tm(wt, wb)
                    state_store(pj2, st_k, m)
            for m in range(8):
                wt, wb = wload(nwi_scr[16 + m], B("nwi_scr", 16 + m))
                pj = tm(wt, wb)
                p3 = ps[pj][:, 0:N].rearrange("p (t d) -> p t d", d=128)
                for jj0, sl0, cn in rr:
                    wl = [B("Vaug", sl0 + x_) for x_ in range(cn)]
                    dve.tensor_copy(Vaug[:, sl0:sl0 + cn, m, 0:64], p3[:, jj0:jj0 + cn, 0:64], R=[PS[pj]], W=wl)
                    dve.tensor_copy(Vaug[:, sl0:sl0 + cn, m, 128:192], p3[:, jj0:jj0 + cn, 64:128], R=[PS[pj]], W=wl)
                if st_row0 is not None:
                    state_store(pj, st_v, m)
            for m in range(8):
                wt, wb, pj = fm(24 + m)
                for jj0, sl0, cn in rr:
                    act.activation(sg1[:, m, sl0 * 128:(sl0 + cn) * 128], ps[pj][:, jj0 * 128:(jj0 + cn) * 128],
                                   AF.Silu, R=[PS[pj]], W=[B("sg1", sl0 + x_) for x_ in range(cn)])

        def na_attn(blk, QB, qtiles, chunks, g, xsrc, xname, dst, dstname):
            NT = QB // 128
            qs0 = qtiles[0] % 8
            qc0 = qs0 * 128
            qbufs = [B("qT", qs0 + x_) for x_ in range(NT)]
            sgbufs = [B("sg1", qs0 + x_) for x_ in range(NT)]
            nch = len(chunks)
            SBK = [0, 1, 4, 5]
            LA = 4
            scnt = [0]
            sslot = {}

            def qrange(i):
                kind, idx = chunks[i]
                if kind != "r" or blk is None:
                    return 0, QB
                rows = []
                for ii in range(8):
                    r = 8 * blk + ii
                    rs = min(max(r - 4, 0), 56)
                    if any(rs <= kr <= rs + 7 for kr in (2 * idx, 2 * idx + 1)):
                        rows.append(ii)
                return 64 * rows[0], 64 * (rows[-1] + 1)

            def accs_of(m):
                if m % 2 == 0:
                    return [(ps[2], PS[2]), (ps[3], PS[3])]
                return [(ps_out[:, 0:512], PSO0), (ps_out[:, 512:1024], PSO1)]

            def load_tab(h):
                if blk is not None and h < 16:
                    G.dma(SP, tabs[h % 2][:], tab_scr[h], R=[B("tab_scr", h)], W=[B("tabs", h % 2)])

            def qk(st):
                h, i = st
                m = h // 2
                kind, idx = chunks[i]
                if kind == "r":
                    sl = idx % 8
                    lhs, lb = Kpad[:, h, sl * 128:(sl + 1) * 128], B("Kpad", sl)
                else:
                    lhs, lb = Kc_pad[:, h, idx * 128:(idx + 1) * 128], B("Kc_pad")
                sbk = SBK[scnt[0] % 4]
                scnt[0] += 1
                sslot[st] = sbk
                q0, q1 = qrange(i)
                pe.matmul(ps[sbk][:, q0:q1], lhs, qT[:, m, qc0 + q0:qc0 + q1], start=True, stop=True,
                          R=[lb] + qbufs, W=[PS[sbk]])

            def pv(st):
                h, i = st
                m, e = h // 2, h % 2
                acc_t, acc_b = accs_of(m)[e]
                kind, idx = chunks[i]
                sbk = sslot[st]
                q0, q1 = qrange(i)
                p = rot("PT", 6)
                act.activation(PT[p][:, q0:q1], ps[sbk][:, q0:q1], AF.Exp, scale=NA_SCALE, R=[PS[sbk]],
                               W=[B("PT", p)])
                if kind == "r" and blk is not None:
                    tb = tabs[h % 2]
                    u0 = 10 - 2 * (idx - 4 * blk)
                    ti_ = tb[:, u0 * 64:u0 * 64 + 512]
                    tf_ = tb[:, 1408 + u0 * 64:1408 + u0 * 64 + 512]
                    if blk == 0 and idx <= 3:
                        segs = [(0, 256, tf_), (256, 512, ti_)]
                    elif blk == 7 and idx >= 28:
                        segs = [(0, 320, ti_), (320, 512, tf_)]
                    else:
                        segs = [(0, 512, ti_)]
                    for a0, a1, tt in segs:
                        a0, a1 = max(a0, q0), min(a1, q1)
                        if a1 <= a0:
                            continue
                        dve.tensor_tensor(PT[p][:, a0:a1], PT[p][:, a0:a1], tt[:, a0:a1], ALU.mult,
                                          R=[B("PT", p), B("tabs", h % 2)], W=[B("PT", p)])
                if kind == "r":
                    sl = idx % 8
                    lhs, lb = Vaug[:, sl, m, 64 * e:64 * e + 128], B("Vaug", sl)
                else:
                    lhs, lb = Vc_aug[:, idx, m, 64 * e:64 * e + 128], B("Vc_aug")
                pe.matmul(acc_t[:, q0:q1], lhs, PT[p][:, q0:q1], start=(i == 0), stop=(i == nch - 1),
                          R=[lb, B("PT", p)], W=[acc_b])

            def tail(m):
                (A_t, A_b), (B_t, B_b) = accs_of(m)
                dve.tensor_copy(Dst[0:64, 0:QB], B_t[0:64, 0:QB], R=[B_b], W=[B("Dst")])
                dve.tensor_copy(Dst[64:128, 0:QB], A_t[64:128, 0:QB], R=[A_b], W=[B("Dst")])
                dve.tensor_copy(Osb[0:64, 0:QB], A_t[0:64, 0:QB], R=[A_b], W=[B("Osb")])
                dve.tensor_copy(Osb[64:128, 0:QB], B_t[64:128, 0:QB], R=[B_b], W=[B("Osb")])
                pe.matmul(A_t[:, 0:QB], sw[:], Dst[:, 0:QB], start=True, stop=True, R=[B("sw"), B("Dst")],
                          W=[A_b])
                ri = m % 2
                act.activation(recip[ri][:, 0:QB], A_t[:, 0:QB], AF.Ln, R=[A_b], W=[B("recip", ri)])
                act.activation(recip[ri][:, 0:QB], recip[ri][:, 0:QB], AF.Exp, scale=-1.0, R=[B("recip", ri)],
                               W=[B("recip", ri)])
                dve.tensor_tensor(recip[ri][:, 0:QB], recip[ri][:, 0:QB], sg1[:, m, qc0:qc0 + QB], ALU.mult,
                                  R=[B("recip", ri)] + sgbufs, W=[B("recip", ri)])
                dve.tensor_tensor(sg1[:, m, qc0:qc0 + QB], Osb[:, 0:QB], recip[ri][:, 0:QB], ALU.mult,
                                  R=[B("Osb"), B("recip", ri)], W=sgbufs)

            steps = [(h, i) for h in range(16) for i in range(nch)]
            ns = len(steps)
            TD = min(4, nch - 1)
            load_tab(0)
            load_tab(1)
            for k in range(min(LA, ns)):
                qk(steps[k])
            for k in range(ns):
                h, i = steps[k]
                pv(steps[k])
                if k + LA < ns:
                    qk(steps[k + LA])
                if i == nch - 1 and h + 2 < 16:
                    load_tab(h + 2)
                if h % 2 == 0 and h >= 2 and i == TD:
                    tail(h // 2 - 1)
            tail(7)
            for c in range(8):
                wt, wb = wload(nwo_scr[c], B("nwo_scr", c))
                for j in range(NT):
                    for n in range(2):
                        pe.matmul(out_banks[j][n], sg1[:, c, qc0 + j * 128:qc0 + (j + 1) * 128],
                                  wt[:, n * 512:(n + 1) * 512], start=(c == 0), stop=(c == 7),
                                  R=sgbufs + [wb], W=[out_banks[j][2 + n]])
            for j in range(NT):
                r = qtiles[j] * 128
                epilogue(out_banks[j], 1, g, xsrc[r:r + 128, :], B(xname, r), dst[r:r + 128, :], B(dstname, r))

        for pb in range(NPB):
            tiles = [2 * pb, 2 * pb + 1]
            proj_group(x1p, "x1p", 1, tiles, pb * 256)
            if stage == 6:
                raise _Stop()
            na_attn(None, 256, tiles, [("r", tiles[0]), ("r", tiles[1])], 1, x1p, "x1p", y_p, "y_p")
            if stage == 7:
                raise _Stop()
        if stage == 8:
            raise _Stop()
        for k in range(2):
            xtile, xbuf = load_x(c_nak[k * 128:(k + 1) * 128, :], None)
            i = rot("xn", 2)
            act.copy(xn[i][:], xtile[:], R=[xbuf], W=[B("xn", i)])
            pj = next_pj()
            psb = ps[pj][:].bitcast(BF16)
            for m in range(8):
                pe.transpose(psb[:, m * 128:(m + 1) * 128], xn[i][:, m * 128:(m + 1) * 128], ident_bf[:],
                             R=[B("xn", i), B("ident")], W=[PS[pj]])
            kc4 = Kc_pad[:].rearrange("p (m e) t -> p m e t", e=2)
            p3 = psb.rearrange("p (m t) -> p m t", m=8)
            dve.tensor_copy(kc4[0:64, :, 0, k * 128:(k + 1) * 128], p3[0:64, :, :], R=[PS[pj]], W=[B("Kc_pad")])
            dve.tensor_copy(kc4[64:128, :, 1, k * 128:(k + 1) * 128], p3[64:128, :, :], R=[PS[pj]], W=[B("Kc_pad")])
            xtile, xbuf = load_x(c_nav[k * 128:(k + 1) * 128, :], None)
            x3 = xtile[:].rearrange("p (m d) -> p m d", m=8)
            dve.tensor_copy(Vc_aug[:, k, :, 0:64], x3[:, :, 0:64], R=[xbuf], W=[B("Vc_aug")])
            dve.tensor_copy(Vc_aug[:, k, :, 128:192], x3[:, :, 64:128], R=[xbuf], W=[B("Vc_aug")])
        if stage == 9:
            raise _Stop()
        groups = [[0, 1]] + [[4 * c + 2, 4 * c + 3, 4 * c + 4, 4 * c + 5] for c in range(7)] + [[30, 31]]
        proj_group(x1s, "x1s", 0, groups[0], None)
        for b in range(8):
            proj_group(x1s, "x1s", 0, groups[b + 1], None)
            chunks = [("c", 0), ("c", 1)] + [("r", t) for t in range(max(0, 4 * b - 2), min(31, 4 * b + 5) + 1)]
            na_attn(b, 512, [4 * b, 4 * b + 1, 4 * b + 2, 4 * b + 3], chunks, 0, x1s, "x1s", y_s, "y_s")


def build_nc(debug=False, stage=99):
    gd = build_program(None, None, debug, stage)
    nc = bass.Bass("TRN2", target_bir_lowering=False)
    g = build_program(nc, gd.need, debug, stage)
    return nc, g


def rope_tables():
    t = np.arange(SEQ_S)
    pos = np.stack([t // GRID_W, t % GRID_W], axis=-1).astype(np.float32)
    inv = (10000.0 ** (-np.arange(16, dtype=np.float32) / 16)).astype(np.float32)
    ang = pos[:, :, None] * inv
    cos, sin = np.cos(ang).astype(np.float32), np.sin(ang).astype(np.float32)
    cos_full = np.concatenate([cos[:, 0], cos[:, 0], cos[:, 1], cos[:, 1]], axis=1)
    sin_full = np.concatenate([sin[:, 0], sin[:, 0], sin[:, 1], sin[:, 1]], axis=1)
    sin_signed = np.concatenate([-sin[:, 0], sin[:, 0], -sin[:, 1], sin[:, 1]], axis=1)
    cs_tab = np.ascontiguousarray(np.concatenate([cos_full, sin_full], axis=1).T)
    kt_tab = np.ascontiguousarray(np.concatenate([cos_full, sin_signed], axis=1))
    return cs_tab, kt_tab


def make_in_maps(inp):
    cs_tab, kt_tab = rope_tables()
    ident = np.eye(128, dtype=np.float32)
    j2 = np.zeros((128, 128), np.float32)
    for m in range(128):
        j2[(m // 64) * 64 + 63 - (m % 64), m] = 1.0
    selm = np.zeros((2, 256), np.float32)
    selm[0, 0:128] = 1.0
    selm[1, 128:256] = 1.0
    i2 = np.eye(2, dtype=np.float32)
    rb = inp["na_rel_bias"][0]
    rbp = np.zeros((16, 23, 127), np.float32)
    rbp[:, 4:19, 48:79] = rb[:, ::-1, ::-1]
    mask = np.zeros((128, 2, 22, 64), np.float32)
    qc = np.arange(64)
    cstart = np.clip(qc - 8, 0, 48)
    for half in range(2):
        for kcp in range(64):
            kc = 63 - kcp
            colok = (kc >= cstart) & (kc < cstart + 16)
            for ui in range(22):
                dr = half - (ui - 10)
                if -4 <= dr <= 3:
                    mask[half * 64 + kcp, 0, ui, :] = colok
                if -7 <= dr <= 7:
                    mask[half * 64 + kcp, 1, ui, :] = colok
    mask = mask.reshape(128, 2, 22 * 64)
    swm = np.zeros((128, 128), np.float32)
    for m in range(128):
        swm[(m + 64) % 128, m] = 1.0
    shared = {
        "w_ada": inp["w_ada"], "b_ada": inp["b_ada"], "pre_g": inp["pre_norm_g"], "post_g": inp["post_norm_g"],
        "mla_w_in": inp["mla_w_in"][0],
        "gq_cols": np.ascontiguousarray(inp["mla_q_norm_g"][0].reshape(2, 128).T),
        "mla_w_qb": inp["mla_w_qb"][0], "gkv": inp["mla_kv_norm_g"][0], "mla_w_kvb": inp["mla_w_kvb"][0],
        "mla_w_out": inp["mla_w_out"][0], "na_w_in": inp["na_w_in"][0], "rbp": rbp, "na_w_out": inp["na_w_out"][0],
        "ident_in": ident, "j2_in": j2, "selm_in": selm, "i2_in": i2, "cs_tab": cs_tab, "kt_tab": kt_tab,
        "mask_in": mask, "sw_in": swm,
    }
    shared = {k: np.ascontiguousarray(v, dtype=np.float32) for k, v in shared.items()}
    maps = []
    for i in range(8):
        cc = np.stack([inp["c"][i].reshape(8, 128).T, inp["c_ctx"].reshape(8, 128).T], axis=-1)
        m = dict(shared)
        m["xs"] = np.ascontiguousarray(inp["x_sample"][i])
        m["xp"] = np.ascontiguousarray(inp["x_prompt"][4 * i:4 * i + 4].reshape(1024, 1024))
        m["c_ckv"] = np.ascontiguousarray(inp["cache_mla_ckv"][i, 0])
        m["c_kr"] = np.ascontiguousarray(inp["cache_mla_krope"][i, 0])
        m["c_nak"] = np.ascontiguousarray(inp["cache_na_k"][i, 0].reshape(256, 1024))
        m["c_nav"] = np.ascontiguousarray(inp["cache_na_v"][i, 0].reshape(256, 1024))
        m["ccols"] = np.ascontiguousarray(cc.reshape(128, 16), dtype=np.float32)
        maps.append(m)
    return maps


_NC_CACHE = {}


def kernel(**inputs):
    inp = {k: np.asarray(v) for k, v in inputs.items()}
    if "nc" not in _NC_CACHE:
        _NC_CACHE["nc"] = build_nc()[0]
    nc = _NC_CACHE["nc"]
    maps = make_in_maps(inp)
    res = run_bass_kernel_spmd(nc, maps, core_ids=list(range(8)))
    r = res.results
    y_p = np.concatenate([r[i]["y_p"].reshape(4, 256, 1024) for i in range(8)], axis=0)
    y_s = np.stack([r[i]["y_s"] for i in range(8)], axis=0)
    s_ckv = np.concatenate([r[i]["st_ckv"].reshape(4, 1, 256, 128) for i in range(8)], axis=0)
    s_kr = np.concatenate([r[i]["st_kr"].reshape(4, 1, 256, 64) for i in range(8)], axis=0)
    s_k = np.concatenate([r[i]["st_k"].reshape(4, 1, 256, 16, 64) for i in range(8)], axis=0)
    s_v = np.concatenate([r[i]["st_v"].reshape(4, 1, 256, 16, 64) for i in range(8)], axis=0)
    return (y_p.astype(np.float32), y_s.astype(np.float32), s_ckv.astype(np.float32), s_kr.astype(np.float32),
            s_k.astype(np.float32), s_v.astype(np.float32))
```

```python
import numpy as np
from contextlib import ExitStack
import concourse.bass as bass
import concourse.mybir as mybir
from concourse.bass_utils import run_bass_kernel_spmd

F32 = mybir.dt.float32
BF16 = mybir.dt.bfloat16
AF = mybir.ActivationFunctionType
ALU = mybir.AluOpType

PE, ACT, DVE, POOL, SP = "pe", "act", "dve", "pool", "sp"
ENGS = (PE, ACT, DVE, POOL, SP)
NDS = 24

D = 1024
SEQ_S = 4096
SEQ_P = 256
NPB = 4
EPS = 1e-6
MLA_SCALE = 192.0 ** -0.5
NA_SCALE = 0.125
GRID_W = 64


class _Stop(Exception):
    pass


class Dummy:
    def __getitem__(self, k):
        return self

    def __getattr__(self, k):
        return lambda *a, **kw: self


class Buf:
    __slots__ = ("name", "w", "r")

    def __init__(self, name):
        self.name = name
        self.w = None
        self.r = {}


class EngProxy:
    def __init__(self, gen, name, obj):
        self._g = gen
        self._n = name
        self._o = obj

    def __getattr__(self, op):
        g, n, o = self._g, self._n, self._o

        def f(*args, R=(), W=(), **kw):
            return g.emit(n, (lambda: getattr(o, op)(*args, **kw)), R, W)
        return f


class Gen:
    def __init__(self, nc, marked, es):
        self.nc = nc
        self.dry = nc is None
        self.marked = marked if marked is not None else set()
        self.need = set()
        self.idx = {e: 0 for e in ENGS}
        self.sigcount = {e: 0 for e in ENGS}
        self.sigval = {}
        self.waited = {e: {} for e in ENGS}
        self.dma_val = {}
        self.dma_rr = {SP: 0, POOL: 0, ACT: 0}
        self.bufs = {}
        self.ninst = 0
        if self.dry:
            self.engobj = {e: Dummy() for e in ENGS}
            self.sem = {}
            self.dsem = {}
        else:
            self.engobj = {PE: nc.tensor, ACT: nc.scalar, DVE: nc.vector, POOL: nc.gpsimd, SP: nc.sync}
            self.sem = {e: es.enter_context(nc.semaphore("s_" + e)) for e in (PE, ACT, DVE, POOL)}
            self.dsem = {}
            for q in (SP, POOL):
                for k in range(NDS):
                    self.dsem[(q, k)] = es.enter_context(nc.semaphore("d_%s_%d" % (q, k)))
        self.pe = EngProxy(self, PE, self.engobj[PE])
        self.act = EngProxy(self, ACT, self.engobj[ACT])
        self.dve = EngProxy(self, DVE, self.engobj[DVE])
        self.pool = EngProxy(self, POOL, self.engobj[POOL])

    def B(self, *key):
        b = self.bufs.get(key)
        if b is None:
            b = Buf(key)
            self.bufs[key] = b
        return b

    def wait(self, eng, ev):
        if ev[0] == "c":
            _, pe_, pi = ev
            if self.dry:
                self.need.add((pe_, pi))
                return
            val = self.sigval[(pe_, pi)]
            key = ("c", pe_)
            if self.waited[eng].get(key, 0) >= val:
                return
            self.engobj[eng].wait_ge(self.sem[pe_], val)
            self.waited[eng][key] = val
        else:
            _, qk, val = ev
            key = ("d", qk)
            if self.waited[eng].get(key, 0) >= val:
                return
            if not self.dry:
                self.engobj[eng].wait_ge(self.dsem[qk], val)
            self.waited[eng][key] = val

    def _deps(self, eng, R, W):
        for b in R:
            if b.w is not None:
                self._dep(eng, b.w, True)
            if b.name[0] in ("ps", "ps_out"):
                for k, ev in list(b.r.items()):
                    if k != eng:
                        self._dep(eng, ev, False)
        for b in W:
            if b.w is not None:
                self._dep(eng, b.w, False)
            for ev in list(b.r.values()):
                self._dep(eng, ev, False)

    def _dep(self, eng, ev, raw):
        if ev[0] == "c" and ev[1] == eng:
            if eng == PE or not raw:
                return
        self.wait(eng, ev)

    def emit(self, eng, fn, R=(), W=()):
        self._deps(eng, R, W)
        i = self.idx[eng]
        self.idx[eng] += 1
        self.ninst += 1
        me = ("c", eng, i)
        if not self.dry:
            ins = fn()
            if (eng, i) in self.marked:
                ins.then_inc(self.sem[eng], 1)
                self.sigcount[eng] += 1
                self.sigval[(eng, i)] = self.sigcount[eng]
        for b in R:
            b.r[eng] = me
        for b in W:
            b.w = me
            b.r = {}
        return me

    def dma(self, q, out, in_, R=(), W=(), **kw):
        self._deps(q, R, W)
        k = self.dma_rr[q]
        self.dma_rr[q] = (k + 1) % (NDS if q == SP else 4)
        qk = (q, k)
        prev = self.dma_val.get(qk, 0)
        if prev > 0:
            self.wait(q, ("d", qk, prev))
        val = prev + 16
        self.dma_val[qk] = val
        self.ninst += 1
        if not self.dry:
            self.engobj[q].dma_start(out=out, in_=in_, **kw).then_inc(self.dsem[qk], 16)
        me = ("d", qk, val)
        for b in R:
            b.r[("d", qk)] = me
        for b in W:
            b.w = me
            b.r = {}
        return me

    def barrier(self):
        for e in ENGS:
            for o in (PE, ACT, DVE, POOL):
                if o != e and self.idx[o] > 0:
                    self.wait(e, ("c", o, self.idx[o] - 1))
            for qk, v in self.dma_val.items():
                self.wait(e, ("d", qk, v))

    def finish(self):
        for qk, v in self.dma_val.items():
            self.wait(SP, ("d", qk, v))


def build_program(nc, marked, debug=False, stage=99):
    es = ExitStack()
    G = Gen(nc, marked, es)
    try:
        _build_body(nc, G, es, debug, stage)
    except _Stop:
        G.finish()
        return G
    G.finish()
    es.close()
    return G


def _build_body(nc, G, es, debug, stage):
    dry = nc is None
    B = G.B
    pe, act, dve = G.pe, G.act, G.dve

    def dram(name, shape, dt=F32, kind="Internal"):
        if dry:
            return Dummy()
        if kind == "Internal":
            return nc.dram_tensor(name, list(shape), dt).ap()
        return nc.dram_tensor(name, list(shape), dt, kind=kind).ap()

    def sb(stack, name, shape, dt):
        if dry:
            return Dummy()
        return stack.enter_context(nc.sbuf_tensor(name, list(shape), dt))

    def psum(stack, name, shape, dt=F32):
        if dry:
            return Dummy()
        return stack.enter_context(nc.psum_tensor(name, list(shape), dt))

    IN = lambda n, s: dram(n, s, F32, "ExternalInput")
    OUT = lambda n, s: dram(n, s, F32, "ExternalOutput")

    xs = IN("xs", [SEQ_S, D])
    xp = IN("xp", [NPB * SEQ_P, D])
    c_ckv = IN("c_ckv", [256, 128])
    c_kr = IN("c_kr", [256, 64])
    c_nak = IN("c_nak", [256, 1024])
    c_nav = IN("c_nav", [256, 1024])
    ccols = IN("ccols", [128, 16])
    w_ada = IN("w_ada", [2, 1024, 3072])
    b_ada = IN("b_ada", [2, 3072])
    pre_g = IN("pre_g", [2, 1024])
    post_g = IN("post_g", [2, 1024])
    mla_w_in = IN("mla_w_in", [1024, 2496])
    gq_cols = IN("gq_cols", [128, 2])
    mla_w_qb = IN("mla_w_qb", [256, 3072])
    gkv = IN("gkv", [128])
    mla_w_kvb = IN("mla_w_kvb", [128, 4096])
    mla_w_out = IN("mla_w_out", [2048, 1024])
    na_w_in = IN("na_w_in", [1024, 4096])
    rbp = IN("rbp", [16, 23, 127])
    na_w_out = IN("na_w_out", [1024, 1024])
    ident_in = IN("ident_in", [128, 128])
    j2_in = IN("j2_in", [128, 128])
    selm_in = IN("selm_in", [2, 256])
    i2_in = IN("i2_in", [2, 2])
    cs_tab = IN("cs_tab", [128, SEQ_S])
    kt_tab = IN("kt_tab", [SEQ_S, 128])
    mask_in = IN("mask_in", [128, 2, 22 * 64])
    sw_in = IN("sw_in", [128, 128])

    y_s = OUT("y_s", [SEQ_S, D])
    y_p = OUT("y_p", [NPB * SEQ_P, D])
    st_ckv = OUT("st_ckv", [NPB * SEQ_P, 128])
    st_kr = OUT("st_kr", [NPB * SEQ_P, 64])
    st_k = OUT("st_k", [NPB * SEQ_P, 1024])
    st_v = OUT("st_v", [NPB * SEQ_P, 1024])

    x1s = dram("x1s", [SEQ_S, D]) if not debug else OUT("x1s", [SEQ_S, D])
    x1p = dram("x1p", [NPB * SEQ_P, D]) if not debug else OUT("x1p", [NPB * SEQ_P, D])
    wg_scr = dram("wg_scr", [16, 128, 8 * 128], BF16)
    wo_scr = dram("wo_scr", [16, 128, 1024], BF16)
    nwi_scr = dram("nwi_scr", [32, 128, 8 * 128], BF16)
    nwo_scr = dram("nwo_scr", [8, 128, 1024], BF16)
    tab_scr = dram("tab_scr", [16, 128, 2 * 22 * 64], BF16)

    P0 = ExitStack()
    es.enter_context(P0)
    ident_bf = sb(P0, "ident_bf", [128, 128], BF16)
    ones_bf = sb(P0, "ones_bf", [128, 128], BF16)
    j2_bf = sb(P0, "j2_bf", [128, 128], BF16)
    selm = sb(P0, "selm", [2, 256], F32)
    i2 = sb(P0, "i2", [2, 2], F32)
    scs = sb(P0, "scs", [128, 16], F32)
    modcol = sb(P0, "modcol", [128, 2, 32], F32)
    Gbc = sb(P0, "Gbc", [128, 2, 2, 1024], F32)
    epsc = sb(P0, "epsc", [128, 1], F32)
    xt = [sb(P0, "xt%d" % i, [128, 1024], F32) for i in range(3)]
    xn = [sb(P0, "xn%d" % i, [128, 1024], BF16) for i in range(2)]
    junk = sb(P0, "junk", [128, 1024], BF16)
    ytmp = sb(P0, "ytmp", [128, 1024], F32)
    stats = [sb(P0, "stat%d" % i, [128, 4], F32) for i in range(8)]
    hT = sb(P0, "hT", [128, 8, 512], BF16)
    wpool = [sb(P0, "wp%d" % i, [128, 1024], BF16) for i in range(6)]
    PT = [sb(P0, "PT%d" % i, [128, 512], BF16) for i in range(6)]
    ones_f32 = sb(P0, "ones_f32", [128, 128], F32)
    recip = [sb(P0, "recip%d" % i, [128, 512], F32) for i in range(2)]
    sgT = sb(P0, "sgT", [128, 16, 512], BF16)

    ps = [psum(P0, "ps%d" % i, [128, 512]) for i in range(6)]
    ps_out = psum(P0, "ps_out", [128, 1024])
    PS = [B("ps", i) for i in range(6)]
    PSO = B("ps_out")
    PSO0 = B("ps_out", 0)
    PSO1 = B("ps_out", 1)

    cnt = {"xt": 0, "xn": 0, "st": 0, "wp": 0, "pj": 0, "PT": 0}

    def rot(name, n):
        v = cnt[name]
        cnt[name] = (v + 1) % n
        return v

    pjbanks = [[4, 5]]
    pjc = [0]

    def next_pj():
        v = pjbanks[0][pjc[0] % len(pjbanks[0])]
        pjc[0] += 1
        return v

    def set_pj(lst):
        pjbanks[0] = lst

    G.dma(POOL, ident_bf[:], ident_in[:], W=[B("ident")])
    G.dma(POOL, j2_bf[:], j2_in[:], W=[B("j2")])
    G.dma(SP, selm[:], selm_in[:], W=[B("selm")])
    G.dma(SP, i2[:], i2_in[:], W=[B("i2")])
    dve.memset(ones_bf[:], 1.0, W=[B("ones")])
    dve.memset(ones_f32[:], 1.0, W=[B("ones_f32")])
    dve.memset(epsc[:], EPS, W=[B("epsc")])

    W0 = ExitStack()
    es.enter_context(W0)
    w_in_qk = sb(W0, "w_in_qk", [128, 8, 448], BF16)
    wqb_bf = sb(W0, "wqb_bf", [128, 2, 3072], BF16)
    wkvb_bf = sb(W0, "wkvb_bf", [128, 4096], BF16)
    set_pj([0, 1, 2, 3, 4, 5])
    TP = ExitStack()
    if True:
        mask_bf = sb(TP, "mask_bf", [128, 2, 1408], BF16)
        Hraw = [sb(TP, "Hraw%d" % i, [128, 1408], F32) for i in range(1)]
        Tp = [sb(TP, "Tp%d" % i, [128, 2816], BF16) for i in range(1)]
        Tfin = [sb(TP, "Tfin%d" % i, [128, 2816], BF16) for i in range(1)]
        G.dma(POOL, mask_bf[:], mask_in[:], W=[B("mask_bf")])
        for c in range(8):
            G.dma(POOL, w_in_qk[:, c, :], mla_w_in[c * 128:(c + 1) * 128, 0:448], W=[B("w_in_qk")])
        for c in range(2):
            for hh in range(2):
                G.dma(POOL, wqb_bf[:, c, hh * 1536:(hh + 1) * 1536],
                      mla_w_qb[c * 128:(c + 1) * 128, hh * 1536:(hh + 1) * 1536], W=[B("wqb_bf")])
        for hh in range(2):
            G.dma(POOL, wkvb_bf[:, hh * 2048:(hh + 1) * 2048], mla_w_kvb[:, hh * 2048:(hh + 1) * 2048],
                  W=[B("wkvb_bf")])
        for h in range(16):
            G.dma(POOL, wg_scr[h].rearrange("p (c j) -> p c j", c=8),
                  mla_w_in[:, 448 + h * 128: 448 + (h + 1) * 128].rearrange("(c p) j -> p c j", p=128),
                  W=[B("wg_scr", h)])
        for h in range(16):
            G.dma(POOL, wo_scr[h], mla_w_out[h * 128:(h + 1) * 128, :], W=[B("wo_scr", h)])
        na_casts = []
        for m in range(32):
            na_casts.append(lambda m=m: G.dma(
                POOL, nwi_scr[m].rearrange("p (c j) -> p c j", c=8),
                na_w_in[:, m * 128:(m + 1) * 128].rearrange("(c p) j -> p c j", p=128), W=[B("nwi_scr", m)]))
        for c in range(8):
            na_casts.append(lambda c=c: G.dma(POOL, nwo_scr[c], na_w_out[c * 128:(c + 1) * 128, :],
                                              W=[B("nwo_scr", c)]))


        def table_head(h):
            hb = 0
            tbanks = [(ps_out[:, 0:512], PSO0), (ps_out[:, 512:1024], PSO1)]
            for half in range(2):
                if dry:
                    src = Dummy()
                else:
                    src = bass.AP(rbp.tensor, h * 23 * 127 + (1 - half) * 127, [[1, 64], [127, 22], [1, 64]])
                G.dma(SP, Hraw[hb][half * 64:(half + 1) * 64, :].rearrange("p (u q) -> p u q", u=22), src,
                      W=[B("Hraw", hb)])
            act.activation(Hraw[hb][:], Hraw[hb][:], AF.Exp, R=[B("Hraw", hb)], W=[B("Hraw", hb)])
            for k in range(2):
                dve.tensor_tensor(Tp[hb][:, k * 1408:(k + 1) * 1408], Hraw[hb][:], mask_bf[:, k, :], ALU.mult,
                                  R=[B("Hraw", hb), B("mask_bf")], W=[B("Tp", hb)])
            for n in range(6):
                w = min(512, 2816 - n * 512)
                tb_t, tb_b = tbanks[n % 2]
                pe.matmul(tb_t[:, 0:w], j2_bf[:], Tp[hb][:, n * 512:n * 512 + w], start=True, stop=True,
                          R=[B("j2"), B("Tp", hb)], W=[tb_b])
                if n % 2 == 0:
                    dve.tensor_copy(Tfin[hb][:, n * 512:n * 512 + w], tb_t[:, 0:w], R=[tb_b], W=[B("Tfin", hb)])
                else:
                    act.copy(Tfin[hb][:, n * 512:n * 512 + w], tb_t[:, 0:w], R=[tb_b], W=[B("Tfin", hb)])
            G.dma(SP, tab_scr[h], Tfin[hb][:], R=[B("Tfin", hb)], W=[B("tab_scr", h)])

    set_pj([4, 5])
    if stage == 1:
        raise _Stop()
    with ExitStack() as T0:
        wa = [sb(T0, "wa%d" % i, [128, 3072], F32) for i in range(2)]
        mod_sb = sb(T0, "mod_sb", [2, 3072], F32)
        bada = sb(T0, "bada", [2, 3072], F32)
        gpre2 = sb(T0, "gpre2", [2, 1024], F32)
        gpost2 = sb(T0, "gpost2", [2, 1024], F32)
        Arow = sb(T0, "Arow", [2, 1024], F32)
        Grow = sb(T0, "Grow", [2, 1024], F32)
        scin = sb(T0, "scin", [128, 16], F32)
        G.dma(SP, scin[:], ccols[:], W=[B("scin")])
        act.activation(scs[:], scin[:], AF.Silu, R=[B("scin")], W=[B("scs")])
        for l in range(2):
            G.dma(SP, bada[:], b_ada[l].partition_broadcast(2), W=[B("bada")])
            G.dma(SP, gpre2[:], pre_g[l].partition_broadcast(2), W=[B("gpre2")])
            G.dma(SP, gpost2[:], post_g[l].partition_broadcast(2), W=[B("gpost2")])
            G.dma(SP, wa[0][:], w_ada[l, 0:128, :], W=[B("wa", 0)])
            for k in range(8):
                wb = k % 2
                if k + 1 < 8:
                    G.dma(SP, wa[1 - wb][:], w_ada[l, (k + 1) * 128:(k + 2) * 128, :], W=[B("wa", 1 - wb)])
                for n in range(6):
                    pe.matmul(ps[n][0:2, :], scs[:, 2 * k:2 * k + 2], wa[wb][:, n * 512:(n + 1) * 512],
                              start=(k == 0), stop=(k == 7), R=[B("scs"), B("wa", wb)], W=[PS[n]])
                table_head(l * 8 + k)
            for n in range(6):
                dve.tensor_tensor(mod_sb[:, n * 512:(n + 1) * 512], ps[n][0:2, :], bada[:, n * 512:(n + 1) * 512],
                                  ALU.add, R=[PS[n], B("bada")], W=[B("mod_sb")])
            dve.scalar_tensor_tensor(Arow[:], mod_sb[:, 1024:2048], 1.0, gpre2[:], ALU.add, ALU.mult,
                                     R=[B("mod_sb"), B("gpre2")], W=[B("Arow")])
            dve.tensor_tensor(Grow[:], mod_sb[:, 2048:3072], gpost2[:], ALU.mult,
                              R=[B("mod_sb"), B("gpost2")], W=[B("Grow")])
            for c in range(8):
                pe.matmul(ps[0][:, 2 * c:2 * c + 2], Arow[:, c * 128:(c + 1) * 128], i2[:], start=True, stop=True,
                          R=[B("Arow"), B("i2")], W=[PS[0]])
                pe.matmul(ps[0][:, 16 + 2 * c:16 + 2 * c + 2], mod_sb[:, c * 128:(c + 1) * 128], i2[:],
                          start=True, stop=True, R=[B("mod_sb"), B("i2")], W=[PS[0]])
            dve.tensor_copy(modcol[:, l, :], ps[0][:, 0:32], R=[PS[0]], W=[B("modcol", l)])
            for g in range(2):
                for n in range(2):
                    pe.matmul(ps[1 + n][:, :], selm[:, g * 128:(g + 1) * 128], Grow[:, n * 512:(n + 1) * 512],
                              start=True, stop=True, R=[B("selm"), B("Grow")], W=[PS[1 + n]])
                    dve.tensor_copy(Gbc[:, l, g, n * 512:(n + 1) * 512], ps[1 + n][:, :], R=[PS[1 + n]],
                                    W=[B("Gbc", l, g)])
        G.barrier()
    TP.close()

    if stage == 2:
        raise _Stop()
    def load_x(src_ap, srcbuf):
        i = rot("xt", 3)
        G.dma(SP, xt[i][:], src_ap, R=[srcbuf] if srcbuf is not None else [], W=[B("xt", i)])
        return xt[i], B("xt", i)

    def rstd_of(src_ap, srcbuf, n, width_scale):
        i = rot("st", 8)
        st = stats[i]
        sbuf = B("stat", i)
        act.activation(junk[:, 0:n], src_ap, AF.Square, accum_out=st[:, 0:1], R=[srcbuf], W=[sbuf, B("junk")])
        act.activation(st[:, 1:2], st[:, 0:1], AF.Ln, bias=epsc[:], scale=width_scale, R=[sbuf, B("epsc")], W=[sbuf])
        act.activation(st[:, 2:3], st[:, 1:2], AF.Exp, scale=-0.5, R=[sbuf], W=[sbuf])
        return st[:, 2:3], sbuf

    def make_hT(xtile, xbuf, l, g, col0):
        r_ap, rbuf = rstd_of(xtile[:], xbuf, 1024, 1.0 / D)
        if stage == 3.01:
            raise _Stop()
        i = rot("xn", 2)
        act.activation(xn[i][:], xtile[:], AF.Identity, scale=r_ap, R=[xbuf, rbuf], W=[B("xn", i)])
        if stage == 3.02:
            raise _Stop()
        pj = next_pj()
        psb = ps[pj][:].bitcast(BF16)
        for c in range(8):
            pe.transpose(psb[:, c * 128:(c + 1) * 128], xn[i][:, c * 128:(c + 1) * 128], ident_bf[:],
                         R=[B("xn", i), B("ident")], W=[PS[pj]])
        if stage == 3.03:
            raise _Stop()
        for c in range(8):
            eng = dve if c % 2 == 0 else dve
            eng.tensor_scalar(hT[:, c, col0:col0 + 128], psb[:, c * 128:(c + 1) * 128],
                              modcol[:, l, 2 * c + g:2 * c + g + 1], modcol[:, l, 16 + 2 * c + g:16 + 2 * c + g + 1],
                              ALU.mult, ALU.add, R=[PS[pj], B("modcol", l)], W=[B("hT", col0 // 128)])

    def wload(src_ap, srcbuf):
        i = rot("wp", 6)
        G.dma(SP, wpool[i][:], src_ap, R=[srcbuf], W=[B("wp", i)])
        return wpool[i], B("wp", i)

    def epilogue(bk, l, g, xsrc_ap, xsrcbuf, dst_ap, dstbuf):
        a0, a1, b0, b1 = bk
        i = rot("st", 8)
        st = stats[i]
        sbuf = B("stat", i)
        act.activation(junk[:, 0:512], a0, AF.Square, accum_out=st[:, 0:1], R=[b0], W=[sbuf, B("junk")])
        act.activation(junk[:, 512:1024], a1, AF.Square, accum_out=st[:, 3:4], R=[b1], W=[sbuf, B("junk")])
        dve.tensor_tensor(st[:, 0:1], st[:, 0:1], st[:, 3:4], ALU.add, R=[sbuf], W=[sbuf])
        act.activation(st[:, 1:2], st[:, 0:1], AF.Ln, bias=epsc[:], scale=1.0 / D, R=[sbuf, B("epsc")], W=[sbuf])
        act.activation(st[:, 2:3], st[:, 1:2], AF.Exp, scale=-0.5, R=[sbuf], W=[sbuf])
        r_ap = st[:, 2:3]
        xr, xrb = load_x(xsrc_ap, xsrcbuf)
        dve.scalar_tensor_tensor(ytmp[:, 0:512], a0, r_ap, Gbc[:, l, g, 0:512], ALU.mult, ALU.mult,
                                 R=[b0, sbuf, B("Gbc", l, g)], W=[B("ytmp")])
        dve.scalar_tensor_tensor(ytmp[:, 512:1024], a1, r_ap, Gbc[:, l, g, 512:1024], ALU.mult, ALU.mult,
                                 R=[b1, sbuf, B("Gbc", l, g)], W=[B("ytmp")])
        dve.tensor_tensor(xr[:], ytmp[:], xr[:], ALU.add, R=[B("ytmp"), xrb], W=[xrb])
        G.dma(POOL, dst_ap, xr[:], R=[xrb], W=[dstbuf])

    out_banks = [(ps_out[:, 0:512], ps_out[:, 512:1024], PSO0, PSO1),
                 (ps[0][:, :], ps[1][:, :], PS[0], PS[1]),
                 (ps[2][:, :], ps[3][:, :], PS[2], PS[3]),
                 (ps[4][:, :], ps[5][:, :], PS[4], PS[5])]

    with ExitStack() as L0:
        Wqlat = sb(L0, "Wqlat", [128, 2, 16, 128], BF16)
        WqrAB = sb(L0, "WqrAB", [128, 2, 16, 128], BF16)
        Wv = sb(L0, "Wv", [128, 16, 128], BF16)
        gqc = sb(L0, "gqc", [128, 2], F32)
        gkv_bc = sb(L0, "gkv_bc", [128, 128], F32)
        cs_prompt = sb(L0, "cs_prompt", [128, 256], F32)
        ckvT = sb(L0, "ckvT", [128, 34 * 128], BF16)
        krT2 = sb(L0, "krT2", [128, 34 * 128], BF16)
        ckv_tok = sb(L0, "ckv_tok", [128, 34, 128], BF16)
        qaT = sb(L0, "qaT", [128, 2, 512], BF16)
        sq = sb(L0, "sq", [128, 2, 512], BF16)
        rstd_bc = sb(L0, "rstd_bc", [128, 512], F32)
        csr = sb(L0, "csr", [128, 512], F32)
        cstab = [sb(L0, "cstab%d" % i, [128, 512], F32) for i in range(2)]
        QlatT = [sb(L0, "QlatT%d" % i, [128, 512], BF16) for i in range(4)]
        QrT = [sb(L0, "QrT%d" % i, [128, 512], BF16) for i in range(4)]
        OlatT = [sb(L0, "OlatT%d" % i, [128, 512], BF16) for i in range(2)]
        denacc = [sb(L0, "denacc%d" % i, [128, 512], F32) for i in range(4)]
        kv32 = [sb(L0, "kv32_%d" % i, [128, 128], F32) for i in range(2)]
        kr32 = [sb(L0, "kr32_%d" % i, [128, 64], F32) for i in range(2)]
        krf = sb(L0, "krf", [128, 64], F32)
        krtmp = sb(L0, "krtmp", [128, 64], F32)
        krf2 = [sb(L0, "krf2_%d" % i, [128, 128], BF16) for i in range(2)]
        ctab = [sb(L0, "ctab%d" % i, [128, 128], F32) for i in range(2)]
        cache32 = sb(L0, "cache32", [128, 192], F32)

        with ExitStack() as T1:
            WqnT = sb(T1, "WqnT", [128, 16, 256], BF16)
            WkT = sb(T1, "WkT", [128, 16, 128], BF16)
            G.dma(SP, gqc[:], gq_cols[:], W=[B("gqc")])
            G.dma(SP, gkv_bc[:], gkv.partition_broadcast(128), W=[B("gkv_bc")])
            dve.memset(cs_prompt[0:64, :], 1.0, W=[B("cs_prompt")])
            dve.memset(cs_prompt[64:128, :], 0.0, W=[B("cs_prompt")])
            dve.tensor_copy(Wv[:], wkvb_bf[:].rearrange("p (h t) -> p h t", h=16)[:, :, 128:256],
                            R=[B("wkvb_bf")], W=[B("Wv")])
            for j in range(4):
                pj = next_pj()
                psb = ps[pj][:].bitcast(BF16)
                for hh in range(4):
                    h = 4 * j + hh
                    for c in range(2):
                        pe.transpose(psb[:, (hh * 2 + c) * 128:(hh * 2 + c + 1) * 128],
                                     wqb_bf[:, c, h * 192:h * 192 + 128], ident_bf[:],
                                     R=[B("wqb_bf"), B("ident")], W=[PS[pj]])
                dve.tensor_copy(WqnT[:, 4 * j:4 * j + 4, :], psb.rearrange("p (h t) -> p h t", h=4),
                                R=[PS[pj]], W=[B("WqnT")])
            for j in range(2):
                pj = next_pj()
                psb = ps[pj][:].bitcast(BF16)
                for hh in range(8):
                    h = 8 * j + hh
                    pe.transpose(psb[:, hh * 128:(hh + 1) * 128], wkvb_bf[:, h * 256:h * 256 + 128], ident_bf[:],
                                 R=[B("wkvb_bf"), B("ident")], W=[PS[pj]])
                dve.tensor_copy(WkT[:, 8 * j:8 * j + 8, :], psb.rearrange("p (h t) -> p h t", h=8),
                                R=[PS[pj]], W=[B("WkT")])
            for c in range(2):
                for j in range(4):
                    pj = next_pj()
                    for hh in range(4):
                        h = 4 * j + hh
                        pe.matmul(ps[pj][:, hh * 128:(hh + 1) * 128], WqnT[:, h, c * 128:(c + 1) * 128], WkT[:, h, :],
                                  start=True, stop=True, R=[B("WqnT"), B("WkT")], W=[PS[pj]])
                    dve.tensor_scalar(Wqlat[:, c, 4 * j:4 * j + 4, :], ps[pj][:].rearrange("p (h t) -> p h t", h=4),
                                      gqc[:, c:c + 1], None, ALU.mult, R=[PS[pj], B("gqc")], W=[B("Wqlat")])
            for c in range(2):
                w4 = wqb_bf[:, c, :].rearrange("p (h t) -> p h t", h=16)
                dve.tensor_scalar(WqrAB[:, c, :, 0:64], w4[:, :, 128:192], gqc[:, c:c + 1], None, ALU.mult,
                                  R=[B("wqb_bf"), B("gqc")], W=[B("WqrAB")])
                for a in range(2):
                    o = 32 * a
                    dve.tensor_scalar(WqrAB[:, c, :, 64 + o:64 + o + 16], w4[:, :, 128 + o + 16:128 + o + 32],
                                      gqc[:, c:c + 1], -1.0, ALU.mult, ALU.mult,
                                      R=[B("wqb_bf"), B("gqc")], W=[B("WqrAB")])
                    dve.tensor_scalar(WqrAB[:, c, :, 64 + o + 16:64 + o + 32], w4[:, :, 128 + o:128 + o + 16],
                                      gqc[:, c:c + 1], None, ALU.mult,
                                      R=[B("wqb_bf"), B("gqc")], W=[B("WqrAB")])
            G.barrier()

        if stage == 3:
            raise _Stop()

        def kpass_front(xsrc_ap, g, slot):
            xtile, xbuf = load_x(xsrc_ap, None)
            make_hT(xtile, xbuf, 0, g, slot * 128)

        def kpass_back(g, slot, chunk, rope_pos, st_row):
            pj = next_pj()
            for c in range(8):
                pe.matmul(ps[pj][:, 0:192], hT[:, c, slot * 128:(slot + 1) * 128], w_in_qk[:, c, 256:448],
                          start=(c == 0), stop=(c == 7), R=[B("hT", slot), B("w_in_qk")], W=[PS[pj]])
            r_ap, rbuf = rstd_of(ps[pj][:, 0:128], PS[pj], 128, 1.0 / 128)
            i = chunk % 2
            dve.scalar_tensor_tensor(kv32[i][:], ps[pj][:, 0:128], r_ap, gkv_bc[:], ALU.mult, ALU.mult,
                                     R=[PS[pj], rbuf, B("gkv_bc")], W=[B("kv32", i)])
            act.copy(ckv_tok[:, chunk, :], kv32[i][:], R=[B("kv32", i)], W=[B("ckv_tok", chunk)])
            dve.tensor_copy(kr32[i][:], ps[pj][:, 128:192], R=[PS[pj]], W=[B("kr32", i)])
            if st_row is not None:
                G.dma(POOL, st_ckv[st_row:st_row + 128, :], kv32[i][:], R=[B("kv32", i)], W=[B("st_ckv", st_row)])
                G.dma(POOL, st_kr[st_row:st_row + 128, :], kr32[i][:], R=[B("kr32", i)], W=[B("st_kr", st_row)])
            if rope_pos is not None:
                G.dma(SP, ctab[i][:], kt_tab[rope_pos:rope_pos + 128, :], W=[B("ctab", i)])
                dve.tensor_tensor(krf[:], kr32[i][:], ctab[i][:, 0:64], ALU.mult,
                                  R=[B("kr32", i), B("ctab", i)], W=[B("krf")])
                k4 = kr32[i][:].rearrange("p (a s f) -> p a s f", a=2, s=2)
                t4 = krtmp[:].rearrange("p (a s f) -> p a s f", a=2, s=2)
                s4 = ctab[i][:, 64:128].rearrange("p (a s f) -> p a s f", a=2, s=2)
                dve.tensor_tensor(t4[:, :, 0, :], k4[:, :, 1, :], s4[:, :, 0, :], ALU.mult,
                                  R=[B("kr32", i), B("ctab", i)], W=[B("krtmp")])
                dve.tensor_tensor(t4[:, :, 1, :], k4[:, :, 0, :], s4[:, :, 1, :], ALU.mult,
                                  R=[B("kr32", i), B("ctab", i)], W=[B("krtmp")])
                dve.tensor_tensor(krf[:], krf[:], krtmp[:], ALU.add, R=[B("krf"), B("krtmp")], W=[B("krf")])
                src, sbuf_ = krf, B("krf")
            else:
                src, sbuf_ = kr32[i], B("kr32", i)
            dve.tensor_copy(krf2[i][:, 0:64], src[:], R=[sbuf_], W=[B("krf2", i)])
            dve.tensor_copy(krf2[i][:, 64:128], src[:], R=[sbuf_], W=[B("krf2", i)])
            k_transposes(chunk, krf2[i], B("krf2", i))

        def k_transposes(chunk, krsrc, krbuf):
            pj = next_pj()
            psb = ps[pj][:].bitcast(BF16)
            pe.transpose(psb[:, 0:128], ckv_tok[:, chunk, :], ident_bf[:], R=[B("ckv_tok", chunk), B("ident")],
                         W=[PS[pj]])
            pe.transpose(psb[:, 128:256], krsrc[:], ident_bf[:], R=[krbuf, B("ident")], W=[PS[pj]])
            dve.tensor_copy(ckvT[:, chunk * 128:(chunk + 1) * 128], psb[:, 0:128], R=[PS[pj]], W=[B("ckvT", chunk)])
            dve.tensor_copy(krT2[:, chunk * 128:(chunk + 1) * 128], psb[:, 128:256], R=[PS[pj]],
                            W=[B("krT2", chunk)])

        def cache_chunk(chunk):
            i = chunk % 2
            G.dma(SP, cache32[:, 0:128], c_ckv[chunk * 128:(chunk + 1) * 128, :], W=[B("cache32")])
            G.dma(SP, cache32[:, 128:192], c_kr[chunk * 128:(chunk + 1) * 128, :], W=[B("cache32")])
            act.copy(ckv_tok[:, chunk, :], cache32[:, 0:128], R=[B("cache32")], W=[B("ckv_tok", chunk)])
            dve.tensor_copy(krf2[i][:, 0:64], cache32[:, 128:192], R=[B("cache32")], W=[B("krf2", i)])
            dve.tensor_copy(krf2[i][:, 64:128], cache32[:, 128:192], R=[B("cache32")], W=[B("krf2", i)])
            k_transposes(chunk, krf2[i], B("krf2", i))

        def qblock(xsrc, g, row0, QB, nchunks, cs_ap, csbuf, dst, dstname):
            NT = QB // 128
            hbufs = [B("hT", j_) for j_ in range(NT)]
            set_pj([0, 1, 2, 3, 4, 5])
            kch = [B("ckvT", k) for k in range(nchunks)]
            for j in range(NT):
                xtile, xbuf = load_x(xsrc[row0 + j * 128:row0 + (j + 1) * 128, :], None)
                make_hT(xtile, xbuf, 0, g, j * 128)
            if stage == 3.3:
                raise _Stop()
            for c2 in range(2):
                pj = next_pj()
                for c in range(8):
                    pe.matmul(ps[pj][:, 0:QB], w_in_qk[:, c, c2 * 128:(c2 + 1) * 128], hT[:, c, 0:QB],
                              start=(c == 0), stop=(c == 7), R=[B("w_in_qk")] + hbufs, W=[PS[pj]])
                act.copy(qaT[:, c2, 0:QB], ps[pj][:, 0:QB], R=[PS[pj]], W=[B("qaT")])
                act.activation(sq[:, c2, 0:QB], ps[pj][:, 0:QB], AF.Square, R=[PS[pj]], W=[B("sq")])
            pj = next_pj()
            for c2 in range(2):
                pe.matmul(ps[pj][:, 0:QB], ones_bf[:], sq[:, c2, 0:QB], start=(c2 == 0), stop=(c2 == 1),
                          R=[B("ones"), B("sq")], W=[PS[pj]])
            act.activation(rstd_bc[:, 0:QB], ps[pj][:, 0:QB], AF.Ln, bias=epsc[:], scale=1.0 / 256,
                           R=[PS[pj], B("epsc")], W=[B("rstd_bc")])
            act.activation(rstd_bc[:, 0:QB], rstd_bc[:, 0:QB], AF.Exp, scale=-0.5, R=[B("rstd_bc")], W=[B("rstd_bc")])
            dve.tensor_tensor(csr[:, 0:QB], cs_ap, rstd_bc[:, 0:QB], ALU.mult, R=[csbuf, B("rstd_bc")], W=[B("csr")])
            if stage == 3.4:
                raise _Stop()
            for h in range(16):
                wt, wb = wload(wg_scr[h], B("wg_scr", h))
                pj = next_pj()
                for c in range(8):
                    pe.matmul(ps[pj][:, 0:QB], wt[:, c * 128:(c + 1) * 128], hT[:, c, 0:QB],
                              start=(c == 0), stop=(c == 7), R=[wb] + hbufs, W=[PS[pj]])
                act.activation(sgT[:, h, 0:QB], ps[pj][:, 0:QB], AF.Silu, R=[PS[pj]], W=[B("sgT", h)])

            if stage == 3.5:
                raise _Stop()

            set_pj([4, 5])

            def head_q(h):
                i = h % 4
                pj = next_pj()
                for c2 in range(2):
                    pe.matmul(ps[pj][:, 0:QB], Wqlat[:, c2, h, :], qaT[:, c2, 0:QB], start=(c2 == 0), stop=(c2 == 1),
                              R=[B("Wqlat"), B("qaT")], W=[PS[pj]])
                dve.tensor_tensor(QlatT[i][:, 0:QB], ps[pj][:, 0:QB], rstd_bc[:, 0:QB], ALU.mult,
                                  R=[PS[pj], B("rstd_bc")], W=[B("QlatT", i)])
                pj = next_pj()
                for c2 in range(2):
                    pe.matmul(ps[pj][:, 0:QB], WqrAB[:, c2, h, :], qaT[:, c2, 0:QB], start=(c2 == 0), stop=(c2 == 1),
                              R=[B("WqrAB"), B("qaT")], W=[PS[pj]])
                dve.tensor_tensor(QrT[i][:, 0:QB], ps[pj][:, 0:QB], csr[:, 0:QB], ALU.mult,
                                  R=[PS[pj], B("csr")], W=[B("QrT", i)])

            SBK = [0, 1, 3]
            LA = 3
            sslot = {}
            scnt = [0]

            def o_bank(h):
                if h % 2 == 0:
                    return ps[2], PS[2]
                return ps_out[:, 0:512], PSO0

            def qk(st):
                h, kc = st
                i = h % 4
                s = SBK[scnt[0] % 3]
                scnt[0] += 1
                sslot[st] = s
                pe.matmul(ps[s][:, 0:QB], ckvT[:, kc * 128:(kc + 1) * 128], QlatT[i][:, 0:QB], start=True, stop=False,
                          R=[kch[kc], B("QlatT", i)], W=[PS[s]])
                pe.matmul(ps[s][:, 0:QB], krT2[:, kc * 128:(kc + 1) * 128], QrT[i][:, 0:QB], start=False, stop=True,
                          R=[B("krT2", kc), B("QrT", i)], W=[PS[s]])

            def pv(st):
                h, kc = st
                s = sslot[st]
                o_t, o_b = o_bank(h)
                p = rot("PT", 6)
                act.activation(PT[p][:, 0:QB], ps[s][:, 0:QB], AF.Exp, scale=MLA_SCALE, R=[PS[s]], W=[B("PT", p)])
                a = 2 * (h % 2) + (kc % 2)
                if kc < 2:
                    dve.tensor_copy(denacc[a][:, 0:QB], PT[p][:, 0:QB], R=[B("PT", p)], W=[B("denacc", a)])
                else:
                    dve.tensor_tensor(denacc[a][:, 0:QB], denacc[a][:, 0:QB], PT[p][:, 0:QB], ALU.add,
                                      R=[B("PT", p), B("denacc", a)], W=[B("denacc", a)])
                pe.matmul(o_t[:, 0:QB], ckv_tok[:, kc, :], PT[p][:, 0:QB], start=(kc == 0), stop=(kc == nchunks - 1),
                          R=[B("ckv_tok", kc), B("PT", p)], W=[o_b])

            def head_tail_a(h):
                i = h % 2
                o_t, o_b = o_bank(h)
                act.copy(OlatT[i][:, 0:QB], o_t[:, 0:QB], R=[o_b], W=[B("OlatT", i)])

            def head_tail_b(h):
                i = h % 2
                pj = next_pj()
                na = min(2, nchunks)
                for a_ in range(na):
                    a = 2 * i + a_
                    pe.matmul(ps[pj][:, 0:QB], ones_f32[:], denacc[a][:, 0:QB], start=(a_ == 0), stop=(a_ == na - 1),
                              R=[B("ones_f32"), B("denacc", a)], W=[PS[pj]])
                act.activation(recip[i][:, 0:QB], ps[pj][:, 0:QB], AF.Ln, R=[PS[pj]], W=[B("recip", i)])
                act.activation(recip[i][:, 0:QB], recip[i][:, 0:QB], AF.Exp, scale=-1.0, R=[B("recip", i)],
                               W=[B("recip", i)])
                dve.tensor_tensor(recip[i][:, 0:QB], recip[i][:, 0:QB], sgT[:, h, 0:QB], ALU.mult,
                                  R=[B("recip", i), B("sgT", h)], W=[B("recip", i)])
                pj = next_pj()
                pe.matmul(ps[pj][:, 0:QB], Wv[:, h, :], OlatT[i][:, 0:QB], start=True, stop=True,
                          R=[B("Wv"), B("OlatT", i)], W=[PS[pj]])
                dve.tensor_tensor(sgT[:, h, 0:QB], ps[pj][:, 0:QB], recip[i][:, 0:QB], ALU.mult,
                                  R=[PS[pj], B("recip", i)], W=[B("sgT", h)])

            steps = [(h, kc) for h in range(16) for kc in range(nchunks)]
            ns = len(steps)
            TD = min(3, nchunks - 1)
            QD = 4 if nchunks > 4 else 2
            hq_done = [0]

            def ensure_hq(upto_step):
                while hq_done[0] < 16 and hq_done[0] * nchunks <= upto_step:
                    head_q(hq_done[0])
                    hq_done[0] += 1

            ensure_hq(LA + QD)
            for k in range(min(LA, ns)):
                qk(steps[k])
            for k in range(ns):
                h, kc = steps[k]
                pv(steps[k])
                ensure_hq(k + 1 + LA + QD)
                if k + LA < ns:
                    qk(steps[k + LA])
                if kc == nchunks - 1:
                    head_tail_a(h)
                if kc == TD and h >= 1:
                    head_tail_b(h - 1)
            head_tail_b(15)
            if stage == 3.6:
                raise _Stop()
            for h in range(16):
                wt, wb = wload(wo_scr[h], B("wo_scr", h))
                for j in range(NT):
                    for n in range(2):
                        pe.matmul(out_banks[j][n], sgT[:, h, j * 128:(j + 1) * 128],
                                  wt[:, n * 512:(n + 1) * 512], start=(h == 0), stop=(h == 15),
                                  R=[B("sgT", h), wb], W=[out_banks[j][2 + n]])
            for j in range(NT):
                r = row0 + j * 128
                epilogue(out_banks[j], 0, g, xsrc[r:r + 128, :], None, dst[r:r + 128, :], B(dstname, r))


        for b in range(NPB):
            set_pj([0, 1, 2, 3, 4, 5])
            for t in range(2):
                kpass_front(xp[b * 256 + t * 128: b * 256 + (t + 1) * 128, :], 1, t)
            for t in range(2):
                kpass_back(1, t, t, None, b * 256 + t * 128)
            qblock(xp, 1, b * 256, 256, 2, cs_prompt[:, 0:256], B("cs_prompt"), x1p, "x1p")
        if stage == 4:
            raise _Stop()
        cache_chunk(0)
        cache_chunk(1)
        set_pj([0, 1, 2, 3, 4, 5])
        for t in range(2):
            kpass_front(xs[t * 128:(t + 1) * 128, :], 0, t % 4)
        for t in range(32):
            kpass_back(0, t % 4, 2 + t, t * 128, None)
            if t + 2 < 32:
                kpass_front(xs[(t + 2) * 128:(t + 3) * 128, :], 0, (t + 2) % 4)
        for qb in range(8):
            i = qb % 2
            G.dma(SP, cstab[i][:], cs_tab[:, qb * 512:(qb + 1) * 512], W=[B("cstab", i)])
            qblock(xs, 0, qb * 512, 512, 34, cstab[i][:, 0:512], B("cstab", i), x1s, "x1s")
            for f_ in na_casts[5 * qb:5 * qb + 5]:
                f_()
        G.barrier()


    W0.close()
    if debug == 1:
        return
    if stage == 5:
        raise _Stop()
    with ExitStack() as L1:
        Kpad = sb(L1, "Kpad", [128, 16, 1024], BF16)
        Vaug = sb(L1, "Vaug", [128, 8, 8, 192], BF16)
        Kc_pad = sb(L1, "Kc_pad", [128, 16, 256], BF16)
        Vc_aug = sb(L1, "Vc_aug", [128, 2, 8, 192], BF16)
        qT = sb(L1, "qT", [128, 8, 1024], BF16)
        Tint_all = sb(L1, "Tint_all", [128, 16, 576], BF16)
        tfull = [sb(L1, "tfull%d" % i, [128, 896], BF16) for i in range(2)]
        sw = sb(L1, "sw", [128, 128], F32)
        Dst = sb(L1, "Dst", [128, 512], F32)
        Osb = sb(L1, "Osb", [128, 512], F32)
        vst = [sb(L1, "vst%d" % i, [128, 512], F32) for i in range(2)]
        sg1 = sgT[:].rearrange("p h q -> p (h q)").rearrange("p (m t) -> p m t", m=8)

        G.dma(SP, sw[:], sw_in[:], W=[B("sw")])
        for h_ in range(16):
            G.dma(SP, Tint_all[:, h_, :], tab_scr[h_][:, 448:1024], R=[B("tab_scr", h_)], W=[B("Tint", h_)])
        dve.memset(Kpad[:], 0.0, W=[B("Kpad", s_) for s_ in range(8)])
        dve.memset(Kc_pad[:], 0.0, W=[B("Kc_pad")])
        dve.memset(Vaug[:, :, :, 64:128], 1.0, W=[B("Vaug", s_) for s_ in range(8)])
        dve.memset(Vc_aug[:, :, :, 64:128], 1.0, W=[B("Vc_aug")])

        def runs(tiles):
            out = []
            for jj, t in enumerate(tiles):
                sl = t % 8
                if out and out[-1][1] + out[-1][2] == sl:
                    out[-1][2] += 1
                else:
                    out.append([jj, sl, 1])
            return out

        vcnt = [0]

        def proj_group(xsrc, xname, g, tiles, st_row0):
            n = len(tiles)
            N = n * 128
            rr = runs(tiles)
            hbufs1 = [B("hT", j_) for j_ in range(n)]
            set_pj([0, 1, 2, 3, 4, 5])
            for jj, t in enumerate(tiles):
                r = t * 128
                xtile, xbuf = load_x(xsrc[r:r + 128, :], B(xname, r))
                make_hT(xtile, xbuf, 1, g, jj * 128)

            def fm(widx):
                wt, wb = wload(nwi_scr[widx], B("nwi_scr", widx))
                pj = next_pj()
                for c in range(8):
                    pe.matmul(ps[pj][:, 0:N], wt[:, c * 128:(c + 1) * 128], hT[:, c, 0:N], start=(c == 0),
                              stop=(c == 7), R=[wb] + hbufs1, W=[PS[pj]])
                return wt, wb, pj

            def tm(wt, wb):
                pj = next_pj()
                for jj in range(n):
                    for c in range(8):
                        pe.matmul(ps[pj][:, jj * 128:(jj + 1) * 128], hT[:, c, jj * 128:(jj + 1) * 128],
                                  wt[:, c * 128:(c + 1) * 128], start=(c == 0), stop=(c == 7),
                                  R=[wb, B("hT", jj)], W=[PS[pj]])
                return pj

            def state_store(pj, dst, m):
                i = vcnt[0] % 2
                vcnt[0] += 1
                act.copy(vst[i][:, 0:N], ps[pj][:, 0:N], R=[PS[pj]], W=[B("vst", i)])
                for jj in range(n):
                    r = st_row0 + jj * 128
                    G.dma(POOL, dst[r:r + 128, m * 128:(m + 1) * 128], vst[i][:, jj * 128:(jj + 1) * 128],
                          R=[B("vst", i)], W=[B("stkv", vcnt[0], jj)])

            for m in range(8):
                wt, wb, pj = fm(m)
                for jj0, sl0, cn in rr:
                    act.copy(qT[:, m, sl0 * 128:(sl0 + cn) * 128], ps[pj][:, jj0 * 128:(jj0 + cn) * 128],
                             R=[PS[pj]], W=[B("qT", sl0 + x_) for x_ in range(cn)])
            for m in range(8):
                wt, wb, pj = fm(8 + m)
                for jj0, sl0, cn in rr:
                    wl = [B("Kpad", sl0 + x_) for x_ in range(cn)]
                    dve.tensor_copy(Kpad[0:64, 2 * m, sl0 * 128:(sl0 + cn) * 128],
                                    ps[pj][0:64, jj0 * 128:(jj0 + cn) * 128], R=[PS[pj]], W=wl)
                    dve.tensor_copy(Kpad[64:128, 2 * m + 1, sl0 * 128:(sl0 + cn) * 128],
                                    ps[pj][64:128, jj0 * 128:(jj0 + cn) * 128], R=[PS[pj]], W=wl)
                if st_row0 is not None:
                    pj2 = tm(wt, wb)
                    state_store(pj2, st_k, m)
            for m in range(8):
                wt, wb = wload(nwi_scr[16 + m], B("nwi_scr", 16 + m))
                pj = tm(wt, wb)
                p3 = ps[pj][:, 0:N].rearrange("p (t d) -> p t d", d=128)
                for jj0, sl0, cn in rr:
                    wl = [B("Vaug", sl0 + x_) for x_ in range(cn)]
                    dve.tensor_copy(Vaug[:, sl0:sl0 + cn, m, 0:64], p3[:, jj0:jj0 + cn, 0:64], R=[PS[pj]], W=wl)
                    dve.tensor_copy(Vaug[:, sl0:sl0 + cn, m, 128:192], p3[:, jj0:jj0 + cn, 64:128], R=[PS[pj]], W=wl)
                if st_row0 is not None:
                    state_store(pj, st_v, m)
            for m in range(8):
                wt, wb, pj = fm(24 + m)
                for jj0, sl0, cn in rr:
                    act.activation(sg1[:, m, sl0 * 128:(sl0 + cn) * 128], ps[pj][:, jj0 * 128:(jj0 + cn) * 128],
                                   AF.Silu, R=[PS[pj]], W=[B("sg1", sl0 + x_) for x_ in range(cn)])

        def na_attn(blk, QB, qtiles, chunks, g, xsrc, xname, dst, dstname):
            NT = QB // 128
            qs0 = qtiles[0] % 8
            qc0 = qs0 * 128
            qbufs = [B("qT", qs0 + x_) for x_ in range(NT)]
            sgbufs = [B("sg1", qs0 + x_) for x_ in range(NT)]
            nch = len(chunks)
            SBK = [0, 1, 4, 5]
            LA = 4
            scnt = [0]
            sslot = {}

            def qrange(i):
                kind, idx = chunks[i]
                if kind != "r" or blk is None:
                    return 0, QB
                rows = []
                for ii in range(8):
                    r = 8 * blk + ii
                    rs = min(max(r - 4, 0), 56)
                    if any(rs <= kr <= rs + 7 for kr in (2 * idx, 2 * idx + 1)):
                        rows.append(ii)
                return 64 * rows[0], 64 * (rows[-1] + 1)

            def accs_of(m):
                if m % 2 == 0:
                    return [(ps[2], PS[2]), (ps[3], PS[3])]
                return [(ps_out[:, 0:512], PSO0), (ps_out[:, 512:1024], PSO1)]

            def load_tab(h):
                if blk in (0, 7) and h < 16:
                    G.dma(SP, tfull[h % 2][:], tab_scr[h][:, 1408 + 256:1408 + 1152], R=[B("tab_scr", h)],
                          W=[B("tfull", h % 2)])

            def qk(st):
                h, i = st
                m = h // 2
                kind, idx = chunks[i]
                if kind == "r":
                    sl = idx % 8
                    lhs, lb = Kpad[:, h, sl * 128:(sl + 1) * 128], B("Kpad", sl)
                else:
                    lhs, lb = Kc_pad[:, h, idx * 128:(idx + 1) * 128], B("Kc_pad")
                sbk = SBK[scnt[0] % 4]
                scnt[0] += 1
                sslot[st] = sbk
                q0, q1 = qrange(i)
                pe.matmul(ps[sbk][:, q0:q1], lhs, qT[:, m, qc0 + q0:qc0 + q1], start=True, stop=True,
                          R=[lb] + qbufs, W=[PS[sbk]])

            def pv(st):
                h, i = st
                m, e = h // 2, h % 2
                acc_t, acc_b = accs_of(m)[e]
                kind, idx = chunks[i]
                sbk = sslot[st]
                q0, q1 = qrange(i)
                p = rot("PT", 6)
                act.activation(PT[p][:, q0:q1], ps[sbk][:, q0:q1], AF.Exp, scale=NA_SCALE, R=[PS[sbk]],
                               W=[B("PT", p)])
                if kind == "r" and blk is not None:
                    u0 = 10 - 2 * (idx - 4 * blk)
                    if blk == 0 and idx <= 3:
                        segs = [(0, 256, 1), (256, 512, 0)]
                    elif blk == 7 and idx >= 28:
                        segs = [(0, 320, 0), (320, 512, 1)]
                    else:
                        segs = [(0, 512, 0)]
                    for a0, a1, full in segs:
                        a0, a1 = max(a0, q0), min(a1, q1)
                        if a1 <= a0:
                            continue
                        if full:
                            c0 = u0 * 64 + a0 - 256
                            assert 0 <= c0 and c0 + (a1 - a0) <= 896
                            tt, tbuf = tfull[h % 2][:, c0:c0 + (a1 - a0)], B("tfull", h % 2)
                        else:
                            c0 = u0 * 64 + a0 - 448
                            assert 0 <= c0 and c0 + (a1 - a0) <= 576, (blk, idx, a0, a1, u0)
                            tt, tbuf = Tint_all[:, h, c0:c0 + (a1 - a0)], B("Tint", h)
                        dve.tensor_tensor(PT[p][:, a0:a1], PT[p][:, a0:a1], tt, ALU.mult,
                                          R=[B("PT", p), tbuf], W=[B("PT", p)])
                if kind == "r":
                    sl = idx % 8
                    lhs, lb = Vaug[:, sl, m, 64 * e:64 * e + 128], B("Vaug", sl)
                else:
                    lhs, lb = Vc_aug[:, idx, m, 64 * e:64 * e + 128], B("Vc_aug")
                pe.matmul(acc_t[:, q0:q1], lhs, PT[p][:, q0:q1], start=(i == 0), stop=(i == nch - 1),
                          R=[lb, B("PT", p)], W=[acc_b])

            def tail(m):
                (A_t, A_b), (B_t, B_b) = accs_of(m)
                dve.tensor_copy(Dst[0:64, 0:QB], B_t[0:64, 0:QB], R=[B_b], W=[B("Dst")])
                dve.tensor_copy(Dst[64:128, 0:QB], A_t[64:128, 0:QB], R=[A_b], W=[B("Dst")])
                dve.tensor_copy(Osb[0:64, 0:QB], A_t[0:64, 0:QB], R=[A_b], W=[B("Osb")])
                dve.tensor_copy(Osb[64:128, 0:QB], B_t[64:128, 0:QB], R=[B_b], W=[B("Osb")])
                pe.matmul(A_t[:, 0:QB], sw[:], Dst[:, 0:QB], start=True, stop=True, R=[B("sw"), B("Dst")],
                          W=[A_b])
                ri = m % 2
                act.activation(recip[ri][:, 0:QB], A_t[:, 0:QB], AF.Ln, R=[A_b], W=[B("recip", ri)])
                act.activation(recip[ri][:, 0:QB], recip[ri][:, 0:QB], AF.Exp, scale=-1.0, R=[B("recip", ri)],
                               W=[B("recip", ri)])
                dve.tensor_tensor(recip[ri][:, 0:QB], recip[ri][:, 0:QB], sg1[:, m, qc0:qc0 + QB], ALU.mult,
                                  R=[B("recip", ri)] + sgbufs, W=[B("recip", ri)])
                dve.tensor_tensor(sg1[:, m, qc0:qc0 + QB], Osb[:, 0:QB], recip[ri][:, 0:QB], ALU.mult,
                                  R=[B("Osb"), B("recip", ri)], W=sgbufs)

            steps = [(h, i) for h in range(16) for i in range(nch)]
            ns = len(steps)
            TD = min(4, nch - 1)
            load_tab(0)
            load_tab(1)
            for k in range(min(LA, ns)):
                qk(steps[k])
            for k in range(ns):
                h, i = steps[k]
                pv(steps[k])
                if k + LA < ns:
                    qk(steps[k + LA])
                if i == nch - 1 and h + 2 < 16:
                    load_tab(h + 2)
                if h % 2 == 0 and h >= 2 and i == TD:
                    tail(h // 2 - 1)
            tail(7)
            for c in range(8):
                wt, wb = wload(nwo_scr[c], B("nwo_scr", c))
                for j in range(NT):
                    for n in range(2):
                        pe.matmul(out_banks[j][n], sg1[:, c, qc0 + j * 128:qc0 + (j + 1) * 128],
                                  wt[:, n * 512:(n + 1) * 512], start=(c == 0), stop=(c == 7),
                                  R=sgbufs + [wb], W=[out_banks[j][2 + n]])
            for j in range(NT):
                r = qtiles[j] * 128
                epilogue(out_banks[j], 1, g, xsrc[r:r + 128, :], B(xname, r), dst[r:r + 128, :], B(dstname, r))

        for pb in range(NPB):
            tiles = [2 * pb, 2 * pb + 1]
            proj_group(x1p, "x1p", 1, tiles, pb * 256)
            if stage == 6:
                raise _Stop()
            na_attn(None, 256, tiles, [("r", tiles[0]), ("r", tiles[1])], 1, x1p, "x1p", y_p, "y_p")
            if stage == 7:
                raise _Stop()
        if stage == 8:
            raise _Stop()
        for k in range(2):
            xtile, xbuf = load_x(c_nak[k * 128:(k + 1) * 128, :], None)
            i = rot("xn", 2)
            act.copy(xn[i][:], xtile[:], R=[xbuf], W=[B("xn", i)])
            pj = next_pj()
            psb = ps[pj][:].bitcast(BF16)
            for m in range(8):
                pe.transpose(psb[:, m * 128:(m + 1) * 128], xn[i][:, m * 128:(m + 1) * 128], ident_bf[:],
                             R=[B("xn", i), B("ident")], W=[PS[pj]])
            kc4 = Kc_pad[:].rearrange("p (m e) t -> p m e t", e=2)
            p3 = psb.rearrange("p (m t) -> p m t", m=8)
            dve.tensor_copy(kc4[0:64, :, 0, k * 128:(k + 1) * 128], p3[0:64, :, :], R=[PS[pj]], W=[B("Kc_pad")])
            dve.tensor_copy(kc4[64:128, :, 1, k * 128:(k + 1) * 128], p3[64:128, :, :], R=[PS[pj]], W=[B("Kc_pad")])
            xtile, xbuf = load_x(c_nav[k * 128:(k + 1) * 128, :], None)
            x3 = xtile[:].rearrange("p (m d) -> p m d", m=8)
            dve.tensor_copy(Vc_aug[:, k, :, 0:64], x3[:, :, 0:64], R=[xbuf], W=[B("Vc_aug")])
            dve.tensor_copy(Vc_aug[:, k, :, 128:192], x3[:, :, 64:128], R=[xbuf], W=[B("Vc_aug")])
        if stage == 9:
            raise _Stop()
        groups = [[0, 1]] + [[4 * c + 2, 4 * c + 3, 4 * c + 4, 4 * c + 5] for c in range(7)] + [[30, 31]]
        proj_group(x1s, "x1s", 0, groups[0], None)
        for b in range(8):
            proj_group(x1s, "x1s", 0, groups[b + 1], None)
            chunks = [("c", 0), ("c", 1)] + [("r", t) for t in range(max(0, 4 * b - 2), min(31, 4 * b + 5) + 1)]
            na_attn(b, 512, [4 * b, 4 * b + 1, 4 * b + 2, 4 * b + 3], chunks, 0, x1s, "x1s", y_s, "y_s")


def build_nc(debug=False, stage=99):
    gd = build_program(None, None, debug, stage)
    nc = bass.Bass("TRN2", target_bir_lowering=False)
    g = build_program(nc, gd.need, debug, stage)
    return nc, g


def rope_tables():
    t = np.arange(SEQ_S)
    pos = np.stack([t // GRID_W, t % GRID_W], axis=-1).astype(np.float32)
    inv = (10000.0 ** (-np.arange(16, dtype=np.float32) / 16)).astype(np.float32)
    ang = pos[:, :, None] * inv
    cos, sin = np.cos(ang).astype(np.float32), np.sin(ang).astype(np.float32)
    cos_full = np.concatenate([cos[:, 0], cos[:, 0], cos[:, 1], cos[:, 1]], axis=1)
    sin_full = np.concatenate([sin[:, 0], sin[:, 0], sin[:, 1], sin[:, 1]], axis=1)
    sin_signed = np.concatenate([-sin[:, 0], sin[:, 0], -sin[:, 1], sin[:, 1]], axis=1)
    cs_tab = np.ascontiguousarray(np.concatenate([cos_full, sin_full], axis=1).T)
    kt_tab = np.ascontiguousarray(np.concatenate([cos_full, sin_signed], axis=1))
    return cs_tab, kt_tab


def make_in_maps(inp):
    cs_tab, kt_tab = rope_tables()
    ident = np.eye(128, dtype=np.float32)
    j2 = np.zeros((128, 128), np.float32)
    for m in range(128):
        j2[(m // 64) * 64 + 63 - (m % 64), m] = 1.0
    selm = np.zeros((2, 256), np.float32)
    selm[0, 0:128] = 1.0
    selm[1, 128:256] = 1.0
    i2 = np.eye(2, dtype=np.float32)
    rb = inp["na_rel_bias"][0]
    rbp = np.zeros((16, 23, 127), np.float32)
    rbp[:, 4:19, 48:79] = rb[:, ::-1, ::-1]
    mask = np.zeros((128, 2, 22, 64), np.float32)
    qc = np.arange(64)
    cstart = np.clip(qc - 8, 0, 48)
    for half in range(2):
        for kcp in range(64):
            kc = 63 - kcp
            colok = (kc >= cstart) & (kc < cstart + 16)
            for ui in range(22):
                dr = half - (ui - 10)
                if -4 <= dr <= 3:
                    mask[half * 64 + kcp, 0, ui, :] = colok
                if -7 <= dr <= 7:
                    mask[half * 64 + kcp, 1, ui, :] = colok
    mask = mask.reshape(128, 2, 22 * 64)
    swm = np.zeros((128, 128), np.float32)
    for m in range(128):
        swm[(m + 64) % 128, m] = 1.0
    shared = {
        "w_ada": inp["w_ada"], "b_ada": inp["b_ada"], "pre_g": inp["pre_norm_g"], "post_g": inp["post_norm_g"],
        "mla_w_in": inp["mla_w_in"][0],
        "gq_cols": np.ascontiguousarray(inp["mla_q_norm_g"][0].reshape(2, 128).T),
        "mla_w_qb": inp["mla_w_qb"][0], "gkv": inp["mla_kv_norm_g"][0], "mla_w_kvb": inp["mla_w_kvb"][0],
        "mla_w_out": inp["mla_w_out"][0], "na_w_in": inp["na_w_in"][0], "rbp": rbp, "na_w_out": inp["na_w_out"][0],
        "ident_in": ident, "j2_in": j2, "selm_in": selm, "i2_in": i2, "cs_tab": cs_tab, "kt_tab": kt_tab,
        "mask_in": mask, "sw_in": swm,
    }
    shared = {k: np.ascontiguousarray(v, dtype=np.float32) for k, v in shared.items()}
    maps = []
    for i in range(8):
        cc = np.stack([inp["c"][i].reshape(8, 128).T, inp["c_ctx"].reshape(8, 128).T], axis=-1)
        m = dict(shared)
        m["xs"] = np.ascontiguousarray(inp["x_sample"][i])
        m["xp"] = np.ascontiguousarray(inp["x_prompt"][4 * i:4 * i + 4].reshape(1024, 1024))
        m["c_ckv"] = np.ascontiguousarray(inp["cache_mla_ckv"][i, 0])
        m["c_kr"] = np.ascontiguousarray(inp["cache_mla_krope"][i, 0])
        m["c_nak"] = np.ascontiguousarray(inp["cache_na_k"][i, 0].reshape(256, 1024))
        m["c_nav"] = np.ascontiguousarray(inp["cache_na_v"][i, 0].reshape(256, 1024))
        m["ccols"] = np.ascontiguousarray(cc.reshape(128, 16), dtype=np.float32)
        maps.append(m)
    return maps


_NC_CACHE = {}


def kernel(**inputs):
    inp = {k: np.asarray(v) for k, v in inputs.items()}
    if "nc" not in _NC_CACHE:
        _NC_CACHE["nc"] = build_nc()[0]
    nc = _NC_CACHE["nc"]
    maps = make_in_maps(inp)
    res = run_bass_kernel_spmd(nc, maps, core_ids=list(range(8)))
    r = res.results
    y_p = np.concatenate([r[i]["y_p"].reshape(4, 256, 1024) for i in range(8)], axis=0)
    y_s = np.stack([r[i]["y_s"] for i in range(8)], axis=0)
    s_ckv = np.concatenate([r[i]["st_ckv"].reshape(4, 1, 256, 128) for i in range(8)], axis=0)
    s_kr = np.concatenate([r[i]["st_kr"].reshape(4, 1, 256, 64) for i in range(8)], axis=0)
    s_k = np.concatenate([r[i]["st_k"].reshape(4, 1, 256, 16, 64) for i in range(8)], axis=0)
    s_v = np.concatenate([r[i]["st_v"].reshape(4, 1, 256, 16, 64) for i in range(8)], axis=0)
    return (y_p.astype(np.float32), y_s.astype(np.float32), s_ckv.astype(np.float32), s_kr.astype(np.float32),
            s_k.astype(np.float32), s_v.astype(np.float32))
```
